# Optimizing a Trainium2 kernel written in Bass

```python
import jax, jax.numpy as jnp
from jax import lax
import numpy as np

D_MODEL = 1024
BATCH = 2
SEQ = 8192
DEPTH = 2

GRID_W = 64
CTX_LEN = 256
HG_HEAD_DIM = 128
HG_DIM = D_MODEL
HG_HEADS = HG_DIM // HG_HEAD_DIM
CHUNK = 64
CV_DIM = D_MODEL
CONV_K = 31
FFN_DIM = ((8 * D_MODEL // 3 + 255) // 256) * 256
FFN_K = 3
N_MOD = 6
EPS = 1e-6
G_MIN = 1e-6
SPLIT_SIZES = (HG_DIM, HG_DIM, HG_DIM, HG_DIM, HG_DIM, CV_DIM, CV_DIM, D_MODEL, D_MODEL)
P_TOTAL = 5 * HG_DIM + 2 * CV_DIM + 2 * D_MODEL

kernel_name = 'hybrid_hgrn2_conformer_convffn_dit'


def rmsnorm(x, w):
    xf = x.astype(jnp.float32)
    y = xf * lax.rsqrt(jnp.mean(xf * xf, axis=-1, keepdims=True) + EPS)
    return (y * w.astype(jnp.float32)).astype(x.dtype)


def layernorm(x, w, b):
    xf = x.astype(jnp.float32)
    mu = jnp.mean(xf, axis=-1, keepdims=True)
    xc = xf - mu
    y = xc * lax.rsqrt(jnp.mean(xc * xc, axis=-1, keepdims=True) + EPS)
    return (y * w.astype(jnp.float32) + b.astype(jnp.float32)).astype(x.dtype)


def split_proj(p):
    return jnp.split(p, np.cumsum(SPLIT_SIZES)[:-1], axis=-1)


def heads(t):
    return t.reshape(t.shape[0], t.shape[1], HG_HEADS, HG_HEAD_DIM)


def dwconv1d(x, w, b):
    k = w.shape[0]
    y = lax.conv_general_dilated(x, w[:, None, :], window_strides=(1,), padding=[(k // 2, k // 2)],
                                 dimension_numbers=('NWC', 'WIO', 'NWC'), feature_group_count=x.shape[-1])
    return y + b


def dwconv2d(x, w, b):
    k = w.shape[0]
    y = lax.conv_general_dilated(x, w[:, :, None, :], window_strides=(1, 1),
                                 padding=[(k // 2, k // 2), (k // 2, k // 2)],
                                 dimension_numbers=('NHWC', 'HWIO', 'NHWC'), feature_group_count=x.shape[-1])
    return y + b


def gla_chunked(q, k, v, logf, s0):
    bsz, length, nh, _ = q.shape
    dv = v.shape[-1]
    n_chunks = length // CHUNK

    def to_chunks(t):
        return t.astype(jnp.float32).reshape(bsz, n_chunks, CHUNK, nh, t.shape[-1]).transpose(1, 0, 3, 2, 4)

    tril = jnp.tril(jnp.ones((CHUNK, CHUNK), dtype=bool))[:, :, None]

    def step(state, inp):
        qc, kc, vc, fc = inp
        b = jnp.cumsum(fc, axis=2)
        o_inter = jnp.einsum('bhtd,bhde->bhte', qc * jnp.exp(b), state)
        rel = b[:, :, :, None, :] - b[:, :, None, :, :]
        decay = jnp.where(tril, jnp.exp(jnp.minimum(rel, 0.0)), 0.0)
        scores = jnp.einsum('bhtd,bhsd,bhtsd->bhts', qc, kc, decay)
        o = o_inter + jnp.einsum('bhts,bhse->bhte', scores, vc)
        b_end = b[:, :, -1]
        new_state = jnp.exp(b_end)[..., None] * state + jnp.einsum(
            'bhsd,bhse->bhde', kc * jnp.exp(jnp.minimum(b_end[:, :, None, :] - b, 0.0)), vc)
        return new_state, o

    s_fin, o = lax.scan(step, s0.astype(jnp.float32),
                        (to_chunks(q), to_chunks(k), to_chunks(v), to_chunks(logf)))
    o = o.transpose(1, 0, 3, 2, 4).reshape(bsz, length, nh, dv)
    return o, s_fin


def hgrn2_gates(f_logit, lb):
    f = f_logit.astype(jnp.float32)
    lb = lb.astype(jnp.float32)
    g = lb + (1.0 - lb) * jax.nn.sigmoid(f)
    log_g = jnp.log(jnp.clip(g, G_MIN, 1.0))
    k = (1.0 - lb) * jax.nn.sigmoid(-f)
    return k, log_g


def hgrn2_bidir(q, i, f_fwd, f_bwd, lb_fwd, lb_bwd, s_fwd, s_bwd):
    q, i = heads(q), heads(i)
    k_f, lg_f = hgrn2_gates(heads(f_fwd), lb_fwd.reshape(HG_HEADS, HG_HEAD_DIM))
    k_b, lg_b = hgrn2_gates(heads(f_bwd), lb_bwd.reshape(HG_HEADS, HG_HEAD_DIM))
    o_f, sf = gla_chunked(q, k_f, i, lg_f, s_fwd)
    flip = lambda t: jnp.flip(t, axis=1)
    o_b, sb = gla_chunked(flip(q), flip(k_b), flip(i), flip(lg_b), s_bwd)
    return o_f + flip(o_b), sf, sb


def mixer_merge(o_hg, g, cv_a, cv_b, gate_hg, gate_cv, gn_w, w_hg_out, dw_w, dw_b, ln_w, ln_b, w_cv_out, w_out):
    bsz, length = g.shape[:2]
    o = rmsnorm(o_hg.astype(g.dtype), gn_w) * jax.nn.silu(heads(g))
    y_hg = o.reshape(bsz, length, HG_DIM) @ w_hg_out
    u = cv_a * jax.nn.sigmoid(cv_b)
    u = jax.nn.silu(layernorm(dwconv1d(u, dw_w, dw_b), ln_w, ln_b))
    y_cv = u @ w_cv_out
    y = jax.nn.sigmoid(gate_hg) * y_hg + jax.nn.sigmoid(gate_cv) * y_cv
    return y @ w_out


def conv_ffn(h, w_up, dw_w, dw_b, w_down):
    u, v = jnp.split(h @ w_up, 2, axis=-1)
    u = dwconv2d(u, dw_w, dw_b)
    return (jax.nn.gelu(u, approximate=False) * v) @ w_down


def setup_inputs(seed: int = 0) -> dict:
    key = jax.random.key(seed)
    ks = jax.random.split(key, 23)
    nrm = lambda k, shape, s: s * jax.random.normal(k, shape, jnp.float32)
    return {
        'x': nrm(ks[0], (BATCH, SEQ, D_MODEL), 1.0),
        'c': nrm(ks[1], (BATCH, D_MODEL), 1.0),
        'ctx': nrm(ks[2], (BATCH, CTX_LEN, D_MODEL), 1.0),
        'c_ctx': nrm(ks[3], (D_MODEL,), 1.0),
        'w_mod': nrm(ks[4], (DEPTH, D_MODEL, N_MOD * D_MODEL), 0.5 * D_MODEL ** -0.5),
        'b_mod': nrm(ks[5], (DEPTH, N_MOD * D_MODEL), 0.01),
        'norm1_w': 1.0 + nrm(ks[6], (DEPTH, D_MODEL), 0.02),
        'w_in': nrm(ks[7], (DEPTH, D_MODEL, P_TOTAL), D_MODEL ** -0.5),
        'hg_lb_logits': nrm(ks[8], (2, DEPTH, HG_DIM), 0.5),
        'hg_gnorm_w': 1.0 + nrm(ks[9], (DEPTH, HG_HEAD_DIM), 0.02),
        'w_hg_out': nrm(ks[10], (DEPTH, HG_DIM, D_MODEL), HG_DIM ** -0.5),
        'cv_dw_w': nrm(ks[11], (DEPTH, CONV_K, CV_DIM), CONV_K ** -0.5),
        'cv_dw_b': nrm(ks[12], (DEPTH, CV_DIM), 0.01),
        'cv_ln_w': 1.0 + nrm(ks[13], (DEPTH, CV_DIM), 0.02),
        'cv_ln_b': nrm(ks[14], (DEPTH, CV_DIM), 0.01),
        'w_cv_out': nrm(ks[15], (DEPTH, CV_DIM, D_MODEL), CV_DIM ** -0.5),
        'w_out': nrm(ks[16], (DEPTH, D_MODEL, D_MODEL), D_MODEL ** -0.5),
        'norm2_w': 1.0 + nrm(ks[17], (DEPTH, D_MODEL), 0.02),
        'w_up': nrm(ks[18], (DEPTH, D_MODEL, 2 * FFN_DIM), D_MODEL ** -0.5),
        'ffn_dw_w': nrm(ks[19], (DEPTH, FFN_K, FFN_K, FFN_DIM), 1.0 / FFN_K),
        'ffn_dw_b': nrm(ks[20], (DEPTH, FFN_DIM), 0.01),
        'w_down': nrm(ks[21], (DEPTH, FFN_DIM, D_MODEL), FFN_DIM ** -0.5),
        'final_norm_w': 1.0 + nrm(ks[22], (D_MODEL,), 0.02),
    }


def reference(x, c, ctx, c_ctx, w_mod, b_mod, norm1_w, w_in, hg_lb_logits, hg_gnorm_w, w_hg_out,
              cv_dw_w, cv_dw_b, cv_ln_w, cv_ln_b, w_cv_out, w_out, norm2_w, w_up, ffn_dw_w, ffn_dw_b,
              w_down, final_norm_w):
    bsz, length, _ = x.shape
    rows = length // GRID_W
    lb_sm = jax.nn.softmax(hg_lb_logits.astype(jnp.float32), axis=1)
    lbs = jnp.cumsum(lb_sm, axis=1) - lb_sm[:, :1]
    cx = ctx
    zero_state = jnp.zeros((bsz, HG_HEADS, HG_HEAD_DIM, HG_HEAD_DIM), jnp.float32)
    for l in range(DEPTH):
        mod = (jax.nn.silu(c) @ w_mod[l] + b_mod[l])[:, None, :]
        mod_c = jax.nn.silu(c_ctx) @ w_mod[l] + b_mod[l]
        sh1, sc1, gt1, sh2, sc2, gt2 = jnp.split(mod, N_MOD, axis=-1)
        csh1, csc1, cgt1, csh2, csc2, cgt2 = jnp.split(mod_c, N_MOD, axis=-1)

        h = rmsnorm(x, norm1_w[l]) * (1 + sc1) + sh1
        hc = rmsnorm(cx, norm1_w[l]) * (1 + csc1) + csh1
        pq, pff, pfb, pi, pg, pa, pb, pgh, pgc = split_proj(h @ w_in[l])
        cq, cff, cfb, ci, cg, ca, cb, cgh, cgc = split_proj(hc @ w_in[l])
        o_ctx, s_f, s_b = hgrn2_bidir(cq, ci, cff, cfb, lbs[0, l], lbs[1, l], zero_state, zero_state)
        o_lat, _, _ = hgrn2_bidir(pq, pi, pff, pfb, lbs[0, l], lbs[1, l], s_f, s_b)
        layer_w = (hg_gnorm_w[l], w_hg_out[l], cv_dw_w[l], cv_dw_b[l], cv_ln_w[l], cv_ln_b[l], w_cv_out[l], w_out[l])
        x = x + gt1 * mixer_merge(o_lat, pg, pa, pb, pgh, pgc, *layer_w)

        h2 = (rmsnorm(x, norm2_w[l]) * (1 + sc2) + sh2).reshape(bsz, rows, GRID_W, D_MODEL)
        x = x + gt2 * conv_ffn(h2, w_up[l], ffn_dw_w[l], ffn_dw_b[l], w_down[l]).reshape(bsz, length, D_MODEL)

        if l < DEPTH - 1:
            cx = cx + cgt1 * mixer_merge(o_ctx, cg, ca, cb, cgh, cgc, *layer_w)
            hc2 = (rmsnorm(cx, norm2_w[l]) * (1 + csc2) + csh2)[:, None]
            cx = cx + cgt2 * conv_ffn(hc2, w_up[l], ffn_dw_w[l], ffn_dw_b[l], w_down[l])[:, 0]
    return rmsnorm(x, final_norm_w)
```

```python
import contextlib
import types
import numpy as np
import concourse.bass as bass
import concourse.mybir as mybir
from concourse.bass_utils import run_bass_kernel_spmd

F32 = mybir.dt.float32
BF16 = mybir.dt.bfloat16
AF = mybir.ActivationFunctionType
ALU = mybir.AluOpType

NSEM_ENG = 8
DEBUG = False
NLAYERS_RUN = 2

D = 1024
KD = 8
BATCH = 2
SEQ = 8192
DEPTH = 2
CTX = 256
PTOT = 9216
FF = 2816
KF = 22
NCORE = 8
NSEG = 4
NT = SEQ // NSEG
HX = 128
W = NT + 2 * HX
CS, CE = 1, 35
LO, HI = CS * 64, CE * 64
T0, T1 = HX, HX + NT
EPS = 1e-6
G_MIN = 1e-6
NST = 2 * 8 * 128 + 16

PL = 557
O_N1W, O_BMOD, O_GNW, O_CVW, O_CVB, O_LNW, O_LNB, O_N2W, O_FW, O_FB = 0, 8, 56, 57, 305, 313, 321, 329, 337, 535
O_LBZ = 2 * PL
O_FNW = O_LBZ + 32
NPV = O_FNW + 8


def freeze(fn, depth=0):
    if not isinstance(fn, types.FunctionType) or fn.__closure__ is None:
        return fn
    cells = []
    for c in fn.__closure__:
        try:
            v = c.cell_contents
        except ValueError:
            cells.append(c)
            continue
        if isinstance(v, types.FunctionType) and depth < 4:
            v = freeze(v, depth + 1)
        cells.append(types.CellType(v))
    g = types.FunctionType(fn.__code__, fn.__globals__, fn.__name__, fn.__defaults__, tuple(cells))
    g.__kwdefaults__ = fn.__kwdefaults__
    return g


class Buf:
    def __init__(self, name, t=None):
        self.name = name
        self.t = t
        self.last_w = None
        self.reads = []
        self.dsem = None

    def __getitem__(self, k):
        return self.t[k]


class Sched:
    ENGS = ("pe", "act", "dve", "pool", "sp")

    def __init__(self, nc, stack):
        self.nc = nc
        self.stack = stack
        self.streams = {e: [] for e in self.ENGS}
        self.count = {e: 0 for e in self.ENGS}
        self.sems = {}
        self.semval = {}
        self.waited = {e: {} for e in self.ENGS}
        self.final_events = []
        self.pending = {}
        self.last_ev = {}
        self.uid = 0
        self.free_dsems = []

    def sem(self, key):
        if key not in self.sems:
            self.sems[key] = self.stack.enter_context(self.nc.semaphore("s_%s_%s" % (key[0], key[1])))
            self.semval[key] = 0
        return self.sems[key]

    def sbuf(self, name, shape, dt, stack=None):
        self.uid += 1
        name = "%s_%d" % (name, self.uid)
        t = (stack or self.stack).enter_context(self.nc.sbuf_tensor(name, list(shape), dt))
        b = Buf(name, t)
        if stack is not None:
            stack.callback(self._release, b)
        return b

    def _release(self, b):
        if b.dsem is not None:
            self.free_dsems.append(b.dsem)
            b.dsem = None

    def psum(self, name, shape, dt):
        t = self.stack.enter_context(self.nc.psum_tensor(name, list(shape), dt))
        return Buf(name, t)

    def _deps(self, eng, reads, writes, acc):
        evs = []
        for b in reads:
            if b.last_w is not None:
                evs.append(b.last_w)
        for b in writes:
            if b.last_w is not None and not (acc and b.last_w[0][0] == "pe" and eng == "pe"):
                evs.append(b.last_w)
            evs.extend(b.reads)
        best = {}
        for (k, v) in evs:
            best[k] = max(best.get(k, 0), v)
        p = self.pending.pop(eng, None)
        if p:
            for (k, v) in p:
                best[k] = max(best.get(k, 0), v)
        waits = []
        w = self.waited[eng]
        for k, v in best.items():
            if w.get(k, 0) < v:
                waits.append((k, v))
                w[k] = v
        return waits

    def _commit(self, ev, reads, writes):
        for b in reads:
            b.reads.append(ev)
        for b in writes:
            b.last_w = ev
            b.reads = []
        self.last_ev[ev[0]] = ev[1]

    def op(self, eng, fn, reads=(), writes=(), acc=False):
        reads = [b for b in reads if b is not None]
        writes = [b for b in writes if b is not None]
        waits = self._deps(eng, reads, writes, acc)
        idx = self.count[eng]
        self.count[eng] += 1
        key = (eng, idx % NSEM_ENG)
        self.sem(key)
        ev = (key, idx // NSEM_ENG + 1)
        self.streams[eng].append((waits, freeze(fn), key, 1))
        self._commit(ev, reads, writes)
        return ev

    def dma(self, eng, out_ap, in_ap, reads=(), writes=(), sembuf=None):
        reads = [b for b in reads if b is not None]
        writes = [b for b in writes if b is not None]
        waits = self._deps(eng, reads, writes, False)
        if sembuf.dsem is None:
            if self.free_dsems:
                sembuf.dsem = self.free_dsems.pop()
            else:
                sembuf.dsem = ("dma", sembuf.name)
                self.sem(sembuf.dsem)
        key = sembuf.dsem
        self.semval[key] += 16
        ev = (key, self.semval[key])
        self.streams[eng].append((waits, lambda E: E.dma_start(out=out_ap, in_=in_ap), key, 16))
        self._commit(ev, reads, writes)
        return ev

    def barrier(self):
        evs = list(self.last_ev.items())
        for e in self.ENGS:
            self.pending[e] = list(evs)

    def emit(self, final=False):
        nc = self.nc
        sems = self.sems
        streams = self.streams
        final_events = list(self.last_ev.items()) if final else []
        with nc.Block() as block:
            def run(E, eng):
                for (waits, fn, key, inc) in streams[eng]:
                    for (k, v) in waits:
                        E.wait_ge(sems[k], v)
                    ins = fn(E)
                    ins.then_inc(sems[key], inc)
                if eng == "sp":
                    for (k, v) in final_events:
                        E.wait_ge(sems[k], v)
                streams[eng] = []

            @block.tensor
            def _(E):
                run(E, "pe")

            @block.scalar
            def _(E):
                run(E, "act")

            @block.vector
            def _(E):
                run(E, "dve")

            @block.gpsimd
            def _(E):
                run(E, "pool")

            @block.sync
            def _(E):
                run(E, "sp")


def blocks(lo, hi, n):
    out = []
    t = lo
    while t < hi:
        out.append((t, min(hi, t + n)))
        t += n
    return out


def build():
    nc = bass.Bass("TRN2", target_bir_lowering=False)
    dt_in = lambda name, shape: nc.dram_tensor(name, list(shape), F32, kind="ExternalInput").ap()
    xT4 = dt_in("xT4", [NSEG, KD, 128, W])
    cxT_in = dt_in("cxT", [KD, 128, CTX])
    cvec = dt_in("cvec", [128, KD, 2])
    valid4_d = dt_in("valid4", [NSEG, 128, W])
    pvec_d = dt_in("pvec", [128, NPV])
    cst_d = dt_in("cst", [128, 6, 128])
    rst_d = dt_in("rst", [128, 512])
    mcol_d = dt_in("mcol", [128, 2, W])
    fold4_d = dt_in("fold4", [NSEG, 128, 2, NSEG])
    validown_d = dt_in("validown", [128, W])
    foldown_d = dt_in("foldown", [128, 2, NSEG])
    onehot_d = dt_in("onehot", [128, NSEG])
    w_mod = dt_in("w_mod", [DEPTH, D, 6 * D])
    w_in = dt_in("w_in", [DEPTH, D, PTOT])
    w_hg_out = dt_in("w_hg_out", [DEPTH, D, D])
    w_cv_out = dt_in("w_cv_out", [DEPTH, D, D])
    w_out = dt_in("w_out", [DEPTH, D, D])
    w_up = dt_in("w_up", [DEPTH, D, 2 * FF])
    w_down = dt_in("w_down", [DEPTH, FF, D])
    x_out = nc.dram_tensor("x_out", [KD, 128, NT], F32, kind="ExternalOutput").ap()
    itn = lambda name, shape, dt=F32: nc.dram_tensor(name, list(shape), dt, kind="Internal").ap()
    ST = itn("ST", [NSEG, 128, NST])
    X1 = itn("X1", [NSEG, KD, 128, NT])
    XW = itn("XW", [NSEG, KD, 128, W])
    XWO = itn("XWO", [KD, 128, W])
    CX1 = itn("CX1", [KD, 128, CTX])
    SCB = itn("SCB", [8, 128, 128])
    xmid = itn("xmid", [KD, 128, W])
    xnew = itn("xnew", [KD, 128, NT])
    cxmid = itn("cxmid", [KD, 128, CTX])
    oT_d = itn("oT_d", [8, 128, W], BF16)
    oTc_d = itn("oTc_d", [8, 128, CTX], BF16)
    DR = {n: Buf(n) for n in ("xmid", "xnew", "cxmid", "oT_d", "oTc_d", "x_out", "cx_out", "ST", "X1", "XW", "XWO", "CX1", "SCB", "ext")}

    with contextlib.ExitStack() as st:
        S = Sched(nc, st)
        PS = [S.psum("ps%d" % i, [128, 512], F32) for i in range(7)]
        PSB = S.psum("psb", [128, 1024], BF16)
        psi = [0]

        def nps():
            psi[0] = (psi[0] + 1) % 7
            return PS[psi[0]]

        pvec = S.sbuf("pvec", [128, NPV], F32)
        cst = S.sbuf("cst", [128, 6, 128], F32)
        identb = S.sbuf("identb", [128, 128], BF16)
        maskFb = S.sbuf("maskFb", [32, 96], BF16)
        maskBb = S.sbuf("maskBb", [32, 96], BF16)
        valid = S.sbuf("valid", [128, W], BF16)
        ones_c = S.sbuf("ones_c", [128, CTX], BF16)
        rst = S.sbuf("rst", [128, 512], F32)
        foldm = S.sbuf("foldm", [128, 2, NSEG], F32)
        modv = S.sbuf("modv", [128, 48, 2], F32)
        A1 = S.sbuf("A1", [128, KD, 2], F32)
        A2 = S.sbuf("A2", [128, KD, 2], F32)
        lbv = S.sbuf("lbv", [128, 2, 3, KD], F32)
        rcp = S.sbuf("rcp", [128, KD], F32)
        epsc = S.sbuf("epsc", [128, 1], F32)
        carry = S.sbuf("carry", [128, 8, 128], F32)
        S.op("pool", lambda E: E.memset(epsc[:, :], EPS), writes=[epsc])
        S.dma("sp", pvec[:, :], pvec_d, writes=[pvec], sembuf=pvec)
        S.dma("sp", cst[:, :, :], cst_d, writes=[cst], sembuf=cst)
        S.dma("sp", rst[:, :], rst_d, writes=[rst], sembuf=rst)
        S.dma("pool", identb[:, :], cst_d[:, 0, :], writes=[identb], sembuf=identb)
        S.dma("pool", maskFb[:, :], cst_d[0:32, 3, 0:96], writes=[maskFb], sembuf=maskFb)
        S.dma("pool", maskBb[:, :], cst_d[0:32, 4, 0:96], writes=[maskBb], sembuf=maskBb)
        S.op("pool", lambda E: E.memset(ones_c[:, :], 1.0), writes=[ones_c])
        onesD = cst[:, 1, :]
        ones128 = cst[:, 2, :]

        def layer_setup(l):
            pb = l * PL
            for d in range(2):
                lb_ap = lbv[:, d, 0, :]
                if l == 0:
                    S.op("dve", lambda E, lb_ap=lb_ap: E.memset(lb_ap, 0.0), writes=[lbv])
                else:
                    z0 = pvec[:, O_LBZ + d * 16: O_LBZ + d * 16 + 8]
                    z1 = pvec[:, O_LBZ + d * 16 + 8: O_LBZ + d * 16 + 16]
                    S.op("dve", lambda E, lb_ap=lb_ap, z0=z0, z1=z1: E.tensor_tensor(out=lb_ap, in0=z1, in1=z0, op=ALU.subtract), reads=[pvec], writes=[lbv])
                    S.op("act", lambda E, lb_ap=lb_ap: E.activation(out=lb_ap, in_=lb_ap, func=AF.Sigmoid), reads=[lbv], writes=[lbv])
                a_ap = lbv[:, d, 1, :]
                sm_ap = lbv[:, d, 2, :]
                S.op("dve", lambda E, lb_ap=lb_ap, a_ap=a_ap: E.tensor_scalar(out=a_ap, in0=lb_ap, scalar1=-1.0, scalar2=1.0, op0=ALU.mult, op1=ALU.add), reads=[lbv], writes=[lbv])
                S.op("dve", lambda E, lb_ap=lb_ap, sm_ap=sm_ap: E.tensor_scalar(out=sm_ap, in0=lb_ap, scalar1=-1.0, scalar2=G_MIN, op0=ALU.mult, op1=ALU.add), reads=[lbv], writes=[lbv])
                S.op("dve", lambda E, a_ap=a_ap: E.reciprocal(out=rcp[:, :], in_=a_ap), reads=[lbv], writes=[rcp])
                S.op("dve", lambda E, sm_ap=sm_ap: E.tensor_tensor(out=sm_ap, in0=sm_ap, in1=rcp[:, :], op=ALU.mult), reads=[lbv, rcp], writes=[lbv])

            with contextlib.ExitStack() as ph:
                csb = S.sbuf("csb", [128, KD, 2], F32, ph)
                scb = S.sbuf("scb", [128, KD, 2], F32, ph)
                wm = [S.sbuf("wm%d" % i, [128, KD, 512], F32, ph) for i in range(2)]
                S.dma("sp", csb[:, :, :], cvec, writes=[csb], sembuf=csb)
                S.op("act", lambda E: E.activation(out=scb[:, :, :], in_=csb[:, :, :], func=AF.Silu), reads=[csb], writes=[scb])
                pm = PS[0]
                for piece in range(12):
                    buf = wm[piece % 2]
                    S.dma("sp", buf[:, :, :], w_mod[l, :, piece * 512:(piece + 1) * 512].rearrange("(k p) c -> p k c", p=128), writes=[buf], sembuf=buf)
                    for mm in range(4):
                        m = piece * 4 + mm
                        for k in range(KD):
                            S.op("pe", lambda E, buf=buf, k=k, mm=mm, m=m: E.matmul(pm[:, m * 2:m * 2 + 2], lhsT=buf[:, k, mm * 128:(mm + 1) * 128], rhs=scb[:, k, :], start=(k == 0), stop=(k == KD - 1)),
                                 reads=[buf, scb], writes=[pm], acc=True)
                S.op("dve", lambda E: E.tensor_tensor(out=modv[:, :, :], in0=pm[:, 0:96].rearrange("p (m s) -> p m s", s=2),
                                                      in1=pvec[:, pb + O_BMOD: pb + O_BMOD + 48].unsqueeze(2).to_broadcast([128, 48, 2]), op=ALU.add),
                     reads=[pm, pvec], writes=[modv])
                for (Ax, off, onw) in ((A1, 8, O_N1W), (A2, 32, O_N2W)):
                    S.op("dve", lambda E, Ax=Ax, off=off, onw=onw: E.scalar_tensor_tensor(out=Ax[:, :, :], in0=modv[:, off:off + 8, :], scalar=1.0,
                                                                                          in1=pvec[:, pb + onw: pb + onw + 8].unsqueeze(2).to_broadcast([128, 8, 2]),
                                                                                          op0=ALU.add, op1=ALU.mult), reads=[modv, pvec], writes=[Ax])
                S.barrier()
                S.emit()
        SH1 = lambda k, s: modv[:, 0 + k, s:s + 1]
        G1 = lambda k, s: modv[:, 16 + k, s:s + 1]
        SH2 = lambda k, s: modv[:, 24 + k, s:s + 1]
        G2 = lambda k, s: modv[:, 40 + k, s:s + 1]

        def wload(dst, src_ap):
            S.dma("pool", dst[:, :, :], src_ap.rearrange("(k p) c -> p k c", p=128), writes=[dst], sembuf=dst)

        def proj(ps_ap, psbuf, wbuf, wap_fn, rbuf, rap_fn, nk=KD):
            for k in range(nk):
                S.op("pe", lambda E, k=k: E.matmul(ps_ap, lhsT=wap_fn(k), rhs=rap_fn(k), start=(k == 0), stop=(k == nk - 1)),
                     reads=[wbuf, rbuf], writes=[psbuf], acc=True)

        class NormTmp:
            def __init__(self, ph, n=512, nbuf=1):
                self.xbs = [S.sbuf("n_xb", [128, KD, n], F32, ph) for _ in range(nbuf)]
                self.sqs = [S.sbuf("n_sq", [128, KD, n], F32, ph) for _ in range(nbuf)]
                self.cnt = 0
                self.xb = self.xbs[0]
                self.sq = self.sqs[0]
                self.rstd = S.sbuf("n_rstd", [128, n], F32, ph)
                self.tmp = [S.sbuf("n_tmp%d" % i, [128, n], F32, ph) for i in range(2)]

            def rotate(self):
                self.cnt += 1
                self.xb = self.xbs[self.cnt % len(self.xbs)]
                self.sq = self.sqs[self.cnt % len(self.sqs)]

        def norm_block(nt, src_ap, srcbuf, n, Afn, SHfn, outbuf, out_fn):
            nt.rotate()
            xb, sq = nt.xb, nt.sq
            S.dma("sp", xb[:, :, 0:n], src_ap, reads=[srcbuf], writes=[xb], sembuf=xb)
            S.op("act", lambda E: E.activation(out=sq[:, :, 0:n], in_=xb[:, :, 0:n], func=AF.Square), reads=[xb], writes=[sq])
            ps = nps()
            for k in range(KD):
                S.op("pe", lambda E, k=k: E.matmul(ps[:, 0:n], lhsT=onesD, rhs=sq[:, k, 0:n], start=(k == 0), stop=(k == KD - 1)),
                     reads=[cst, sq], writes=[ps], acc=True)
            S.op("act", lambda E: E.activation(out=nt.rstd[:, 0:n], in_=ps[:, 0:n], func=AF.Ln, bias=epsc[:, 0:1], scale=1.0), reads=[ps, epsc], writes=[nt.rstd])
            S.op("act", lambda E: E.activation(out=nt.rstd[:, 0:n], in_=nt.rstd[:, 0:n], func=AF.Exp, scale=-0.5), reads=[nt.rstd], writes=[nt.rstd])
            for k in range(KD):
                tmp = nt.tmp[k % 2]
                S.op("dve", lambda E, k=k, tmp=tmp: E.scalar_tensor_tensor(out=tmp[:, 0:n], in0=xb[:, k, 0:n], scalar=Afn(k), in1=nt.rstd[:, 0:n], op0=ALU.mult, op1=ALU.mult),
                     reads=[xb, nt.rstd, A1, A2, pvec], writes=[tmp])
                sh = SHfn(k)
                if sh is None:
                    S.op("act", lambda E, k=k, tmp=tmp: E.activation(out=out_fn(k), in_=tmp[:, 0:n], func=AF.Copy), reads=[tmp], writes=[outbuf])
                else:
                    S.op("act", lambda E, k=k, tmp=tmp, sh=sh: E.activation(out=out_fn(k), in_=tmp[:, 0:n], func=AF.Identity, bias=sh, scale=1.0),
                         reads=[tmp, modv], writes=[outbuf])


        def run_pass(kind, l, i, xT, cxT, DRX, DRC, x_dst, x_dst_buf, ctx_stream, is_last, valid_src, fold_src):
            pb = l * PL
            st_in = ST
            ctx_needed = not (l == 0 and i > 0)
            S.dma("sp", foldm[:, :, :], fold_src, writes=[foldm], sembuf=foldm)
            S.dma("pool", valid[:, :], valid_src, writes=[valid], sembuf=valid)
            cx_out = CX1
            mixer = contextlib.ExitStack()
            hT = S.sbuf("hT", [128, KD, W], BF16, mixer)
            hTc = S.sbuf("hTc", [128, KD, CTX], BF16, mixer)
            with contextlib.ExitStack() as ph:
                nt = NormTmp(ph, nbuf=2)
                for (b0, b1) in blocks(0, W, 512):
                    norm_block(nt, xT[:, :, b0:b1].rearrange("k p t -> p k t"), DRX, b1 - b0,
                               lambda k: A1[:, k, 0:1], lambda k: SH1(k, 0), hT, lambda k, b0=b0, b1=b1: hT[:, k, b0:b1])
                if kind == "B" and ctx_needed:
                    norm_block(nt, cxT.rearrange("k p t -> p k t"), DRC, CTX,
                               lambda k: A1[:, k, 1:2], lambda k: SH1(k, 1), hTc, lambda k: hTc[:, k, 0:CTX])
                S.barrier()
                S.emit()

            hg = contextlib.ExitStack()
            full = (kind == "B")
            NW = 5
            wh = [S.sbuf("wh%d" % i, [128, KD, NW * 128], BF16, hg) for i in range(2)]
            nchw = W // 64
            kT = {d: S.sbuf("kT%d" % d, [64, nchw, 128], BF16, hg) for d in range(2)}
            vT = S.sbuf("vT", [64, nchw, 128], BF16, hg)
            vT2 = S.sbuf("vT2", [32, nchw, 128], BF16, hg)
            stt_ = {d: S.sbuf("st%d" % d, [128, nchw], F32, hg) for d in range(2)}
            if full:
                qt = {d: S.sbuf("qt%d" % d, [128, W], BF16, hg) for d in range(2)}
                kt = {d: S.sbuf("kt%d" % d, [128, W], BF16, hg) for d in range(2)}
                srt = {d: S.sbuf("sr%d" % d, [128, nchw], F32, hg) for d in range(2)}
                of = S.sbuf("of", [128, W], F32, hg)
                ob2 = S.sbuf("ob2", [128, W], F32, hg)
                ob = kt[0]
                stin_r = [S.sbuf("stin0", [128, NSEG, 2, 128], F32, hg)] * 2
                stA = S.sbuf("stA", [128, NSEG, 16], F32, hg)
                S.dma("sp", stA[:, :, :], st_in[:, :, 2048:2064].rearrange("s p c -> p s c"), reads=[DR["ST"]], writes=[stA], sembuf=stA)
                Sbf = {d: [S.sbuf("Sbf%d_%d" % (d, i), [128, 128], BF16, hg) for i in range(2)] for d in range(2)}
                atall = {d: S.sbuf("atall%d" % d, [32, nchw, 96], BF16, hg) for d in range(2)}
                alpha = S.sbuf("alpha", [128, 2], F32, hg)
            Sst = {d: S.sbuf("Sst%d" % d, [128, 128], F32, hg) for d in range(2)}
            stpack = S.sbuf("stpack", [128, NST], F32, hg) if not full else None
            nslots = 1 if full else 2
            tpd = {(sl, d): {n: S.sbuf("t%d%d_%s" % (sl, d, n), [128, 512], F32, hg) for n in ("s", "sn", "c", "D", "dq", "e2", "e3")} for d in range(2) for sl in range(nslots)}
            tp = tpd[(0, 0)]
            tkbd = {(sl, d): S.sbuf("t_kb%d%d" % (sl, d), [128, 512], BF16, hg) for d in range(2) for sl in range(nslots)}
            tvbs = [S.sbuf("t_vb%d" % sl, [128, 512], BF16, hg) for sl in range(nslots)]

            def drive(gens):
                gens = list(gens)
                while gens:
                    for g_ in list(gens):
                        try:
                            next(g_)
                        except StopIteration:
                            gens.remove(g_)


            def tr32(src, dst, c0, nch, dst2=None):
                for j in range(nch):
                    S.op("pe", lambda E, j=j: E.transpose(PSB[0:64, j * 128:(j + 1) * 128], src[:, j * 64:(j + 1) * 64], identb[:, :]),
                         reads=[src, identb], writes=[PSB], acc=True)
                S.op("act", lambda E: E.activation(out=dst[:, c0:c0 + nch, :], in_=PSB[0:64, 0:nch * 128].rearrange("p (c d) -> p c d", d=128), func=AF.Copy),
                     reads=[PSB], writes=[dst])
                if dst2 is not None:
                    for j in range(nch):
                        S.op("pe", lambda E, j=j: E.transpose(PSB[0:32, j * 128:(j + 1) * 128], src[:, j * 64 + 32:(j + 1) * 64], identb[:, :]),
                             reads=[src, identb], writes=[PSB], acc=True)
                    S.op("act", lambda E: E.activation(out=dst2[:, c0:c0 + nch, :], in_=PSB[0:32, 0:nch * 128].rearrange("p (c d) -> p c d", d=128), func=AF.Copy),
                         reads=[PSB], writes=[dst2])

            def hg_gates(h, wb, hsrc, hbuf, t0, n, vmask, vbuf, want_q, dirs=(0, 1), slot=0, run=True):
                nch = n // 64
                c0 = t0 // 64
                v3 = lambda ap: ap.rearrange("p (c t) -> p c t", t=64)
                tvb = tvbs[slot]

                def vunit():
                    ps = nps()
                    proj(ps[:, 0:n], ps, wb, lambda k: wb[:, k, 3 * 128:4 * 128], hbuf, lambda k: hsrc[:, k, t0:t0 + n])
                    yield
                    S.op("act", lambda E: E.activation(out=tvb[:, 0:n], in_=ps[:, 0:n], func=AF.Copy), reads=[ps], writes=[tvb])
                    yield
                    yield
                    tr32(tvb, vT, c0, nch, vT2 if want_q else None)
                if want_q:
                    psq = nps()
                    proj(psq[:, 0:n], psq, wb, lambda k: wb[:, k, 0:128], hbuf, lambda k: hsrc[:, k, t0:t0 + n])
                def unit(d):
                    lb = lbv[:, d, 0, h:h + 1]
                    a = lbv[:, d, 1, h:h + 1]
                    smin = lbv[:, d, 2, h:h + 1]
                    psf = nps()
                    proj(psf[:, 0:n], psf, wb, lambda k, d=d: wb[:, k, (1 + d) * 128:(2 + d) * 128], hbuf, lambda k: hsrc[:, k, t0:t0 + n])
                    yield
                    T = tpd[(slot, d)]
                    s_, sn, cc, DD, dq, e2, e3 = T["s"], T["sn"], T["c"], T["D"], T["dq"], T["e2"], T["e3"]
                    lg, kk, e1 = s_, sn, dq
                    tkb_ = tkbd[(slot, d)]
                    S.op("act", lambda E: E.activation(out=s_[:, 0:n], in_=psf[:, 0:n], func=AF.Sigmoid), reads=[psf], writes=[s_])
                    S.op("act", lambda E: E.activation(out=sn[:, 0:n], in_=psf[:, 0:n], func=AF.Sigmoid, scale=-1.0), reads=[psf], writes=[sn])
                    yield
                    S.op("dve", lambda E, smin=smin: E.tensor_scalar(out=s_[:, 0:n], in0=s_[:, 0:n], scalar1=smin, scalar2=None, op0=ALU.max), reads=[s_, lbv], writes=[s_])
                    S.op("dve", lambda E, a=a: E.scalar_tensor_tensor(out=kk[:, 0:n], in0=sn[:, 0:n], scalar=a, in1=vmask, op0=ALU.mult, op1=ALU.mult),
                         reads=[sn, lbv, vbuf], writes=[kk])
                    yield
                    S.op("act", lambda E, a=a, lb=lb: E.activation(out=lg[:, 0:n], in_=s_[:, 0:n], func=AF.Ln, scale=a, bias=lb), reads=[s_, lbv], writes=[lg])
                    yield
                    S.op("pool", lambda E: E.tensor_tensor(out=lg[:, 0:n], in0=lg[:, 0:n], in1=vmask, op=ALU.mult), reads=[lg, vbuf], writes=[lg])
                    yield
                    S.op("dve", lambda E: E.tensor_tensor_scan(out=cc[:, 0:n], data0=rst[:, 0:n], data1=lg[:, 0:n], initial=0.0, op0=ALU.mult, op1=ALU.add),
                         reads=[rst, lg], writes=[cc])
                    yield
                    c63 = v3(cc[:, 0:n])[:, :, 63:64]
                    S.op("act", lambda E, d=d: E.activation(out=stt_[d][:, c0:c0 + nch], in_=v3(cc[:, 0:n])[:, :, 63], func=AF.Exp), reads=[cc], writes=[stt_[d]])
                    if d == 0:
                        Dv = cc
                    else:
                        S.op("dve", lambda E: E.tensor_tensor(out=DD[:, 0:n], in0=lg[:, 0:n], in1=cc[:, 0:n], op=ALU.subtract), reads=[lg, cc], writes=[DD])
                        S.op("dve", lambda E, c63=c63: E.tensor_tensor(out=v3(DD[:, 0:n]), in0=v3(DD[:, 0:n]), in1=c63.to_broadcast([128, nch, 64]), op=ALU.add),
                             reads=[DD, cc], writes=[DD])
                        Dv = DD
                    yield
                    S.op("dve", lambda E, c63=c63, Dv=Dv: E.tensor_tensor(out=v3(e3[:, 0:n]), in0=v3(Dv[:, 0:n]), in1=c63.to_broadcast([128, nch, 64]), op=ALU.subtract),
                         reads=[Dv, cc], writes=[e3])
                    if want_q:
                        dref = v3(Dv[:, 0:n])[:, :, 31:32]
                        S.op("dve", lambda E, Dv=Dv, dref=dref: E.tensor_tensor(out=v3(dq[:, 0:n]), in0=v3(Dv[:, 0:n]), in1=dref.to_broadcast([128, nch, 64]), op=ALU.subtract),
                             reads=[Dv], writes=[dq])
                    yield
                    S.op("act", lambda E: E.activation(out=e3[:, 0:n], in_=e3[:, 0:n], func=AF.Exp, scale=-1.0), reads=[e3], writes=[e3])
                    if want_q:
                        S.op("act", lambda E, d=d, Dv=Dv: E.activation(out=srt[d][:, c0:c0 + nch], in_=v3(Dv[:, 0:n])[:, :, 31], func=AF.Exp), reads=[Dv], writes=[srt[d]])
                        S.op("act", lambda E: E.activation(out=e2[:, 0:n], in_=dq[:, 0:n], func=AF.Exp, scale=-1.0), reads=[dq], writes=[e2])
                        S.op("act", lambda E: E.activation(out=e1[:, 0:n], in_=dq[:, 0:n], func=AF.Exp), reads=[dq], writes=[e1])
                    yield
                    S.op("pool", lambda E: E.tensor_tensor(out=tkb_[:, 0:n], in0=kk[:, 0:n], in1=e3[:, 0:n], op=ALU.mult), reads=[kk, e3], writes=[tkb_])
                    if want_q:
                        S.op("dve", lambda E, d=d: E.tensor_tensor(out=qt[d][:, t0:t0 + n], in0=psq[:, 0:n], in1=e1[:, 0:n], op=ALU.mult), reads=[psq, e1], writes=[qt[d]])
                        S.op("pool", lambda E, d=d: E.tensor_tensor(out=kt[d][:, t0:t0 + n], in0=kk[:, 0:n], in1=e2[:, 0:n], op=ALU.mult), reads=[kk, e2], writes=[kt[d]])
                    yield
                    tr32(tkb_, kT[d], c0, nch)

                gens = [unit(d) for d in dirs] + [vunit()]
                if not run:
                    return gens
                drive(gens)

            def hg_scan(chunks_f, chunks_b, with_out, obase, snap=None):
                nsteps = max(len(chunks_f), len(chunks_b))
                POB = {0: [PS[0], PS[6]], 1: [PS[1], PS[3]]}
                PA = {0: PS[2], 1: PS[3]}
                PP = {0: PS[4], 1: PS[5]}
                masks = {0: maskFb, 1: maskBb}
                pending = {0: [], 1: []}

                def flush(d):
                    if not pending[d]:
                        return
                    cl = sorted(pending[d])
                    ta, tb = cl[0] * 64, (cl[-1] + 1) * 64
                    pa = ((cl[0] * 64) % 512)
                    n = tb - ta
                    pob = POB[d][(ta // 512) % 2]
                    dst_ = of if d == 0 else ob2
                    S.op("act", lambda E: E.activation(out=dst_[:, ta:tb], in_=pob[:, pa:pa + n], func=AF.Copy), reads=[pob], writes=[dst_])
                    pending[d] = []

                if with_out:
                    PAr = [PS[2], PS[3], PS[4], PS[5]]
                    pai = 0
                    for i in range(nsteps):
                        for d, chl in ((0, chunks_f), (1, chunks_b)):
                            if i >= len(chl):
                                continue
                            c = chl[i]
                            tsl = slice(c * 64, (c + 1) * 64)
                            h1 = slice(c * 64, c * 64 + 32)
                            h2 = slice(c * 64 + 32, c * 64 + 64)
                            ka, kb_ = (h1, h2) if d == 0 else (h2, h1)
                            pa_ = PAr[pai % 4]
                            pai += 1
                            S.op("pe", lambda E, d=d, tsl=tsl, ka=ka, pa_=pa_: E.matmul(pa_[0:32, 0:64], lhsT=kt[d][:, ka], rhs=qt[d][:, tsl], start=True, stop=True),
                                 reads=[kt[d], qt[d]], writes=[pa_])
                            S.op("pe", lambda E, d=d, kb_=kb_, pa_=pa_: E.matmul(pa_[0:32, 64:96], lhsT=kt[d][:, kb_], rhs=qt[d][:, kb_], start=True, stop=True),
                                 reads=[kt[d], qt[d]], writes=[pa_], acc=True)
                            S.op("dve", lambda E, d=d, c=c, pa_=pa_: E.tensor_tensor(out=atall[d][:, c, :], in0=pa_[0:32, 0:96], in1=masks[d][:, :], op=ALU.mult),
                                 reads=[pa_, masks[d]], writes=[atall[d]])
                for i in range(nsteps):
                    for d, chl in ((0, chunks_f), (1, chunks_b)):
                        if i >= len(chl):
                            continue
                        c = chl[i]
                        tsl = slice(c * 64, (c + 1) * 64)
                        if with_out:
                            sb = Sbf[d][i % 2]
                            S.op("dve", lambda E, d=d, sb=sb, c=c: E.tensor_scalar(out=sb[:, :], in0=Sst[d][:, :], scalar1=srt[d][:, c:c + 1], scalar2=None, op0=ALU.mult),
                                 reads=[Sst[d], srt[d]], writes=[sb])
                            va = vT[0:32, c, :] if d == 0 else vT2[:, c, :]
                            vb = vT2[:, c, :] if d == 0 else vT[0:32, c, :]
                            po = (c * 64) % 512
                            if pending[d] and (c * 64) // 512 != (pending[d][0] * 64) // 512:
                                flush(d)
                            po2 = po + 32 if d == 0 else po
                            pob = POB[d][((c * 64) // 512) % 2]
                            S.op("pe", lambda E, d=d, sb=sb, tsl=tsl, po=po, pob=pob: E.matmul(pob[:, po:po + 64], lhsT=sb[:, :], rhs=qt[d][:, tsl], start=True, stop=False),
                                 reads=[sb, qt[d]], writes=[pob], acc=True)
                            S.op("pe", lambda E, d=d, c=c, va=va, po=po, pob=pob: E.matmul(pob[:, po:po + 64], lhsT=va, rhs=atall[d][:, c, 0:64], start=False, stop=False),
                                 reads=[vT, vT2, atall[d]], writes=[pob], acc=True)
                            S.op("pe", lambda E, d=d, c=c, vb=vb, po2=po2, pob=pob: E.matmul(pob[:, po2:po2 + 32], lhsT=vb, rhs=atall[d][:, c, 64:96], start=False, stop=True),
                                 reads=[vT, vT2, atall[d]], writes=[pob], acc=True)
                            pending[d].append(c)
                        S.op("pe", lambda E, d=d, c=c: E.matmul(PP[d][:, 0:128], lhsT=kT[d][:, c, :], rhs=vT[:, c, :], start=True, stop=True),
                             reads=[kT[d], vT], writes=[PP[d]])
                        S.op("dve", lambda E, d=d, c=c: E.scalar_tensor_tensor(out=Sst[d][:, :], in0=Sst[d][:, :], scalar=stt_[d][:, c:c + 1], in1=PP[d][:, 0:128], op0=ALU.mult, op1=ALU.add),
                             reads=[Sst[d], stt_[d], PP[d]], writes=[Sst[d]])
                        if snap is not None and d == 0 and c == snap[0]:
                            S.op("act", lambda E: E.activation(out=snap[1], in_=Sst[0][:, :], func=AF.Copy), reads=[Sst[0]], writes=[carry])
                if with_out:
                    flush(0)
                    flush(1)

            def gnorm_store(h, lo, hi, dst_d, dstbuf, wb, hsrc, hbuf):
                gnw = pvec[:, pb + O_GNW: pb + O_GNW + 1]
                for (b0, b1) in blocks(lo, hi, 512):
                    n = b1 - b0
                    osq, orn, sgt = tp["s"], tp["sn"], tp["c"]
                    psg = nps()
                    proj(psg[:, 0:n], psg, wb, lambda k: wb[:, k, 4 * 128:5 * 128], hbuf, lambda k: hsrc[:, k, b0:b1])
                    S.op("act", lambda E: E.activation(out=sgt[:, 0:n], in_=psg[:, 0:n], func=AF.Silu), reads=[psg], writes=[sgt])
                    S.op("dve", lambda E: E.tensor_tensor(out=of[:, b0:b1], in0=of[:, b0:b1], in1=ob2[:, b0:b1], op=ALU.add), reads=[of, ob2], writes=[of])
                    S.op("act", lambda E: E.activation(out=osq[:, 0:n], in_=of[:, b0:b1], func=AF.Square), reads=[of], writes=[osq])
                    ps = nps()
                    S.op("pe", lambda E: E.matmul(ps[:, 0:n], lhsT=ones128, rhs=osq[:, 0:n], start=True, stop=True), reads=[cst, osq], writes=[ps])
                    S.op("act", lambda E: E.activation(out=orn[:, 0:n], in_=ps[:, 0:n], func=AF.Ln, bias=epsc[:, 0:1], scale=1.0), reads=[ps, epsc], writes=[orn])
                    S.op("act", lambda E: E.activation(out=orn[:, 0:n], in_=orn[:, 0:n], func=AF.Exp, scale=-0.5), reads=[orn], writes=[orn])
                    S.op("dve", lambda E: E.tensor_tensor(out=orn[:, 0:n], in0=orn[:, 0:n], in1=of[:, b0:b1], op=ALU.mult), reads=[orn, of], writes=[orn])
                    S.op("dve", lambda E: E.scalar_tensor_tensor(out=ob[:, b0:b1], in0=orn[:, 0:n], scalar=gnw, in1=sgt[:, 0:n], op0=ALU.mult, op1=ALU.mult),
                         reads=[orn, pvec, sgt], writes=[ob])
                S.dma("sp", dst_d[h, :, lo:hi], ob[:, lo:hi], reads=[ob], writes=[dstbuf], sembuf=ob)

            def load_head_w(hh):
                wb_ = wh[hh % 2]
                for g in range(NW):
                    S.dma("pool", wb_[:, :, g * 128:(g + 1) * 128], w_in[l, :, g * 1024 + hh * 128: g * 1024 + (hh + 1) * 128].rearrange("(k p) c -> p k c", p=128),
                          writes=[wb_], sembuf=wb_)

            load_head_w(0)
            for h in range(8):
                wb = wh[h % 2]
                if h + 1 < 8:
                    load_head_w(h + 1)
                if kind == "A":
                    dirs = [d for d in (0, 1) if (d == 0 and i < NSEG - 1 and l > 0) or (d == 1 and i > 0)]
                    if h == 0:
                        S.op("pool", lambda E: E.memset(stpack[:, :], 0.0), writes=[stpack])
                    blks = blocks(LO, HI, 512)
                    for j in range(0, len(blks), 2):
                        gens = []
                        for sl, (b0, b1) in enumerate(blks[j:j + 2]):
                            gens += hg_gates(h, wb, hT, hT, b0, b1 - b0, valid[:, b0:b1], valid, False, dirs, slot=sl, run=False)
                        drive(gens)
                    for d in dirs:
                        S.op("pool", lambda E, d=d: E.memset(Sst[d][:, :], 0.0), writes=[Sst[d]])
                    hg_scan(list(range(CS, CS + 32)) if 0 in dirs else [], list(range(CE - 1, CE - 33, -1)) if 1 in dirs else [], False, 0)
                    for d in dirs:
                        S.op("act", lambda E, d=d, h=h: E.activation(out=stpack[:, (d * 8 + h) * 128:(d * 8 + h + 1) * 128], in_=Sst[d][:, :], func=AF.Copy), reads=[Sst[d]], writes=[stpack])
                    for d, (ca, cb_) in ((0, (CS, CS + 32)), (1, (CE - 32, CE))):
                        if d not in dirs:
                            continue
                        col = 2048 + d * 8 + h
                        lgt = tp["dq"]
                        S.op("act", lambda E, d=d, ca=ca, cb_=cb_: E.activation(out=lgt[:, 0:32], in_=stt_[d][:, ca:cb_], func=AF.Ln), reads=[stt_[d]], writes=[lgt])
                        S.op("dve", lambda E: E.tensor_reduce(out=lgt[:, 32:33], in_=lgt[:, 0:32], axis=mybir.AxisListType.X, op=ALU.add), reads=[lgt], writes=[lgt])
                        S.op("act", lambda E, col=col: E.activation(out=stpack[:, col:col + 1], in_=lgt[:, 32:33], func=AF.Exp), reads=[lgt], writes=[stpack])
                else:
                    if ctx_needed:
                        hg_gates(h, wb, hTc, hTc, 0, CTX, ones_c[:, 0:CTX], ones_c, ctx_stream)
                        for d in range(2):
                            S.op("pool", lambda E, d=d: E.memset(Sst[d][:, :], 0.0), writes=[Sst[d]])
                        hg_scan([0, 1, 2, 3], [3, 2, 1, 0], ctx_stream, 0)
                        if ctx_stream:
                            gnorm_store(h, 0, CTX, oTc_d, DR["oTc_d"], wb, hTc, hTc)
                        if l == 0:
                            S.dma("sp", SCB[h], Sst[1][:, :], reads=[Sst[1]], writes=[DR["SCB"]], sembuf=Sst[1])
                    else:
                        S.dma("sp", Sst[1][:, :], SCB[h], reads=[DR["SCB"]], writes=[Sst[1]], sembuf=Sst[1])
                    stin = stin_r[h % 2]
                    for d in range(2):
                        S.dma("sp", stin[:, :, d, 0:128], st_in[:, :, (d * 8 + h) * 128:(d * 8 + h + 1) * 128].rearrange("s p c -> p s c"), reads=[DR["ST"]], writes=[stin], sembuf=stin)
                    for d in range(2):
                        if l == 0 and d == 0:
                            if i > 0:
                                S.op("act", lambda E, h=h: E.activation(out=Sst[0][:, :], in_=carry[:, h, :], func=AF.Copy), reads=[carry], writes=[Sst[0]])
                            continue
                        order = range(NSEG) if d == 0 else range(NSEG - 1, -1, -1)
                        for kseg in order:
                            m = foldm[:, d, kseg:kseg + 1]
                            S.op("dve", lambda E, kseg=kseg, d=d, m=m, h=h: E.tensor_scalar(out=alpha[:, 0:1], in0=stA[:, kseg, d * 8 + h:d * 8 + h + 1], scalar1=-1.0, scalar2=m, op0=ALU.add, op1=ALU.mult),
                                 reads=[stA, foldm], writes=[alpha])
                            S.op("dve", lambda E: E.tensor_scalar(out=alpha[:, 0:1], in0=alpha[:, 0:1], scalar1=1.0, scalar2=None, op0=ALU.add), reads=[alpha], writes=[alpha])
                            S.op("dve", lambda E, d=d: E.tensor_scalar(out=Sst[d][:, :], in0=Sst[d][:, :], scalar1=alpha[:, 0:1], scalar2=None, op0=ALU.mult), reads=[Sst[d], alpha], writes=[Sst[d]])
                            S.op("dve", lambda E, d=d, kseg=kseg, m=m, stin=stin: E.scalar_tensor_tensor(out=Sst[d][:, :], in0=stin[:, kseg, d, 0:128], scalar=m, in1=Sst[d][:, :], op0=ALU.mult, op1=ALU.add),
                                 reads=[stin, foldm, Sst[d]], writes=[Sst[d]])
                    for (b0, b1) in blocks(LO, HI, 512):
                        hg_gates(h, wb, hT, hT, b0, b1 - b0, valid[:, b0:b1], valid, True)
                    hg_scan(list(range(CS, CE)), list(range(CE - 1, CS - 1, -1)), True, 0, snap=((CS + 31, carry[:, h, :]) if l == 0 else None))
                    gnorm_store(h, LO, HI, oT_d, DR["oT_d"], wb, hT, hT)
                S.emit()

            if kind == "A":
                S.dma("sp", ST[i], stpack[:, :], reads=[stpack], writes=[DR["ST"]], sembuf=stpack)
                S.barrier()
                S.emit()
                hg.close()
                mixer.close()
                return
            S.barrier()
            S.emit()
            hg.close()
            mixer.close()

            with contextlib.ExitStack() as ph:
                wres = S.sbuf("wres", [128, KD, 4096], BF16, ph)
                for j in range(8):
                    S.dma("pool", wres[:, :, j * 512:(j + 1) * 512], w_in[l, :, 5120 + j * 512: 5120 + (j + 1) * 512].rearrange("(k p) c -> p k c", p=128), writes=[wres], sembuf=wres)
                w3r = [S.sbuf("w3r%d" % i, [128, KD, 128], BF16, ph) for i in range(3)]
                w3i = [0]

                def w3load(src):
                    w3i[0] = (w3i[0] + 1) % 3
                    b = w3r[w3i[0]]
                    S.dma("pool", b[:, :, :], src.rearrange("(k p) c -> p k c", p=128), writes=[b], sembuf=b)
                    return b
                nt = NormTmp(ph)
                hTb = S.sbuf("hTb", [128, KD, 512], BF16, ph)
                ubs = [S.sbuf("ub%d" % q, [128, 512], F32, ph) for q in range(2)]
                sbbs = [S.sbuf("sbb%d" % q, [128, 512], F32, ph) for q in range(2)]
                accs = [S.sbuf("acc%d" % q, [128, 480], F32, ph) for q in range(2)]
                vTb = S.sbuf("vTb", [128, KD, 480], F32, ph)
                mean = S.sbuf("mean", [128, 480], F32, ph)
                var = S.sbuf("var", [128, 480], F32, ph)
                t1 = [S.sbuf("t1_%d" % i, [128, 480], F32, ph) for i in range(2)]
                yc = S.sbuf("yc", [128, KD, 480], BF16, ph)
                yT = S.sbuf("yT", [128, KD, 480], BF16, ph)
                oTb = S.sbuf("oTb", [128, 8, 480], BF16, ph)
                sgcs = [S.sbuf("sgc%d" % q, [128, 480], F32, ph) for q in range(2)]
                sghs = [S.sbuf("sgh%d" % q, [128, 480], F32, ph) for q in range(2)]
                m1s = [S.sbuf("m1_0", [128, 480], F32, ph)] * 2
                m2s = [S.sbuf("m2_0", [128, 480], F32, ph)] * 2
                xo = [S.sbuf("xo%d" % i, [128, 480], F32, ph) for i in range(2)]
                cvw = lambda tap, c: pvec[:, pb + O_CVW + tap * 8 + c: pb + O_CVW + tap * 8 + c + 1]

                def merge_seq(xsrc, lo, hi, srclo, srchi, s, oTsrc, oTbuf, vmask_buf, dst, dstbuf):
                    for (b0, b1) in blocks(lo, hi, 480):
                        n = b1 - b0
                        a0, a1 = max(srclo, b0 - 15), min(srchi, b1 + 15)
                        m = a1 - a0
                        off = a0 - (b0 - 15)
                        ctr = b0 - a0
                        norm_block(nt, xsrc[:, :, a0:a1].rearrange("k p t -> p k t"), DRX, m,
                                   lambda k: A1[:, k, s:s + 1], lambda k: SH1(k, s), hTb, lambda k: hTb[:, k, 0:m])
                        S.dma("sp", oTb[:, :, 0:n], oTsrc[:, :, b0:b1].rearrange("h p t -> p h t"), reads=[oTbuf], writes=[oTb], sembuf=oTb)
                        def cunit(c, slot):
                            ub_, sbb_, acc_ = ubs[slot], sbbs[slot], accs[slot]
                            psa = nps()
                            proj(psa[:, 0:m], psa, wres, lambda k, c=c: wres[:, k, c * 128:(c + 1) * 128], hTb, lambda k: hTb[:, k, 0:m])
                            psb_ = nps()
                            proj(psb_[:, 0:m], psb_, wres, lambda k, c=c: wres[:, k, 1024 + c * 128:1024 + (c + 1) * 128], hTb, lambda k: hTb[:, k, 0:m])
                            yield
                            S.op("act", lambda E: E.activation(out=sbb_[:, 0:m], in_=psb_[:, 0:m], func=AF.Sigmoid), reads=[psb_], writes=[sbb_])
                            yield
                            S.op("pool", lambda E: E.tensor_tensor(out=sbb_[:, 0:m], in0=sbb_[:, 0:m], in1=vmask_buf[:, a0:a1], op=ALU.mult), reads=[sbb_, vmask_buf], writes=[sbb_])
                            if off > 0 or m < n + 30:
                                S.op("pool", lambda E: E.memset(ub_[:, :], 0.0), writes=[ub_])
                            yield
                            S.op("dve", lambda E: E.tensor_tensor(out=ub_[:, off:off + m], in0=psa[:, 0:m], in1=sbb_[:, 0:m], op=ALU.mult), reads=[psa, sbb_], writes=[ub_])
                            yield
                            S.op("dve", lambda E, c=c: E.tensor_scalar(out=acc_[:, 0:n], in0=ub_[:, 0:n], scalar1=cvw(0, c), scalar2=pvec[:, pb + O_CVB + c: pb + O_CVB + c + 1], op0=ALU.mult, op1=ALU.add),
                                 reads=[ub_, pvec], writes=[acc_])
                            for tap in range(1, 31):
                                yield
                                S.op("dve", lambda E, c=c, tap=tap: E.scalar_tensor_tensor(out=acc_[:, 0:n], in0=ub_[:, tap:tap + n], scalar=cvw(tap, c), in1=acc_[:, 0:n], op0=ALU.mult, op1=ALU.add),
                                     reads=[ub_, pvec, acc_], writes=[acc_])
                            yield
                            S.op("act", lambda E, c=c: E.activation(out=vTb[:, c, 0:n], in_=acc_[:, 0:n], func=AF.Copy), reads=[acc_], writes=[vTb])

                        for c0_ in range(0, KD, 2):
                            gens = [cunit(c0_, 0), cunit(c0_ + 1, 1)]
                            while gens:
                                for g_ in list(gens):
                                    try:
                                        next(g_)
                                    except StopIteration:
                                        gens.remove(g_)
                        S.op("act", lambda E: E.activation(out=nt.sq[:, :, 0:n], in_=vTb[:, :, 0:n], func=AF.Square), reads=[vTb], writes=[nt.sq])
                        psm, psq = nps(), nps()
                        for k in range(KD):
                            S.op("pe", lambda E, k=k: E.matmul(psm[:, 0:n], lhsT=onesD, rhs=vTb[:, k, 0:n], start=(k == 0), stop=(k == KD - 1)), reads=[cst, vTb], writes=[psm], acc=True)
                        for k in range(KD):
                            S.op("pe", lambda E, k=k: E.matmul(psq[:, 0:n], lhsT=onesD, rhs=nt.sq[:, k, 0:n], start=(k == 0), stop=(k == KD - 1)), reads=[cst, nt.sq], writes=[psq], acc=True)
                        S.op("act", lambda E: E.activation(out=mean[:, 0:n], in_=psm[:, 0:n], func=AF.Copy), reads=[psm], writes=[mean])
                        S.op("dve", lambda E: E.tensor_tensor(out=var[:, 0:n], in0=mean[:, 0:n], in1=mean[:, 0:n], op=ALU.mult), reads=[mean], writes=[var])
                        S.op("dve", lambda E: E.tensor_tensor(out=var[:, 0:n], in0=psq[:, 0:n], in1=var[:, 0:n], op=ALU.subtract), reads=[psq, var], writes=[var])
                        S.op("act", lambda E: E.activation(out=var[:, 0:n], in_=var[:, 0:n], func=AF.Ln, bias=epsc[:, 0:1], scale=1.0), reads=[var, epsc], writes=[var])
                        S.op("act", lambda E: E.activation(out=var[:, 0:n], in_=var[:, 0:n], func=AF.Exp, scale=-0.5), reads=[var], writes=[var])
                        for c in range(KD):
                            tt = t1[c % 2]
                            S.op("dve", lambda E, c=c, tt=tt: E.tensor_tensor(out=tt[:, 0:n], in0=vTb[:, c, 0:n], in1=mean[:, 0:n], op=ALU.subtract), reads=[vTb, mean], writes=[tt])
                            S.op("pool", lambda E, tt=tt: E.tensor_tensor(out=tt[:, 0:n], in0=tt[:, 0:n], in1=var[:, 0:n], op=ALU.mult), reads=[tt, var], writes=[tt])
                            S.op("act", lambda E, c=c, tt=tt: E.activation(out=yc[:, c, 0:n], in_=tt[:, 0:n], func=AF.Silu, scale=pvec[:, pb + O_LNW + c: pb + O_LNW + c + 1], bias=pvec[:, pb + O_LNB + c: pb + O_LNB + c + 1]),
                                 reads=[tt, pvec], writes=[yc])
                        for dd in range(KD):
                            sgc, sgh, m1, m2 = sgcs[dd % 2], sghs[dd % 2], m1s[dd % 2], m2s[dd % 2]
                            wcv_ = w3load(w_cv_out[l, :, dd * 128:(dd + 1) * 128])
                            ps1 = nps()
                            proj(ps1[:, 0:n], ps1, wcv_, lambda k, wcv_=wcv_: wcv_[:, k, :], yc, lambda k: yc[:, k, 0:n])
                            psg = nps()
                            proj(psg[:, 0:n], psg, wres, lambda k, dd=dd: wres[:, k, 3072 + dd * 128:3072 + (dd + 1) * 128], hTb, lambda k: hTb[:, k, ctr:ctr + n])
                            S.op("act", lambda E: E.activation(out=sgc[:, 0:n], in_=psg[:, 0:n], func=AF.Sigmoid), reads=[psg], writes=[sgc])
                            S.op("dve", lambda E: E.tensor_tensor(out=m1[:, 0:n], in0=ps1[:, 0:n], in1=sgc[:, 0:n], op=ALU.mult), reads=[ps1, sgc], writes=[m1])
                            whg_ = w3load(w_hg_out[l, :, dd * 128:(dd + 1) * 128])
                            ps2 = nps()
                            proj(ps2[:, 0:n], ps2, whg_, lambda k, whg_=whg_: whg_[:, k, :], oTb, lambda k: oTb[:, k, 0:n])
                            psh = nps()
                            proj(psh[:, 0:n], psh, wres, lambda k, dd=dd: wres[:, k, 2048 + dd * 128:2048 + (dd + 1) * 128], hTb, lambda k: hTb[:, k, ctr:ctr + n])
                            S.op("act", lambda E: E.activation(out=sgh[:, 0:n], in_=psh[:, 0:n], func=AF.Sigmoid), reads=[psh], writes=[sgh])
                            S.op("dve", lambda E: E.tensor_tensor(out=m2[:, 0:n], in0=ps2[:, 0:n], in1=sgh[:, 0:n], op=ALU.mult), reads=[ps2, sgh], writes=[m2])
                            S.op("pool", lambda E, dd=dd: E.tensor_tensor(out=yT[:, dd, 0:n], in0=m1[:, 0:n], in1=m2[:, 0:n], op=ALU.add), reads=[m1, m2], writes=[yT])
                        for e in range(KD):
                            wo_ = w3load(w_out[l, :, e * 128:(e + 1) * 128])
                            pso = nps()
                            proj(pso[:, 0:n], pso, wo_, lambda k, wo_=wo_: wo_[:, k, :], yT, lambda k: yT[:, k, 0:n])
                            xb_ = xo[e % 2]
                            S.op("dve", lambda E, e=e, xb_=xb_: E.scalar_tensor_tensor(out=xb_[:, 0:n], in0=pso[:, 0:n], scalar=G1(e, s), in1=nt.xb[:, e, ctr:ctr + n], op0=ALU.mult, op1=ALU.add),
                                 reads=[pso, modv, nt.xb], writes=[xb_])
                            S.dma("sp", dst[e, :, b0:b1], xb_[:, 0:n], reads=[xb_], writes=[dstbuf], sembuf=xb_)
                    S.emit()

                merge_seq(xT, LO, HI, 0, W, 0, oT_d, DR["oT_d"], valid, xmid, DR["xmid"])
                if ctx_stream:
                    merge_seq(cxT, 0, CTX, 0, CTX, 1, oTc_d, DR["oTc_d"], ones_c, cxmid, DR["cxmid"])
                S.barrier()
                S.emit()

            NF = HI - LO
            ffn = contextlib.ExitStack()
            h2T = S.sbuf("h2T", [128, KD, NF], BF16, ffn)
            h2Tc = S.sbuf("h2Tc", [128, KD, CTX], BF16, ffn) if ctx_stream else None
            with contextlib.ExitStack() as ph:
                nt = NormTmp(ph, nbuf=2)
                for (b0, b1) in blocks(LO, HI, 512):
                    norm_block(nt, xmid[:, :, b0:b1].rearrange("k p t -> p k t"), DR["xmid"], b1 - b0,
                               lambda k: A2[:, k, 0:1], lambda k: SH2(k, 0), h2T, lambda k, b0=b0, b1=b1: h2T[:, k, b0 - LO:b1 - LO])
                if ctx_stream:
                    norm_block(nt, cxmid.rearrange("k p t -> p k t"), DR["cxmid"], CTX,
                               lambda k: A2[:, k, 1:2], lambda k: SH2(k, 1), h2Tc, lambda k: h2Tc[:, k, 0:CTX])
                S.barrier()
                S.emit()
            zT = S.sbuf("zT", [128, KF, NT], BF16, ffn)
            zTc = S.sbuf("zTc", [128, KF, CTX], BF16, ffn) if ctx_stream else None
            mcb = S.sbuf("mcb", [128, 2, 64], BF16, ffn)
            S.dma("pool", mcb[:, :, :], mcol_d[:, :, 0:64], writes=[mcb], sembuf=mcb)
            with contextlib.ExitStack() as ph:
                uc = [S.sbuf("uc%d" % i, [128, NF + 2], F32, ph) for i in range(3)]
                FH = NT // 2
                fa1s = [S.sbuf("fa1_%d" % q, [128, FH], F32, ph) for q in range(2)]
                fa2s = [S.sbuf("fa2_%d" % q, [128, FH], F32, ph) for q in range(2)]
                wuv = [S.sbuf("wuv%d" % i, [128, KD, 256], BF16, ph) for i in range(2)]
                for i in range(3):
                    S.op("pool", lambda E, i=i: E.memset(uc[i][:, :], 0.0), writes=[uc[i]])
                fw = lambda tap, c: pvec[:, pb + O_FW + tap * KF + c: pb + O_FW + tap * KF + c + 1]
                fb = lambda c: pvec[:, pb + O_FB + c: pb + O_FB + c + 1]

                def conv_taps(taps, n, c, fa1, fa2):
                    for j, (si, st0, tap) in enumerate(taps):
                        if j == 0:
                            S.op("dve", lambda E, si=si, st0=st0, tap=tap: E.tensor_scalar(out=fa1[:, 0:n], in0=uc[si][:, st0:st0 + n], scalar1=fw(tap, c), scalar2=fb(c), op0=ALU.mult, op1=ALU.add),
                                 reads=[uc[si], pvec], writes=[fa1])
                        else:
                            S.op("dve", lambda E, si=si, st0=st0, tap=tap: E.scalar_tensor_tensor(out=fa1[:, 0:n], in0=uc[si][:, st0:st0 + n], scalar=fw(tap, c), in1=fa1[:, 0:n], op0=ALU.mult, op1=ALU.add),
                                 reads=[uc[si], pvec, fa1], writes=[fa1])
                        yield
                    S.op("act", lambda E: E.activation(out=fa2[:, 0:n], in_=fa1[:, 0:n], func=AF.Gelu), reads=[fa1], writes=[fa2])

                def drive(gens):
                    gens = list(gens)
                    while gens:
                        for g_ in list(gens):
                            try:
                                next(g_)
                            except StopIteration:
                                gens.remove(g_)

                cc_ = [0]
                for c in range(KF):
                    cc_[0] = c
                    wb_ = wuv[c % 2]
                    S.dma("pool", wb_[:, :, 0:128], w_up[l, :, c * 128:(c + 1) * 128].rearrange("(k p) c -> p k c", p=128), writes=[wb_], sembuf=wb_)
                    S.dma("pool", wb_[:, :, 128:256], w_up[l, :, FF + c * 128:FF + (c + 1) * 128].rearrange("(k p) c -> p k c", p=128), writes=[wb_], sembuf=wb_)
                    for (i0, i1) in blocks(0, NF, 512):
                        ps = nps()
                        proj(ps[:, 0:i1 - i0], ps, wb_, lambda k, wb_=wb_: wb_[:, k, 0:128], h2T, lambda k, i0=i0, i1=i1: h2T[:, k, i0:i1])
                        S.op("dve", lambda E, ps=ps, i0=i0, i1=i1: E.tensor_tensor(out=uc[0][:, 1 + i0:1 + i1], in0=ps[:, 0:i1 - i0], in1=valid[:, LO + i0:LO + i1], op=ALU.mult),
                             reads=[ps, valid], writes=[uc[0]])
                    for mi in range(2):
                        S.op("dve", lambda E, mi=mi: E.tensor_tensor(out=uc[1 + mi][:, 1:1 + NF].rearrange("p (c t) -> p c t", t=64), in0=uc[0][:, 1:1 + NF].rearrange("p (c t) -> p c t", t=64),
                                                                 in1=mcb[:, mi:mi + 1, :].to_broadcast([128, NF // 64, 64]), op=ALU.mult), reads=[uc[0], mcb], writes=[uc[1 + mi]])
                    taps = []
                    for ky in range(3):
                        for kx in range(3):
                            offs = (ky - 1) * 64 + (kx - 1)
                            si = {0: 1, 1: 0, 2: 2}[kx]
                            taps.append((si, 1 + 64 + offs, ky * 3 + kx))
                    drive([conv_taps([(si, st0 + hb * FH, tap) for (si, st0, tap) in taps], FH, c, fa1s[hb], fa2s[hb]) for hb in range(2)])
                    for hb in range(2):
                        fa2 = fa2s[hb]
                        for (j0, j1) in blocks(hb * FH, (hb + 1) * FH, 512):
                            ps = nps()
                            proj(ps[:, 0:j1 - j0], ps, wb_, lambda k, wb_=wb_: wb_[:, k, 128:256], h2T, lambda k, j0=j0, j1=j1: h2T[:, k, 64 + j0:64 + j1])
                            S.op("dve", lambda E, ps=ps, j0=j0, j1=j1, c=c, hb=hb: E.tensor_tensor(out=zT[:, c, j0:j1], in0=ps[:, 0:j1 - j0], in1=fa2[:, j0 - hb * FH:j1 - hb * FH], op=ALU.mult),
                                 reads=[ps, fa2], writes=[zT])
                    if ctx_stream:
                        ps = nps()
                        proj(ps[:, 0:CTX], ps, wb_, lambda k, wb_=wb_: wb_[:, k, 0:128], h2Tc, lambda k: h2Tc[:, k, 0:CTX])
                        S.op("pool", lambda E: E.memset(uc[0][:, 0:CTX + 2], 0.0), writes=[uc[0]])
                        S.op("act", lambda E, ps=ps: E.activation(out=uc[0][:, 1:1 + CTX], in_=ps[:, 0:CTX], func=AF.Copy), reads=[ps], writes=[uc[0]])
                        drive([conv_taps([(0, 0, 3), (0, 1, 4), (0, 2, 5)], CTX, c, fa1s[0], fa2s[0])])
                        fa2 = fa2s[0]
                        ps = nps()
                        proj(ps[:, 0:CTX], ps, wb_, lambda k, wb_=wb_: wb_[:, k, 128:256], h2Tc, lambda k: h2Tc[:, k, 0:CTX])
                        S.op("dve", lambda E, ps=ps, c=c: E.tensor_tensor(out=zTc[:, c, 0:CTX], in0=ps[:, 0:CTX], in1=fa2[:, 0:CTX], op=ALU.mult), reads=[ps, fa2], writes=[zTc])
                        S.op("pool", lambda E: E.memset(uc[0][:, 0:1], 0.0), writes=[uc[0]])
                    S.emit()
                S.barrier()
                S.emit()
            with contextlib.ExitStack() as ph:
                wd = [S.sbuf("wd%d" % i, [128, KF, 128], BF16, ph) for i in range(2)]
                xr = [S.sbuf("xr%d" % i, [128, 512], F32, ph) for i in range(2)]
                xw = [S.sbuf("xw%d" % i, [128, 512], F32, ph) for i in range(2)]
                cnt = 0
                for e in range(KD):
                    wd_ = wd[e % 2]
                    S.dma("pool", wd_[:, :, :], w_down[l, :, e * 128:(e + 1) * 128].rearrange("(c p) n -> p c n", p=128), writes=[wd_], sembuf=wd_)
                    jobs = [(zT, j0, j1, xmid, DR["xmid"], T0, x_dst, 0) for (j0, j1) in blocks(0, NT, 512)]
                    if ctx_stream:
                        jobs.append((zTc, 0, CTX, cxmid, DR["cxmid"], 0, cx_out, 1))
                    for (zb, j0, j1, xsrc, xsb, xoff, dstap, s) in jobs:
                        n = j1 - j0
                        ps = nps()
                        for c in range(KF):
                            S.op("pe", lambda E, c=c, zb=zb, j0=j0, j1=j1, ps=ps, wd_=wd_, n=n: E.matmul(ps[:, 0:n], lhsT=wd_[:, c, :], rhs=zb[:, c, j0:j1], start=(c == 0), stop=(c == KF - 1)),
                                 reads=[wd_, zb], writes=[ps], acc=True)
                        xr_, xw_ = xr[cnt % 2], xw[cnt % 2]
                        cnt += 1
                        S.dma("sp", xr_[:, 0:n], xsrc[e, :, xoff + j0:xoff + j1], reads=[xsb], writes=[xr_], sembuf=xr_)
                        S.op("dve", lambda E, e=e, s=s, ps=ps, xr_=xr_, xw_=xw_, n=n: E.scalar_tensor_tensor(out=xw_[:, 0:n], in0=ps[:, 0:n], scalar=G2(e, s), in1=xr_[:, 0:n], op0=ALU.mult, op1=ALU.add),
                             reads=[ps, modv, xr_], writes=[xw_])
                        if s == 0 and is_last:
                            S.dma("sp", xnew[e, :, j0:j1], xw_[:, 0:n], reads=[xw_], writes=[DR["xnew"]], sembuf=xw_)
                        else:
                            S.dma("sp", dstap[e, :, j0:j1], xw_[:, 0:n], reads=[xw_], writes=[(x_dst_buf if s == 0 else DR["CX1"])], sembuf=xw_)
                S.barrier()
                S.emit()
            ffn.close()
            if is_last:
                with contextlib.ExitStack() as ph:
                    nt = NormTmp(ph)
                    fo = [S.sbuf("fo%d" % i, [128, KD, 512], F32, ph) for i in range(2)]
                    for bi, (j0, j1) in enumerate(blocks(0, NT, 512)):
                        fo_ = fo[bi % 2]
                        norm_block(nt, xnew[:, :, j0:j1].rearrange("k p t -> p k t"), DR["xnew"], j1 - j0,
                                   lambda k: pvec[:, O_FNW + k:O_FNW + k + 1], lambda k: None, fo_, lambda k, fo_=fo_, j0=j0, j1=j1: fo_[:, k, 0:j1 - j0])
                        S.dma("sp", x_dst[:, :, j0:j1].rearrange("k p t -> p k t"), fo_[:, :, 0:j1 - j0], reads=[fo_], writes=[x_dst_buf], sembuf=fo_)
                    S.barrier()
                    S.emit()
            else:
                S.barrier()
                S.emit()

        DRE = DR["ext"]
        with contextlib.ExitStack() as ph:
            zt = S.sbuf("zt", [128, NST], F32, ph)
            S.op("pool", lambda E: E.memset(zt[:, :], 0.0), writes=[zt])
            for q in range(NSEG):
                S.dma("sp", ST[q], zt[:, :], reads=[zt], writes=[DR["ST"]], sembuf=zt)
            S.barrier()
            S.emit()
        for l in range(DEPTH):
            layer_setup(l)
            for i in range(NSEG):
                if l == 0:
                    if i == 0:
                        continue
                    run_pass("A", l, i, xT4[i], cxT_in, DRE, DRE, None, None, False, False, valid4_d[i], fold4_d[i])
                else:
                    run_pass("A", l, i, XW[i], CX1, DR["XW"], DR["CX1"], None, None, False, False, valid4_d[i], fold4_d[i])
            if l == 0:
                for i in range(NSEG):
                    run_pass("B", l, i, xT4[i], cxT_in, DRE, DRE, X1[i], DR["X1"], i == 0, False, valid4_d[i], fold4_d[i])
                for i in range(NSEG):
                    S.dma("sp", XW[i, :, :, HX:HX + NT], X1[i], reads=[DR["X1"]], writes=[DR["XW"]], sembuf=DR["XW"])
                    S.dma("sp", XW[i, :, :, 0:HX], X1[(i - 1) % NSEG, :, :, NT - HX:NT], reads=[DR["X1"]], writes=[DR["XW"]], sembuf=DR["XW"])
                    S.dma("sp", XW[i, :, :, HX + NT:W], X1[(i + 1) % NSEG, :, :, 0:HX], reads=[DR["X1"]], writes=[DR["XW"]], sembuf=DR["XW"])
                S.barrier()
                S.emit()
            else:
                with contextlib.ExitStack() as ph:
                    oh = S.sbuf("oh", [128, NSEG], F32, ph)
                    S.dma("sp", oh[:, :], onehot_d, writes=[oh], sembuf=oh)
                    xs = [S.sbuf("sel%d" % q, [128, KD, 512], F32, ph) for q in range(NSEG)]
                    acc = [S.sbuf("selacc%d" % q, [128, KD, 512], F32, ph) for q in range(2)]
                    for bi, (b0, b1) in enumerate(blocks(0, W, 512)):
                        n = b1 - b0
                        a_ = acc[bi % 2]
                        for q in range(NSEG):
                            S.dma("sp", xs[q][:, :, 0:n], XW[q, :, :, b0:b1].rearrange("k p t -> p k t"), reads=[DR["XW"]], writes=[xs[q]], sembuf=xs[q])
                        S.op("dve", lambda E, a_=a_, n=n: E.tensor_scalar(out=a_[:, :, 0:n], in0=xs[0][:, :, 0:n], scalar1=oh[:, 0:1], scalar2=None, op0=ALU.mult), reads=[xs[0], oh], writes=[a_])
                        for q in range(1, NSEG):
                            S.op("dve", lambda E, a_=a_, n=n, q=q: E.scalar_tensor_tensor(out=a_[:, :, 0:n], in0=xs[q][:, :, 0:n], scalar=oh[:, q:q + 1], in1=a_[:, :, 0:n], op0=ALU.mult, op1=ALU.add),
                                 reads=[xs[q], oh, a_], writes=[a_])
                        S.dma("sp", XWO[:, :, b0:b1].rearrange("k p t -> p k t"), a_[:, :, 0:n], reads=[a_], writes=[DR["XWO"]], sembuf=a_)
                    S.barrier()
                    S.emit()
                run_pass("B", l, 0, XWO, CX1, DR["XWO"], DR["CX1"], x_out, DR["x_out"], False, True, validown_d, foldown_d)
        S.barrier()
        S.emit(final=True)
    return nc


def _chan(v):
    return np.ascontiguousarray(v.reshape(-1, 128).T)


def pack_pvec(inp):
    pv = np.zeros((128, NPV), np.float32)
    for l in range(DEPTH):
        b = l * PL
        pv[:, b + O_N1W:b + O_N1W + 8] = _chan(inp["norm1_w"][l])
        pv[:, b + O_BMOD:b + O_BMOD + 48] = _chan(inp["b_mod"][l])
        pv[:, b + O_GNW] = inp["hg_gnorm_w"][l]
        for tap in range(31):
            pv[:, b + O_CVW + tap * 8: b + O_CVW + tap * 8 + 8] = _chan(inp["cv_dw_w"][l, tap])
        pv[:, b + O_CVB:b + O_CVB + 8] = _chan(inp["cv_dw_b"][l])
        pv[:, b + O_LNW:b + O_LNW + 8] = _chan(inp["cv_ln_w"][l])
        pv[:, b + O_LNB:b + O_LNB + 8] = _chan(inp["cv_ln_b"][l])
        pv[:, b + O_N2W:b + O_N2W + 8] = _chan(inp["norm2_w"][l])
        fw = inp["ffn_dw_w"][l].reshape(9, FF)
        for tap in range(9):
            pv[:, b + O_FW + tap * KF: b + O_FW + (tap + 1) * KF] = _chan(fw[tap])
        pv[:, b + O_FB:b + O_FB + KF] = _chan(inp["ffn_dw_b"][l])
    for d in range(2):
        for l in range(DEPTH):
            pv[:, O_LBZ + d * 16 + l * 8: O_LBZ + d * 16 + l * 8 + 8] = _chan(inp["hg_lb_logits"][d, l])
    pv[:, O_FNW:O_FNW + 8] = _chan(inp["final_norm_w"])
    return pv


def make_consts():
    cst = np.zeros((128, 6, 128), np.float32)
    cst[:, 0, :] = np.eye(128, dtype=np.float32)
    cst[:, 1, :] = 1.0 / D
    cst[:, 2, :] = 1.0 / 128
    s = np.arange(64)[:, None]
    t = np.arange(64)[None, :]
    mF = (s <= t).astype(np.float32)
    mB = (s >= t).astype(np.float32)
    cst[0:32, 3, 0:64] = mF[0:32, :]
    cst[0:32, 3, 64:96] = mF[32:64, 32:64]
    cst[0:32, 4, 0:64] = mB[32:64, :]
    cst[0:32, 4, 64:96] = mB[0:32, 0:32]
    rst = np.ones((128, 512), np.float32)
    rst[:, ::64] = 0.0
    mcol = np.ones((128, 2, W), np.float32)
    pos = np.arange(W)
    mcol[:, 0, pos % 64 == 63] = 0.0
    mcol[:, 1, pos % 64 == 0] = 0.0
    return cst, rst, mcol


def window_T(xfull_b, seg):
    t0 = seg * NT - HX
    w = np.zeros((W, D), np.float32)
    a, b = max(t0, 0), min(t0 + W, SEQ)
    w[a - t0:b - t0] = xfull_b[a:b]
    return np.ascontiguousarray(w.T.reshape(KD, 128, W))


def core_static(inp, c):
    b, seg = c // NSEG, c % NSEG
    t0 = seg * NT - HX
    pos = np.arange(W) + t0
    valid = np.broadcast_to(((pos >= 0) & (pos < SEQ)).astype(np.float32), (128, W)).copy()
    fold = np.zeros((128, 2, NSEG), np.float32)
    for k in range(NSEG):
        fold[:, 0, k] = 1.0 if k < seg else 0.0
        fold[:, 1, k] = 1.0 if k > seg else 0.0
    cvec = np.stack([_chan(inp["c"][b]), _chan(inp["c_ctx"])], axis=-1)
    return dict(valid=valid, foldm=fold, cvec=np.ascontiguousarray(cvec))


_NC_CACHE = {}


def seg_static(seg):
    t0 = seg * NT - HX
    pos = np.arange(W) + t0
    valid = np.broadcast_to(((pos >= 0) & (pos < SEQ)).astype(np.float32), (128, W)).copy()
    fold = np.zeros((128, 2, NSEG), np.float32)
    for k in range(NSEG):
        fold[:, 0, k] = 1.0 if k < seg else 0.0
        fold[:, 1, k] = 1.0 if k > seg else 0.0
    return valid, fold


def kernel(**inp):
    inp = {k: np.asarray(v) for k, v in inp.items()}
    pv = pack_pvec(inp)
    cst, rst, mcol = make_consts()
    x = np.ascontiguousarray(inp["x"], dtype=np.float32)
    ctx = np.ascontiguousarray(inp["ctx"], dtype=np.float32)
    segs = [seg_static(s) for s in range(NSEG)]
    valid4 = np.ascontiguousarray(np.stack([v for v, _ in segs], axis=0))
    fold4 = np.ascontiguousarray(np.stack([f for _, f in segs], axis=0))
    xT4 = [np.ascontiguousarray(np.stack([window_T(x[b], s) for s in range(NSEG)], axis=0)) for b in range(BATCH)]
    cxT = [np.ascontiguousarray(ctx[b].T.reshape(KD, 128, CTX)) for b in range(BATCH)]
    maps = []
    for c in range(NCORE):
        b, seg = c // NSEG, c % NSEG
        oh = np.zeros((128, NSEG), np.float32)
        oh[:, seg] = 1.0
        maps.append(dict(
            xT4=xT4[b], cxT=cxT[b], cvec=np.ascontiguousarray(np.stack([_chan(inp["c"][b]), _chan(inp["c_ctx"])], axis=-1)),
            valid4=valid4, fold4=fold4, validown=segs[seg][0], foldown=segs[seg][1], onehot=oh,
            pvec=pv, cst=cst, rst=rst, mcol=mcol, w_mod=inp["w_mod"], w_in=inp["w_in"],
            w_hg_out=inp["w_hg_out"], w_cv_out=inp["w_cv_out"], w_out=inp["w_out"], w_up=inp["w_up"], w_down=inp["w_down"]))
    if "nc" not in _NC_CACHE:
        _NC_CACHE["nc"] = build()
    res = run_bass_kernel_spmd(_NC_CACHE["nc"], maps, core_ids=list(range(NCORE)))
    out = np.empty((BATCH, SEQ, D), np.float32)
    for c in range(NCORE):
        b, seg = c // NSEG, c % NSEG
        out[b, seg * NT:(seg + 1) * NT, :] = res.results[c]["x_out"].reshape(D, NT).T
    return out
```

```python
import contextlib
import types
import numpy as np
import concourse.bass as bass
import concourse.mybir as mybir
from concourse.bass_utils import run_bass_kernel_spmd

F32 = mybir.dt.float32
BF16 = mybir.dt.bfloat16
AF = mybir.ActivationFunctionType
ALU = mybir.AluOpType

NSEM_ENG = 8
DEBUG = False
NLAYERS_RUN = 2

D = 1024
KD = 8
BATCH = 2
SEQ = 8192
DEPTH = 2
CTX = 256
PTOT = 9216
FF = 2816
KF = 22
NCORE = 8
NSEG = 4
NT = SEQ // NSEG
HX = 128
W = NT + 2 * HX
CS, CE = 1, 35
LO, HI = CS * 64, CE * 64
T0, T1 = HX, HX + NT
EPS = 1e-6
G_MIN = 1e-6
NST = 2 * 8 * 128 + 16

PL = 557
O_N1W, O_BMOD, O_GNW, O_CVW, O_CVB, O_LNW, O_LNB, O_N2W, O_FW, O_FB = 0, 8, 56, 57, 305, 313, 321, 329, 337, 535
O_LBZ = 2 * PL
O_FNW = O_LBZ + 32
NPV = O_FNW + 8


def freeze(fn, depth=0):
    if not isinstance(fn, types.FunctionType) or fn.__closure__ is None:
        return fn
    cells = []
    for c in fn.__closure__:
        try:
            v = c.cell_contents
        except ValueError:
            cells.append(c)
            continue
        if isinstance(v, types.FunctionType) and depth < 4:
            v = freeze(v, depth + 1)
        cells.append(types.CellType(v))
    g = types.FunctionType(fn.__code__, fn.__globals__, fn.__name__, fn.__defaults__, tuple(cells))
    g.__kwdefaults__ = fn.__kwdefaults__
    return g


class Buf:
    def __init__(self, name, t=None):
        self.name = name
        self.t = t
        self.last_w = None
        self.reads = []
        self.dsem = None

    def __getitem__(self, k):
        return self.t[k]


class Sched:
    ENGS = ("pe", "act", "dve", "pool", "sp")

    def __init__(self, nc, stack):
        self.nc = nc
        self.stack = stack
        self.streams = {e: [] for e in self.ENGS}
        self.count = {e: 0 for e in self.ENGS}
        self.sems = {}
        self.semval = {}
        self.waited = {e: {} for e in self.ENGS}
        self.final_events = []
        self.pending = {}
        self.last_ev = {}
        self.uid = 0
        self.free_dsems = []

    def sem(self, key):
        if key not in self.sems:
            self.sems[key] = self.stack.enter_context(self.nc.semaphore("s_%s_%s" % (key[0], key[1])))
            self.semval[key] = 0
        return self.sems[key]

    def sbuf(self, name, shape, dt, stack=None):
        self.uid += 1
        name = "%s_%d" % (name, self.uid)
        t = (stack or self.stack).enter_context(self.nc.sbuf_tensor(name, list(shape), dt))
        b = Buf(name, t)
        if stack is not None:
            stack.callback(self._release, b)
        return b

    def _release(self, b):
        if b.dsem is not None:
            self.free_dsems.append(b.dsem)
            b.dsem = None

    def psum(self, name, shape, dt):
        t = self.stack.enter_context(self.nc.psum_tensor(name, list(shape), dt))
        return Buf(name, t)

    def _deps(self, eng, reads, writes, acc):
        evs = []
        for b in reads:
            if b.last_w is not None:
                evs.append(b.last_w)
        for b in writes:
            if b.last_w is not None and not (acc and b.last_w[0][0] == "pe" and eng == "pe"):
                evs.append(b.last_w)
            evs.extend(b.reads)
        best = {}
        for (k, v) in evs:
            best[k] = max(best.get(k, 0), v)
        p = self.pending.pop(eng, None)
        if p:
            for (k, v) in p:
                best[k] = max(best.get(k, 0), v)
        waits = []
        w = self.waited[eng]
        for k, v in best.items():
            if w.get(k, 0) < v:
                waits.append((k, v))
                w[k] = v
        return waits

    def _commit(self, ev, reads, writes):
        for b in reads:
            b.reads.append(ev)
        for b in writes:
            b.last_w = ev
            b.reads = []
        self.last_ev[ev[0]] = ev[1]

    def op(self, eng, fn, reads=(), writes=(), acc=False):
        reads = [b for b in reads if b is not None]
        writes = [b for b in writes if b is not None]
        waits = self._deps(eng, reads, writes, acc)
        idx = self.count[eng]
        self.count[eng] += 1
        key = (eng, idx % NSEM_ENG)
        self.sem(key)
        ev = (key, idx // NSEM_ENG + 1)
        self.streams[eng].append((waits, freeze(fn), key, 1))
        self._commit(ev, reads, writes)
        return ev

    def dma(self, eng, out_ap, in_ap, reads=(), writes=(), sembuf=None):
        reads = [b for b in reads if b is not None]
        writes = [b for b in writes if b is not None]
        waits = self._deps(eng, reads, writes, False)
        if sembuf.dsem is None:
            if self.free_dsems:
                sembuf.dsem = self.free_dsems.pop()
            else:
                sembuf.dsem = ("dma", sembuf.name)
                self.sem(sembuf.dsem)
        key = sembuf.dsem
        self.semval[key] += 16
        ev = (key, self.semval[key])
        self.streams[eng].append((waits, lambda E: E.dma_start(out=out_ap, in_=in_ap), key, 16))
        self._commit(ev, reads, writes)
        return ev

    def barrier(self):
        evs = list(self.last_ev.items())
        for e in self.ENGS:
            self.pending[e] = list(evs)

    def emit(self, final=False):
        nc = self.nc
        sems = self.sems
        streams = self.streams
        final_events = list(self.last_ev.items()) if final else []
        with nc.Block() as block:
            def run(E, eng):
                for (waits, fn, key, inc) in streams[eng]:
                    for (k, v) in waits:
                        E.wait_ge(sems[k], v)
                    ins = fn(E)
                    ins.then_inc(sems[key], inc)
                if eng == "sp":
                    for (k, v) in final_events:
                        E.wait_ge(sems[k], v)
                streams[eng] = []

            @block.tensor
            def _(E):
                run(E, "pe")

            @block.scalar
            def _(E):
                run(E, "act")

            @block.vector
            def _(E):
                run(E, "dve")

            @block.gpsimd
            def _(E):
                run(E, "pool")

            @block.sync
            def _(E):
                run(E, "sp")


def blocks(lo, hi, n):
    out = []
    t = lo
    while t < hi:
        out.append((t, min(hi, t + n)))
        t += n
    return out


def build():
    nc = bass.Bass("TRN2", target_bir_lowering=False)
    dt_in = lambda name, shape: nc.dram_tensor(name, list(shape), F32, kind="ExternalInput").ap()
    xT4 = dt_in("xT4", [NSEG, KD, 128, W])
    cxT_in = dt_in("cxT", [KD, 128, CTX])
    cvec = dt_in("cvec", [128, KD, 2])
    valid4_d = dt_in("valid4", [NSEG, 128, W])
    pvec_d = dt_in("pvec", [128, NPV])
    cst_d = dt_in("cst", [128, 6, 128])
    rst_d = dt_in("rst", [128, 512])
    mcol_d = dt_in("mcol", [128, 2, W])
    fold4_d = dt_in("fold4", [NSEG, 128, 2, NSEG])
    validown_d = dt_in("validown", [128, W])
    foldown_d = dt_in("foldown", [128, 2, NSEG])
    onehot_d = dt_in("onehot", [128, NSEG])
    w_mod = dt_in("w_mod", [DEPTH, D, 6 * D])
    w_in = dt_in("w_in", [DEPTH, D, PTOT])
    w_hg_out = dt_in("w_hg_out", [DEPTH, D, D])
    w_cv_out = dt_in("w_cv_out", [DEPTH, D, D])
    w_out = dt_in("w_out", [DEPTH, D, D])
    w_up = dt_in("w_up", [DEPTH, D, 2 * FF])
    w_down = dt_in("w_down", [DEPTH, FF, D])
    x_out = nc.dram_tensor("x_out", [KD, 128, NT], F32, kind="ExternalOutput").ap()
    itn = lambda name, shape, dt=F32: nc.dram_tensor(name, list(shape), dt, kind="Internal").ap()
    ST = itn("ST", [NSEG, 128, NST])
    X1 = itn("X1", [NSEG, KD, 128, NT])
    XW = itn("XW", [NSEG, KD, 128, W])
    XWO = itn("XWO", [KD, 128, W])
    CX1 = itn("CX1", [KD, 128, CTX])
    SCB = itn("SCB", [8, 128, 128])
    xmid = itn("xmid", [KD, 128, W])
    xnew = itn("xnew", [KD, 128, NT])
    cxmid = itn("cxmid", [KD, 128, CTX])
    oT_d = itn("oT_d", [8, 128, W], BF16)
    oTc_d = itn("oTc_d", [8, 128, CTX], BF16)
    DR = {n: Buf(n) for n in ("xmid", "xnew", "cxmid", "oT_d", "oTc_d", "x_out", "cx_out", "ST", "X1", "XW", "XWO", "CX1", "SCB", "ext")}

    with contextlib.ExitStack() as st:
        S = Sched(nc, st)
        PS = [S.psum("ps%d" % i, [128, 512], F32) for i in range(7)]
        PSB = S.psum("psb", [128, 1024], BF16)
        psi = [0]

        def nps():
            psi[0] = (psi[0] + 1) % 7
            return PS[psi[0]]

        pvec = S.sbuf("pvec", [128, NPV], F32)
        cst = S.sbuf("cst", [128, 6, 128], F32)
        identb = S.sbuf("identb", [128, 128], BF16)
        maskFb = S.sbuf("maskFb", [32, 96], BF16)
        maskBb = S.sbuf("maskBb", [32, 96], BF16)
        valid = S.sbuf("valid", [128, W], BF16)
        ones_c = S.sbuf("ones_c", [128, CTX], BF16)
        rst = S.sbuf("rst", [128, 512], F32)
        foldm = S.sbuf("foldm", [128, 2, NSEG], F32)
        modv = S.sbuf("modv", [128, 48, 2], F32)
        A1 = S.sbuf("A1", [128, KD, 2], F32)
        A2 = S.sbuf("A2", [128, KD, 2], F32)
        lbv = S.sbuf("lbv", [128, 2, 3, KD], F32)
        rcp = S.sbuf("rcp", [128, KD], F32)
        epsc = S.sbuf("epsc", [128, 1], F32)
        carry = S.sbuf("carry", [128, 8, 128], F32)
        S.op("pool", lambda E: E.memset(epsc[:, :], EPS), writes=[epsc])
        S.dma("sp", pvec[:, :], pvec_d, writes=[pvec], sembuf=pvec)
        S.dma("sp", cst[:, :, :], cst_d, writes=[cst], sembuf=cst)
        S.dma("sp", rst[:, :], rst_d, writes=[rst], sembuf=rst)
        S.dma("pool", identb[:, :], cst_d[:, 0, :], writes=[identb], sembuf=identb)
        S.dma("pool", maskFb[:, :], cst_d[0:32, 3, 0:96], writes=[maskFb], sembuf=maskFb)
        S.dma("pool", maskBb[:, :], cst_d[0:32, 4, 0:96], writes=[maskBb], sembuf=maskBb)
        S.op("pool", lambda E: E.memset(ones_c[:, :], 1.0), writes=[ones_c])
        onesD = cst[:, 1, :]
        ones128 = cst[:, 2, :]

        def layer_setup(l):
            pb = l * PL
            for d in range(2):
                lb_ap = lbv[:, d, 0, :]
                if l == 0:
                    S.op("dve", lambda E, lb_ap=lb_ap: E.memset(lb_ap, 0.0), writes=[lbv])
                else:
                    z0 = pvec[:, O_LBZ + d * 16: O_LBZ + d * 16 + 8]
                    z1 = pvec[:, O_LBZ + d * 16 + 8: O_LBZ + d * 16 + 16]
                    S.op("dve", lambda E, lb_ap=lb_ap, z0=z0, z1=z1: E.tensor_tensor(out=lb_ap, in0=z1, in1=z0, op=ALU.subtract), reads=[pvec], writes=[lbv])
                    S.op("act", lambda E, lb_ap=lb_ap: E.activation(out=lb_ap, in_=lb_ap, func=AF.Sigmoid), reads=[lbv], writes=[lbv])
                a_ap = lbv[:, d, 1, :]
                sm_ap = lbv[:, d, 2, :]
                S.op("dve", lambda E, lb_ap=lb_ap, a_ap=a_ap: E.tensor_scalar(out=a_ap, in0=lb_ap, scalar1=-1.0, scalar2=1.0, op0=ALU.mult, op1=ALU.add), reads=[lbv], writes=[lbv])
                S.op("dve", lambda E, lb_ap=lb_ap, sm_ap=sm_ap: E.tensor_scalar(out=sm_ap, in0=lb_ap, scalar1=-1.0, scalar2=G_MIN, op0=ALU.mult, op1=ALU.add), reads=[lbv], writes=[lbv])
                S.op("dve", lambda E, a_ap=a_ap: E.reciprocal(out=rcp[:, :], in_=a_ap), reads=[lbv], writes=[rcp])
                S.op("dve", lambda E, sm_ap=sm_ap: E.tensor_tensor(out=sm_ap, in0=sm_ap, in1=rcp[:, :], op=ALU.mult), reads=[lbv, rcp], writes=[lbv])

            with contextlib.ExitStack() as ph:
                csb = S.sbuf("csb", [128, KD, 2], F32, ph)
                scb = S.sbuf("scb", [128, KD, 2], F32, ph)
                wm = [S.sbuf("wm%d" % i, [128, KD, 512], F32, ph) for i in range(2)]
                S.dma("sp", csb[:, :, :], cvec, writes=[csb], sembuf=csb)
                S.op("act", lambda E: E.activation(out=scb[:, :, :], in_=csb[:, :, :], func=AF.Silu), reads=[csb], writes=[scb])
                pm = PS[0]
                for piece in range(12):
                    buf = wm[piece % 2]
                    S.dma("sp", buf[:, :, :], w_mod[l, :, piece * 512:(piece + 1) * 512].rearrange("(k p) c -> p k c", p=128), writes=[buf], sembuf=buf)
                    for mm in range(4):
                        m = piece * 4 + mm
                        for k in range(KD):
                            S.op("pe", lambda E, buf=buf, k=k, mm=mm, m=m: E.matmul(pm[:, m * 2:m * 2 + 2], lhsT=buf[:, k, mm * 128:(mm + 1) * 128], rhs=scb[:, k, :], start=(k == 0), stop=(k == KD - 1)),
                                 reads=[buf, scb], writes=[pm], acc=True)
                S.op("dve", lambda E: E.tensor_tensor(out=modv[:, :, :], in0=pm[:, 0:96].rearrange("p (m s) -> p m s", s=2),
                                                      in1=pvec[:, pb + O_BMOD: pb + O_BMOD + 48].unsqueeze(2).to_broadcast([128, 48, 2]), op=ALU.add),
                     reads=[pm, pvec], writes=[modv])
                for (Ax, off, onw) in ((A1, 8, O_N1W), (A2, 32, O_N2W)):
                    S.op("dve", lambda E, Ax=Ax, off=off, onw=onw: E.scalar_tensor_tensor(out=Ax[:, :, :], in0=modv[:, off:off + 8, :], scalar=1.0,
                                                                                          in1=pvec[:, pb + onw: pb + onw + 8].unsqueeze(2).to_broadcast([128, 8, 2]),
                                                                                          op0=ALU.add, op1=ALU.mult), reads=[modv, pvec], writes=[Ax])
                S.barrier()
                S.emit()
        SH1 = lambda k, s: modv[:, 0 + k, s:s + 1]
        G1 = lambda k, s: modv[:, 16 + k, s:s + 1]
        SH2 = lambda k, s: modv[:, 24 + k, s:s + 1]
        G2 = lambda k, s: modv[:, 40 + k, s:s + 1]

        def wload(dst, src_ap):
            S.dma("pool", dst[:, :, :], src_ap.rearrange("(k p) c -> p k c", p=128), writes=[dst], sembuf=dst)

        def proj(ps_ap, psbuf, wbuf, wap_fn, rbuf, rap_fn, nk=KD):
            for k in range(nk):
                S.op("pe", lambda E, k=k: E.matmul(ps_ap, lhsT=wap_fn(k), rhs=rap_fn(k), start=(k == 0), stop=(k == nk - 1)),
                     reads=[wbuf, rbuf], writes=[psbuf], acc=True)

        class NormTmp:
            def __init__(self, ph, n=512, nbuf=1):
                self.xbs = [S.sbuf("n_xb", [128, KD, n], F32, ph) for _ in range(nbuf)]
                self.sqs = [S.sbuf("n_sq", [128, KD, n], F32, ph) for _ in range(nbuf)]
                self.cnt = 0
                self.xb = self.xbs[0]
                self.sq = self.sqs[0]
                self.rstd = S.sbuf("n_rstd", [128, n], F32, ph)
                self.tmp = [S.sbuf("n_tmp%d" % i, [128, n], F32, ph) for i in range(2)]

            def rotate(self):
                self.cnt += 1
                self.xb = self.xbs[self.cnt % len(self.xbs)]
                self.sq = self.sqs[self.cnt % len(self.sqs)]

        def norm_block(nt, src_ap, srcbuf, n, Afn, SHfn, outbuf, out_fn):
            nt.rotate()
            xb, sq = nt.xb, nt.sq
            S.dma("sp", xb[:, :, 0:n], src_ap, reads=[srcbuf], writes=[xb], sembuf=xb)
            S.op("act", lambda E: E.activation(out=sq[:, :, 0:n], in_=xb[:, :, 0:n], func=AF.Square), reads=[xb], writes=[sq])
            ps = nps()
            for k in range(KD):
                S.op("pe", lambda E, k=k: E.matmul(ps[:, 0:n], lhsT=onesD, rhs=sq[:, k, 0:n], start=(k == 0), stop=(k == KD - 1)),
                     reads=[cst, sq], writes=[ps], acc=True)
            S.op("act", lambda E: E.activation(out=nt.rstd[:, 0:n], in_=ps[:, 0:n], func=AF.Ln, bias=epsc[:, 0:1], scale=1.0), reads=[ps, epsc], writes=[nt.rstd])
            S.op("act", lambda E: E.activation(out=nt.rstd[:, 0:n], in_=nt.rstd[:, 0:n], func=AF.Exp, scale=-0.5), reads=[nt.rstd], writes=[nt.rstd])
            for k in range(KD):
                tmp = nt.tmp[k % 2]
                S.op("dve", lambda E, k=k, tmp=tmp: E.scalar_tensor_tensor(out=tmp[:, 0:n], in0=xb[:, k, 0:n], scalar=Afn(k), in1=nt.rstd[:, 0:n], op0=ALU.mult, op1=ALU.mult),
                     reads=[xb, nt.rstd, A1, A2, pvec], writes=[tmp])
                sh = SHfn(k)
                if sh is None:
                    S.op("act", lambda E, k=k, tmp=tmp: E.activation(out=out_fn(k), in_=tmp[:, 0:n], func=AF.Copy), reads=[tmp], writes=[outbuf])
                else:
                    S.op("act", lambda E, k=k, tmp=tmp, sh=sh: E.activation(out=out_fn(k), in_=tmp[:, 0:n], func=AF.Identity, bias=sh, scale=1.0),
                         reads=[tmp, modv], writes=[outbuf])


        def run_pass(kind, l, i, xT, cxT, DRX, DRC, x_dst, x_dst_buf, ctx_stream, is_last, valid_src, fold_src):
            pb = l * PL
            st_in = ST
            ctx_needed = not (l == 0 and i > 0)
            S.dma("sp", foldm[:, :, :], fold_src, writes=[foldm], sembuf=foldm)
            S.dma("pool", valid[:, :], valid_src, writes=[valid], sembuf=valid)
            cx_out = CX1
            mixer = contextlib.ExitStack()
            hT = S.sbuf("hT", [128, KD, W], BF16, mixer)
            hTc = S.sbuf("hTc", [128, KD, CTX], BF16, mixer)
            with contextlib.ExitStack() as ph:
                nt = NormTmp(ph, nbuf=2)
                for (b0, b1) in blocks(0, W, 512):
                    norm_block(nt, xT[:, :, b0:b1].rearrange("k p t -> p k t"), DRX, b1 - b0,
                               lambda k: A1[:, k, 0:1], lambda k: SH1(k, 0), hT, lambda k, b0=b0, b1=b1: hT[:, k, b0:b1])
                if kind == "B" and ctx_needed:
                    norm_block(nt, cxT.rearrange("k p t -> p k t"), DRC, CTX,
                               lambda k: A1[:, k, 1:2], lambda k: SH1(k, 1), hTc, lambda k: hTc[:, k, 0:CTX])
                S.barrier()
                S.emit()

            hg = contextlib.ExitStack()
            full = (kind == "B")
            NW = 5
            wh = [S.sbuf("wh%d" % i, [128, KD, NW * 128], BF16, hg) for i in range(2)]
            nchw = W // 64
            kT = {d: S.sbuf("kT%d" % d, [64, nchw, 128], BF16, hg) for d in range(2)}
            vT = S.sbuf("vT", [64, nchw, 128], BF16, hg)
            vT2 = S.sbuf("vT2", [32, nchw, 128], BF16, hg)
            stt_ = {d: S.sbuf("st%d" % d, [128, nchw], F32, hg) for d in range(2)}
            if full:
                qt = {d: S.sbuf("qt%d" % d, [128, W], BF16, hg) for d in range(2)}
                kt = {d: S.sbuf("kt%d" % d, [128, W], BF16, hg) for d in range(2)}
                srt = {d: S.sbuf("sr%d" % d, [128, nchw], F32, hg) for d in range(2)}
                of = S.sbuf("of", [128, W], F32, hg)
                ob2 = S.sbuf("ob2", [128, W], F32, hg)
                ob = kt[0]
                stin_r = [S.sbuf("stin0", [128, NSEG, 2, 128], F32, hg)] * 2
                stA = S.sbuf("stA", [128, NSEG, 16], F32, hg)
                S.dma("sp", stA[:, :, :], st_in[:, :, 2048:2064].rearrange("s p c -> p s c"), reads=[DR["ST"]], writes=[stA], sembuf=stA)
                Sbf = {d: [S.sbuf("Sbf%d_%d" % (d, i), [128, 128], BF16, hg) for i in range(2)] for d in range(2)}
                atall = {d: S.sbuf("atall%d" % d, [32, nchw, 96], BF16, hg) for d in range(2)}
                alpha = S.sbuf("alpha", [128, 2], F32, hg)
            Sst = {d: S.sbuf("Sst%d" % d, [128, 128], F32, hg) for d in range(2)}
            stpack = S.sbuf("stpack", [128, NST], F32, hg) if not full else None
            nslots = 1 if full else 2
            tpd = {(sl, d): {n: S.sbuf("t%d%d_%s" % (sl, d, n), [128, 512], F32, hg) for n in ("s", "sn", "c", "D", "dq", "e2", "e3")} for d in range(2) for sl in range(nslots)}
            tp = tpd[(0, 0)]
            tkbd = {(sl, d): S.sbuf("t_kb%d%d" % (sl, d), [128, 512], BF16, hg) for d in range(2) for sl in range(nslots)}
            tvbs = [S.sbuf("t_vb%d" % sl, [128, 512], BF16, hg) for sl in range(nslots)]

            def drive(gens):
                gens = list(gens)
                while gens:
                    for g_ in list(gens):
                        try:
                            next(g_)
                        except StopIteration:
                            gens.remove(g_)


            def tr32(src, dst, c0, nch, dst2=None):
                for j in range(nch):
                    S.op("pe", lambda E, j=j: E.transpose(PSB[0:64, j * 128:(j + 1) * 128], src[:, j * 64:(j + 1) * 64], identb[:, :]),
                         reads=[src, identb], writes=[PSB], acc=True)
                S.op("act", lambda E: E.activation(out=dst[:, c0:c0 + nch, :], in_=PSB[0:64, 0:nch * 128].rearrange("p (c d) -> p c d", d=128), func=AF.Copy),
                     reads=[PSB], writes=[dst])
                if dst2 is not None:
                    for j in range(nch):
                        S.op("pe", lambda E, j=j: E.transpose(PSB[0:32, j * 128:(j + 1) * 128], src[:, j * 64 + 32:(j + 1) * 64], identb[:, :]),
                             reads=[src, identb], writes=[PSB], acc=True)
                    S.op("act", lambda E: E.activation(out=dst2[:, c0:c0 + nch, :], in_=PSB[0:32, 0:nch * 128].rearrange("p (c d) -> p c d", d=128), func=AF.Copy),
                         reads=[PSB], writes=[dst2])

            def hg_gates(h, wb, hsrc, hbuf, t0, n, vmask, vbuf, want_q, dirs=(0, 1), slot=0, run=True):
                nch = n // 64
                c0 = t0 // 64
                v3 = lambda ap: ap.rearrange("p (c t) -> p c t", t=64)
                tvb = tvbs[slot]

                def vunit():
                    ps = nps()
                    proj(ps[:, 0:n], ps, wb, lambda k: wb[:, k, 3 * 128:4 * 128], hbuf, lambda k: hsrc[:, k, t0:t0 + n])
                    yield
                    S.op("act", lambda E: E.activation(out=tvb[:, 0:n], in_=ps[:, 0:n], func=AF.Copy), reads=[ps], writes=[tvb])
                    yield
                    yield
                    tr32(tvb, vT, c0, nch, vT2 if want_q else None)
                if want_q:
                    psq = nps()
                    proj(psq[:, 0:n], psq, wb, lambda k: wb[:, k, 0:128], hbuf, lambda k: hsrc[:, k, t0:t0 + n])
                def unit(d):
                    lb = lbv[:, d, 0, h:h + 1]
                    a = lbv[:, d, 1, h:h + 1]
                    smin = lbv[:, d, 2, h:h + 1]
                    psf = nps()
                    proj(psf[:, 0:n], psf, wb, lambda k, d=d: wb[:, k, (1 + d) * 128:(2 + d) * 128], hbuf, lambda k: hsrc[:, k, t0:t0 + n])
                    yield
                    T = tpd[(slot, d)]
                    s_, sn, cc, DD, dq, e2, e3 = T["s"], T["sn"], T["c"], T["D"], T["dq"], T["e2"], T["e3"]
                    lg, kk, e1 = s_, sn, dq
                    tkb_ = tkbd[(slot, d)]
                    S.op("act", lambda E: E.activation(out=s_[:, 0:n], in_=psf[:, 0:n], func=AF.Sigmoid), reads=[psf], writes=[s_])
                    S.op("act", lambda E: E.activation(out=sn[:, 0:n], in_=psf[:, 0:n], func=AF.Sigmoid, scale=-1.0), reads=[psf], writes=[sn])
                    yield
                    S.op("dve", lambda E, smin=smin: E.tensor_scalar(out=s_[:, 0:n], in0=s_[:, 0:n], scalar1=smin, scalar2=None, op0=ALU.max), reads=[s_, lbv], writes=[s_])
                    S.op("dve", lambda E, a=a: E.scalar_tensor_tensor(out=kk[:, 0:n], in0=sn[:, 0:n], scalar=a, in1=vmask, op0=ALU.mult, op1=ALU.mult),
                         reads=[sn, lbv, vbuf], writes=[kk])
                    yield
                    S.op("act", lambda E, a=a, lb=lb: E.activation(out=lg[:, 0:n], in_=s_[:, 0:n], func=AF.Ln, scale=a, bias=lb), reads=[s_, lbv], writes=[lg])
                    yield
                    S.op("pool", lambda E: E.tensor_tensor(out=lg[:, 0:n], in0=lg[:, 0:n], in1=vmask, op=ALU.mult), reads=[lg, vbuf], writes=[lg])
                    yield
                    S.op("dve", lambda E: E.tensor_tensor_scan(out=cc[:, 0:n], data0=rst[:, 0:n], data1=lg[:, 0:n], initial=0.0, op0=ALU.mult, op1=ALU.add),
                         reads=[rst, lg], writes=[cc])
                    yield
                    c63 = v3(cc[:, 0:n])[:, :, 63:64]
                    S.op("act", lambda E, d=d: E.activation(out=stt_[d][:, c0:c0 + nch], in_=v3(cc[:, 0:n])[:, :, 63], func=AF.Exp), reads=[cc], writes=[stt_[d]])
                    if d == 0:
                        Dv = cc
                    else:
                        S.op("dve", lambda E: E.tensor_tensor(out=DD[:, 0:n], in0=lg[:, 0:n], in1=cc[:, 0:n], op=ALU.subtract), reads=[lg, cc], writes=[DD])
                        S.op("dve", lambda E, c63=c63: E.tensor_tensor(out=v3(DD[:, 0:n]), in0=v3(DD[:, 0:n]), in1=c63.to_broadcast([128, nch, 64]), op=ALU.add),
                             reads=[DD, cc], writes=[DD])
                        Dv = DD
                    yield
                    S.op("dve", lambda E, c63=c63, Dv=Dv: E.tensor_tensor(out=v3(e3[:, 0:n]), in0=v3(Dv[:, 0:n]), in1=c63.to_broadcast([128, nch, 64]), op=ALU.subtract),
                         reads=[Dv, cc], writes=[e3])
                    if want_q:
                        dref = v3(Dv[:, 0:n])[:, :, 31:32]
                        S.op("dve", lambda E, Dv=Dv, dref=dref: E.tensor_tensor(out=v3(dq[:, 0:n]), in0=v3(Dv[:, 0:n]), in1=dref.to_broadcast([128, nch, 64]), op=ALU.subtract),
                             reads=[Dv], writes=[dq])
                    yield
                    S.op("act", lambda E: E.activation(out=e3[:, 0:n], in_=e3[:, 0:n], func=AF.Exp, scale=-1.0), reads=[e3], writes=[e3])
                    if want_q:
                        S.op("act", lambda E, d=d, Dv=Dv: E.activation(out=srt[d][:, c0:c0 + nch], in_=v3(Dv[:, 0:n])[:, :, 31], func=AF.Exp), reads=[Dv], writes=[srt[d]])
                        S.op("act", lambda E: E.activation(out=e2[:, 0:n], in_=dq[:, 0:n], func=AF.Exp, scale=-1.0), reads=[dq], writes=[e2])
                        S.op("act", lambda E: E.activation(out=e1[:, 0:n], in_=dq[:, 0:n], func=AF.Exp), reads=[dq], writes=[e1])
                    yield
                    S.op("pool", lambda E: E.tensor_tensor(out=tkb_[:, 0:n], in0=kk[:, 0:n], in1=e3[:, 0:n], op=ALU.mult), reads=[kk, e3], writes=[tkb_])
                    if want_q:
                        S.op("dve", lambda E, d=d: E.tensor_tensor(out=qt[d][:, t0:t0 + n], in0=psq[:, 0:n], in1=e1[:, 0:n], op=ALU.mult), reads=[psq, e1], writes=[qt[d]])
                        S.op("pool", lambda E, d=d: E.tensor_tensor(out=kt[d][:, t0:t0 + n], in0=kk[:, 0:n], in1=e2[:, 0:n], op=ALU.mult), reads=[kk, e2], writes=[kt[d]])
                    yield
                    tr32(tkb_, kT[d], c0, nch)

                gens = [unit(d) for d in dirs] + [vunit()]
                if not run:
                    return gens
                drive(gens)

            def hg_scan(chunks_f, chunks_b, with_out, obase, snap=None):
                nsteps = max(len(chunks_f), len(chunks_b))
                POB = {0: [PS[0], PS[6]], 1: [PS[1], PS[3]]}
                PA = {0: PS[2], 1: PS[3]}
                PP = {0: PS[4], 1: PS[5]}
                masks = {0: maskFb, 1: maskBb}
                pending = {0: [], 1: []}

                def flush(d):
                    if not pending[d]:
                        return
                    cl = sorted(pending[d])
                    ta, tb = cl[0] * 64, (cl[-1] + 1) * 64
                    pa = ((cl[0] * 64) % 512)
                    n = tb - ta
                    pob = POB[d][(ta // 512) % 2]
                    dst_ = of if d == 0 else ob2
                    S.op("act", lambda E: E.activation(out=dst_[:, ta:tb], in_=pob[:, pa:pa + n], func=AF.Copy), reads=[pob], writes=[dst_])
                    pending[d] = []

                if with_out:
                    PAr = [PS[2], PS[3], PS[4], PS[5]]
                    pai = 0
                    for i in range(nsteps):
                        for d, chl in ((0, chunks_f), (1, chunks_b)):
                            if i >= len(chl):
                                continue
                            c = chl[i]
                            tsl = slice(c * 64, (c + 1) * 64)
                            h1 = slice(c * 64, c * 64 + 32)
                            h2 = slice(c * 64 + 32, c * 64 + 64)
                            ka, kb_ = (h1, h2) if d == 0 else (h2, h1)
                            pa_ = PAr[pai % 4]
                            pai += 1
                            S.op("pe", lambda E, d=d, tsl=tsl, ka=ka, pa_=pa_: E.matmul(pa_[0:32, 0:64], lhsT=kt[d][:, ka], rhs=qt[d][:, tsl], start=True, stop=True),
                                 reads=[kt[d], qt[d]], writes=[pa_])
                            S.op("pe", lambda E, d=d, kb_=kb_, pa_=pa_: E.matmul(pa_[0:32, 64:96], lhsT=kt[d][:, kb_], rhs=qt[d][:, kb_], start=True, stop=True),
                                 reads=[kt[d], qt[d]], writes=[pa_], acc=True)
                            S.op("dve", lambda E, d=d, c=c, pa_=pa_: E.tensor_tensor(out=atall[d][:, c, :], in0=pa_[0:32, 0:96], in1=masks[d][:, :], op=ALU.mult),
                                 reads=[pa_, masks[d]], writes=[atall[d]])
                for i in range(nsteps):
                    for d, chl in ((0, chunks_f), (1, chunks_b)):
                        if i >= len(chl):
                            continue
                        c = chl[i]
                        tsl = slice(c * 64, (c + 1) * 64)
                        if with_out:
                            sb = Sbf[d][i % 2]
                            S.op("dve", lambda E, d=d, sb=sb, c=c: E.tensor_scalar(out=sb[:, :], in0=Sst[d][:, :], scalar1=srt[d][:, c:c + 1], scalar2=None, op0=ALU.mult),
                                 reads=[Sst[d], srt[d]], writes=[sb])
                            va = vT[0:32, c, :] if d == 0 else vT2[:, c, :]
                            vb = vT2[:, c, :] if d == 0 else vT[0:32, c, :]
                            po = (c * 64) % 512
                            if pending[d] and (c * 64) // 512 != (pending[d][0] * 64) // 512:
                                flush(d)
                            po2 = po + 32 if d == 0 else po
                            pob = POB[d][((c * 64) // 512) % 2]
                            S.op("pe", lambda E, d=d, sb=sb, tsl=tsl, po=po, pob=pob: E.matmul(pob[:, po:po + 64], lhsT=sb[:, :], rhs=qt[d][:, tsl], start=True, stop=False),
                                 reads=[sb, qt[d]], writes=[pob], acc=True)
                            S.op("pe", lambda E, d=d, c=c, va=va, po=po, pob=pob: E.matmul(pob[:, po:po + 64], lhsT=va, rhs=atall[d][:, c, 0:64], start=False, stop=False),
                                 reads=[vT, vT2, atall[d]], writes=[pob], acc=True)
                            S.op("pe", lambda E, d=d, c=c, vb=vb, po2=po2, pob=pob: E.matmul(pob[:, po2:po2 + 32], lhsT=vb, rhs=atall[d][:, c, 64:96], start=False, stop=True),
                                 reads=[vT, vT2, atall[d]], writes=[pob], acc=True)
                            pending[d].append(c)
                        S.op("pe", lambda E, d=d, c=c: E.matmul(PP[d][:, 0:128], lhsT=kT[d][:, c, :], rhs=vT[:, c, :], start=True, stop=True),
                             reads=[kT[d], vT], writes=[PP[d]])
                        S.op("dve", lambda E, d=d, c=c: E.scalar_tensor_tensor(out=Sst[d][:, :], in0=Sst[d][:, :], scalar=stt_[d][:, c:c + 1], in1=PP[d][:, 0:128], op0=ALU.mult, op1=ALU.add),
                             reads=[Sst[d], stt_[d], PP[d]], writes=[Sst[d]])
                        if snap is not None and d == 0 and c == snap[0]:
                            S.op("act", lambda E: E.activation(out=snap[1], in_=Sst[0][:, :], func=AF.Copy), reads=[Sst[0]], writes=[carry])
                if with_out:
                    flush(0)
                    flush(1)

            def gnorm_store(h, lo, hi, dst_d, dstbuf, wb, hsrc, hbuf):
                gnw = pvec[:, pb + O_GNW: pb + O_GNW + 1]
                for (b0, b1) in blocks(lo, hi, 512):
                    n = b1 - b0
                    osq, orn, sgt = tp["s"], tp["sn"], tp["c"]
                    psg = nps()
                    proj(psg[:, 0:n], psg, wb, lambda k: wb[:, k, 4 * 128:5 * 128], hbuf, lambda k: hsrc[:, k, b0:b1])
                    S.op("act", lambda E: E.activation(out=sgt[:, 0:n], in_=psg[:, 0:n], func=AF.Silu), reads=[psg], writes=[sgt])
                    S.op("dve", lambda E: E.tensor_tensor(out=of[:, b0:b1], in0=of[:, b0:b1], in1=ob2[:, b0:b1], op=ALU.add), reads=[of, ob2], writes=[of])
                    S.op("act", lambda E: E.activation(out=osq[:, 0:n], in_=of[:, b0:b1], func=AF.Square), reads=[of], writes=[osq])
                    ps = nps()
                    S.op("pe", lambda E: E.matmul(ps[:, 0:n], lhsT=ones128, rhs=osq[:, 0:n], start=True, stop=True), reads=[cst, osq], writes=[ps])
                    S.op("act", lambda E: E.activation(out=orn[:, 0:n], in_=ps[:, 0:n], func=AF.Ln, bias=epsc[:, 0:1], scale=1.0), reads=[ps, epsc], writes=[orn])
                    S.op("act", lambda E: E.activation(out=orn[:, 0:n], in_=orn[:, 0:n], func=AF.Exp, scale=-0.5), reads=[orn], writes=[orn])
                    S.op("dve", lambda E: E.tensor_tensor(out=orn[:, 0:n], in0=orn[:, 0:n], in1=of[:, b0:b1], op=ALU.mult), reads=[orn, of], writes=[orn])
                    S.op("dve", lambda E: E.scalar_tensor_tensor(out=ob[:, b0:b1], in0=orn[:, 0:n], scalar=gnw, in1=sgt[:, 0:n], op0=ALU.mult, op1=ALU.mult),
                         reads=[orn, pvec, sgt], writes=[ob])
                S.dma("sp", dst_d[h, :, lo:hi], ob[:, lo:hi], reads=[ob], writes=[dstbuf], sembuf=ob)

            def load_head_w(hh):
                wb_ = wh[hh % 2]
                for g in range(NW):
                    S.dma("pool", wb_[:, :, g * 128:(g + 1) * 128], w_in[l, :, g * 1024 + hh * 128: g * 1024 + (hh + 1) * 128].rearrange("(k p) c -> p k c", p=128),
                          writes=[wb_], sembuf=wb_)

            load_head_w(0)
            for h in range(8):
                wb = wh[h % 2]
                if h + 1 < 8:
                    load_head_w(h + 1)
                if kind == "A":
                    dirs = [d for d in (0, 1) if (d == 0 and i < NSEG - 1 and l > 0) or (d == 1 and i > 0)]
                    if h == 0:
                        S.op("pool", lambda E: E.memset(stpack[:, :], 0.0), writes=[stpack])
                    blks = blocks(LO, HI, 512)
                    for j in range(0, len(blks), 2):
                        gens = []
                        for sl, (b0, b1) in enumerate(blks[j:j + 2]):
                            gens += hg_gates(h, wb, hT, hT, b0, b1 - b0, valid[:, b0:b1], valid, False, dirs, slot=sl, run=False)
                        drive(gens)
                    for d in dirs:
                        S.op("pool", lambda E, d=d: E.memset(Sst[d][:, :], 0.0), writes=[Sst[d]])
                    hg_scan(list(range(CS, CS + 32)) if 0 in dirs else [], list(range(CE - 1, CE - 33, -1)) if 1 in dirs else [], False, 0)
                    for d in dirs:
                        S.op("act", lambda E, d=d, h=h: E.activation(out=stpack[:, (d * 8 + h) * 128:(d * 8 + h + 1) * 128], in_=Sst[d][:, :], func=AF.Copy), reads=[Sst[d]], writes=[stpack])
                    for d, (ca, cb_) in ((0, (CS, CS + 32)), (1, (CE - 32, CE))):
                        if d not in dirs:
                            continue
                        col = 2048 + d * 8 + h
                        lgt = tp["dq"]
                        S.op("act", lambda E, d=d, ca=ca, cb_=cb_: E.activation(out=lgt[:, 0:32], in_=stt_[d][:, ca:cb_], func=AF.Ln), reads=[stt_[d]], writes=[lgt])
                        S.op("dve", lambda E: E.tensor_reduce(out=lgt[:, 32:33], in_=lgt[:, 0:32], axis=mybir.AxisListType.X, op=ALU.add), reads=[lgt], writes=[lgt])
                        S.op("act", lambda E, col=col: E.activation(out=stpack[:, col:col + 1], in_=lgt[:, 32:33], func=AF.Exp), reads=[lgt], writes=[stpack])
                else:
                    if ctx_needed:
                        hg_gates(h, wb, hTc, hTc, 0, CTX, ones_c[:, 0:CTX], ones_c, ctx_stream)
                        for d in range(2):
                            S.op("pool", lambda E, d=d: E.memset(Sst[d][:, :], 0.0), writes=[Sst[d]])
                        hg_scan([0, 1, 2, 3], [3, 2, 1, 0], ctx_stream, 0)
                        if ctx_stream:
                            gnorm_store(h, 0, CTX, oTc_d, DR["oTc_d"], wb, hTc, hTc)
                        if l == 0:
                            S.dma("sp", SCB[h], Sst[1][:, :], reads=[Sst[1]], writes=[DR["SCB"]], sembuf=Sst[1])
                    else:
                        S.dma("sp", Sst[1][:, :], SCB[h], reads=[DR["SCB"]], writes=[Sst[1]], sembuf=Sst[1])
                    stin = stin_r[h % 2]
                    for d in range(2):
                        S.dma("sp", stin[:, :, d, 0:128], st_in[:, :, (d * 8 + h) * 128:(d * 8 + h + 1) * 128].rearrange("s p c -> p s c"), reads=[DR["ST"]], writes=[stin], sembuf=stin)
                    for d in range(2):
                        if l == 0 and d == 0:
                            if i > 0:
                                S.op("act", lambda E, h=h: E.activation(out=Sst[0][:, :], in_=carry[:, h, :], func=AF.Copy), reads=[carry], writes=[Sst[0]])
                            continue
                        order = range(NSEG) if d == 0 else range(NSEG - 1, -1, -1)
                        for kseg in order:
                            m = foldm[:, d, kseg:kseg + 1]
                            S.op("dve", lambda E, kseg=kseg, d=d, m=m, h=h: E.tensor_scalar(out=alpha[:, 0:1], in0=stA[:, kseg, d * 8 + h:d * 8 + h + 1], scalar1=-1.0, scalar2=m, op0=ALU.add, op1=ALU.mult),
                                 reads=[stA, foldm], writes=[alpha])
                            S.op("dve", lambda E: E.tensor_scalar(out=alpha[:, 0:1], in0=alpha[:, 0:1], scalar1=1.0, scalar2=None, op0=ALU.add), reads=[alpha], writes=[alpha])
                            S.op("dve", lambda E, d=d: E.tensor_scalar(out=Sst[d][:, :], in0=Sst[d][:, :], scalar1=alpha[:, 0:1], scalar2=None, op0=ALU.mult), reads=[Sst[d], alpha], writes=[Sst[d]])
                            S.op("dve", lambda E, d=d, kseg=kseg, m=m, stin=stin: E.scalar_tensor_tensor(out=Sst[d][:, :], in0=stin[:, kseg, d, 0:128], scalar=m, in1=Sst[d][:, :], op0=ALU.mult, op1=ALU.add),
                                 reads=[stin, foldm, Sst[d]], writes=[Sst[d]])
                    for (b0, b1) in blocks(LO, HI, 512):
                        hg_gates(h, wb, hT, hT, b0, b1 - b0, valid[:, b0:b1], valid, True)
                    hg_scan(list(range(CS, CE)), list(range(CE - 1, CS - 1, -1)), True, 0, snap=((CS + 31, carry[:, h, :]) if l == 0 else None))
                    gnorm_store(h, LO, HI, oT_d, DR["oT_d"], wb, hT, hT)
                pass

            if kind == "A":
                S.dma("sp", ST[i], stpack[:, :], reads=[stpack], writes=[DR["ST"]], sembuf=stpack)
                S.barrier()
                S.emit()
                hg.close()
                mixer.close()
                return
            S.barrier()
            S.emit()
            hg.close()
            mixer.close()

            with contextlib.ExitStack() as ph:
                wres = S.sbuf("wres", [128, KD, 4096], BF16, ph)
                for j in range(8):
                    S.dma("pool", wres[:, :, j * 512:(j + 1) * 512], w_in[l, :, 5120 + j * 512: 5120 + (j + 1) * 512].rearrange("(k p) c -> p k c", p=128), writes=[wres], sembuf=wres)
                w3r = [S.sbuf("w3r%d" % i, [128, KD, 128], BF16, ph) for i in range(3)]
                w3i = [0]

                def w3load(src):
                    w3i[0] = (w3i[0] + 1) % 3
                    b = w3r[w3i[0]]
                    S.dma("pool", b[:, :, :], src.rearrange("(k p) c -> p k c", p=128), writes=[b], sembuf=b)
                    return b
                nt = NormTmp(ph)
                hTb = S.sbuf("hTb", [128, KD, 512], BF16, ph)
                ubs = [S.sbuf("ub%d" % q, [128, 512], F32, ph) for q in range(2)]
                sbbs = [S.sbuf("sbb%d" % q, [128, 512], F32, ph) for q in range(2)]
                accs = [S.sbuf("acc%d" % q, [128, 480], F32, ph) for q in range(2)]
                vTb = S.sbuf("vTb", [128, KD, 480], F32, ph)
                mean = S.sbuf("mean", [128, 480], F32, ph)
                var = S.sbuf("var", [128, 480], F32, ph)
                t1 = [S.sbuf("t1_%d" % i, [128, 480], F32, ph) for i in range(2)]
                yc = S.sbuf("yc", [128, KD, 480], BF16, ph)
                yT = S.sbuf("yT", [128, KD, 480], BF16, ph)
                oTb = S.sbuf("oTb", [128, 8, 480], BF16, ph)
                sgcs = [S.sbuf("sgc%d" % q, [128, 480], F32, ph) for q in range(2)]
                sghs = [S.sbuf("sgh%d" % q, [128, 480], F32, ph) for q in range(2)]
                m1s = [S.sbuf("m1_0", [128, 480], F32, ph)] * 2
                m2s = [S.sbuf("m2_0", [128, 480], F32, ph)] * 2
                xo = [S.sbuf("xo%d" % i, [128, 480], F32, ph) for i in range(2)]
                cvw = lambda tap, c: pvec[:, pb + O_CVW + tap * 8 + c: pb + O_CVW + tap * 8 + c + 1]

                def merge_seq(xsrc, lo, hi, srclo, srchi, s, oTsrc, oTbuf, vmask_buf, dst, dstbuf):
                    for (b0, b1) in blocks(lo, hi, 480):
                        n = b1 - b0
                        a0, a1 = max(srclo, b0 - 15), min(srchi, b1 + 15)
                        m = a1 - a0
                        off = a0 - (b0 - 15)
                        ctr = b0 - a0
                        norm_block(nt, xsrc[:, :, a0:a1].rearrange("k p t -> p k t"), DRX, m,
                                   lambda k: A1[:, k, s:s + 1], lambda k: SH1(k, s), hTb, lambda k: hTb[:, k, 0:m])
                        S.dma("sp", oTb[:, :, 0:n], oTsrc[:, :, b0:b1].rearrange("h p t -> p h t"), reads=[oTbuf], writes=[oTb], sembuf=oTb)
                        def cunit(c, slot):
                            ub_, sbb_, acc_ = ubs[slot], sbbs[slot], accs[slot]
                            psa = nps()
                            proj(psa[:, 0:m], psa, wres, lambda k, c=c: wres[:, k, c * 128:(c + 1) * 128], hTb, lambda k: hTb[:, k, 0:m])
                            psb_ = nps()
                            proj(psb_[:, 0:m], psb_, wres, lambda k, c=c: wres[:, k, 1024 + c * 128:1024 + (c + 1) * 128], hTb, lambda k: hTb[:, k, 0:m])
                            yield
                            S.op("act", lambda E: E.activation(out=sbb_[:, 0:m], in_=psb_[:, 0:m], func=AF.Sigmoid), reads=[psb_], writes=[sbb_])
                            yield
                            S.op("pool", lambda E: E.tensor_tensor(out=sbb_[:, 0:m], in0=sbb_[:, 0:m], in1=vmask_buf[:, a0:a1], op=ALU.mult), reads=[sbb_, vmask_buf], writes=[sbb_])
                            if off > 0 or m < n + 30:
                                S.op("pool", lambda E: E.memset(ub_[:, :], 0.0), writes=[ub_])
                            yield
                            S.op("dve", lambda E: E.tensor_tensor(out=ub_[:, off:off + m], in0=psa[:, 0:m], in1=sbb_[:, 0:m], op=ALU.mult), reads=[psa, sbb_], writes=[ub_])
                            yield
                            S.op("dve", lambda E, c=c: E.tensor_scalar(out=acc_[:, 0:n], in0=ub_[:, 0:n], scalar1=cvw(0, c), scalar2=pvec[:, pb + O_CVB + c: pb + O_CVB + c + 1], op0=ALU.mult, op1=ALU.add),
                                 reads=[ub_, pvec], writes=[acc_])
                            for tap in range(1, 31):
                                yield
                                S.op("dve", lambda E, c=c, tap=tap: E.scalar_tensor_tensor(out=acc_[:, 0:n], in0=ub_[:, tap:tap + n], scalar=cvw(tap, c), in1=acc_[:, 0:n], op0=ALU.mult, op1=ALU.add),
                                     reads=[ub_, pvec, acc_], writes=[acc_])
                            yield
                            S.op("act", lambda E, c=c: E.activation(out=vTb[:, c, 0:n], in_=acc_[:, 0:n], func=AF.Copy), reads=[acc_], writes=[vTb])

                        for c0_ in range(0, KD, 2):
                            gens = [cunit(c0_, 0), cunit(c0_ + 1, 1)]
                            while gens:
                                for g_ in list(gens):
                                    try:
                                        next(g_)
                                    except StopIteration:
                                        gens.remove(g_)
                        S.op("act", lambda E: E.activation(out=nt.sq[:, :, 0:n], in_=vTb[:, :, 0:n], func=AF.Square), reads=[vTb], writes=[nt.sq])
                        psm, psq = nps(), nps()
                        for k in range(KD):
                            S.op("pe", lambda E, k=k: E.matmul(psm[:, 0:n], lhsT=onesD, rhs=vTb[:, k, 0:n], start=(k == 0), stop=(k == KD - 1)), reads=[cst, vTb], writes=[psm], acc=True)
                        for k in range(KD):
                            S.op("pe", lambda E, k=k: E.matmul(psq[:, 0:n], lhsT=onesD, rhs=nt.sq[:, k, 0:n], start=(k == 0), stop=(k == KD - 1)), reads=[cst, nt.sq], writes=[psq], acc=True)
                        S.op("act", lambda E: E.activation(out=mean[:, 0:n], in_=psm[:, 0:n], func=AF.Copy), reads=[psm], writes=[mean])
                        S.op("dve", lambda E: E.tensor_tensor(out=var[:, 0:n], in0=mean[:, 0:n], in1=mean[:, 0:n], op=ALU.mult), reads=[mean], writes=[var])
                        S.op("dve", lambda E: E.tensor_tensor(out=var[:, 0:n], in0=psq[:, 0:n], in1=var[:, 0:n], op=ALU.subtract), reads=[psq, var], writes=[var])
                        S.op("act", lambda E: E.activation(out=var[:, 0:n], in_=var[:, 0:n], func=AF.Ln, bias=epsc[:, 0:1], scale=1.0), reads=[var, epsc], writes=[var])
                        S.op("act", lambda E: E.activation(out=var[:, 0:n], in_=var[:, 0:n], func=AF.Exp, scale=-0.5), reads=[var], writes=[var])
                        for c in range(KD):
                            tt = t1[c % 2]
                            S.op("dve", lambda E, c=c, tt=tt: E.tensor_tensor(out=tt[:, 0:n], in0=vTb[:, c, 0:n], in1=mean[:, 0:n], op=ALU.subtract), reads=[vTb, mean], writes=[tt])
                            S.op("pool", lambda E, tt=tt: E.tensor_tensor(out=tt[:, 0:n], in0=tt[:, 0:n], in1=var[:, 0:n], op=ALU.mult), reads=[tt, var], writes=[tt])
                            S.op("act", lambda E, c=c, tt=tt: E.activation(out=yc[:, c, 0:n], in_=tt[:, 0:n], func=AF.Silu, scale=pvec[:, pb + O_LNW + c: pb + O_LNW + c + 1], bias=pvec[:, pb + O_LNB + c: pb + O_LNB + c + 1]),
                                 reads=[tt, pvec], writes=[yc])
                        for dd in range(KD):
                            sgc, sgh, m1, m2 = sgcs[dd % 2], sghs[dd % 2], m1s[dd % 2], m2s[dd % 2]
                            wcv_ = w3load(w_cv_out[l, :, dd * 128:(dd + 1) * 128])
                            ps1 = nps()
                            proj(ps1[:, 0:n], ps1, wcv_, lambda k, wcv_=wcv_: wcv_[:, k, :], yc, lambda k: yc[:, k, 0:n])
                            psg = nps()
                            proj(psg[:, 0:n], psg, wres, lambda k, dd=dd: wres[:, k, 3072 + dd * 128:3072 + (dd + 1) * 128], hTb, lambda k: hTb[:, k, ctr:ctr + n])
                            S.op("act", lambda E: E.activation(out=sgc[:, 0:n], in_=psg[:, 0:n], func=AF.Sigmoid), reads=[psg], writes=[sgc])
                            S.op("dve", lambda E: E.tensor_tensor(out=m1[:, 0:n], in0=ps1[:, 0:n], in1=sgc[:, 0:n], op=ALU.mult), reads=[ps1, sgc], writes=[m1])
                            whg_ = w3load(w_hg_out[l, :, dd * 128:(dd + 1) * 128])
                            ps2 = nps()
                            proj(ps2[:, 0:n], ps2, whg_, lambda k, whg_=whg_: whg_[:, k, :], oTb, lambda k: oTb[:, k, 0:n])
                            psh = nps()
                            proj(psh[:, 0:n], psh, wres, lambda k, dd=dd: wres[:, k, 2048 + dd * 128:2048 + (dd + 1) * 128], hTb, lambda k: hTb[:, k, ctr:ctr + n])
                            S.op("act", lambda E: E.activation(out=sgh[:, 0:n], in_=psh[:, 0:n], func=AF.Sigmoid), reads=[psh], writes=[sgh])
                            S.op("dve", lambda E: E.tensor_tensor(out=m2[:, 0:n], in0=ps2[:, 0:n], in1=sgh[:, 0:n], op=ALU.mult), reads=[ps2, sgh], writes=[m2])
                            S.op("pool", lambda E, dd=dd: E.tensor_tensor(out=yT[:, dd, 0:n], in0=m1[:, 0:n], in1=m2[:, 0:n], op=ALU.add), reads=[m1, m2], writes=[yT])
                        for e in range(KD):
                            wo_ = w3load(w_out[l, :, e * 128:(e + 1) * 128])
                            pso = nps()
                            proj(pso[:, 0:n], pso, wo_, lambda k, wo_=wo_: wo_[:, k, :], yT, lambda k: yT[:, k, 0:n])
                            xb_ = xo[e % 2]
                            S.op("dve", lambda E, e=e, xb_=xb_: E.scalar_tensor_tensor(out=xb_[:, 0:n], in0=pso[:, 0:n], scalar=G1(e, s), in1=nt.xb[:, e, ctr:ctr + n], op0=ALU.mult, op1=ALU.add),
                                 reads=[pso, modv, nt.xb], writes=[xb_])
                            S.dma("sp", dst[e, :, b0:b1], xb_[:, 0:n], reads=[xb_], writes=[dstbuf], sembuf=xb_)
                    S.emit()

                merge_seq(xT, LO, HI, 0, W, 0, oT_d, DR["oT_d"], valid, xmid, DR["xmid"])
                if ctx_stream:
                    merge_seq(cxT, 0, CTX, 0, CTX, 1, oTc_d, DR["oTc_d"], ones_c, cxmid, DR["cxmid"])
                S.barrier()
                S.emit()

            NF = HI - LO
            ffn = contextlib.ExitStack()
            h2T = S.sbuf("h2T", [128, KD, NF], BF16, ffn)
            h2Tc = S.sbuf("h2Tc", [128, KD, CTX], BF16, ffn) if ctx_stream else None
            with contextlib.ExitStack() as ph:
                nt = NormTmp(ph, nbuf=2)
                for (b0, b1) in blocks(LO, HI, 512):
                    norm_block(nt, xmid[:, :, b0:b1].rearrange("k p t -> p k t"), DR["xmid"], b1 - b0,
                               lambda k: A2[:, k, 0:1], lambda k: SH2(k, 0), h2T, lambda k, b0=b0, b1=b1: h2T[:, k, b0 - LO:b1 - LO])
                if ctx_stream:
                    norm_block(nt, cxmid.rearrange("k p t -> p k t"), DR["cxmid"], CTX,
                               lambda k: A2[:, k, 1:2], lambda k: SH2(k, 1), h2Tc, lambda k: h2Tc[:, k, 0:CTX])
                S.barrier()
                S.emit()
            zT = S.sbuf("zT", [128, KF, NT], BF16, ffn)
            zTc = S.sbuf("zTc", [128, KF, CTX], BF16, ffn) if ctx_stream else None
            mcb = S.sbuf("mcb", [128, 2, 64], BF16, ffn)
            S.dma("pool", mcb[:, :, :], mcol_d[:, :, 0:64], writes=[mcb], sembuf=mcb)
            with contextlib.ExitStack() as ph:
                uc = [S.sbuf("uc%d" % i, [128, NF + 2], F32, ph) for i in range(3)]
                FH = NT // 2
                fa1s = [S.sbuf("fa1_%d" % q, [128, FH], F32, ph) for q in range(2)]
                fa2s = [S.sbuf("fa2_%d" % q, [128, FH], F32, ph) for q in range(2)]
                wuv = [S.sbuf("wuv%d" % i, [128, KD, 256], BF16, ph) for i in range(2)]
                for i in range(3):
                    S.op("pool", lambda E, i=i: E.memset(uc[i][:, :], 0.0), writes=[uc[i]])
                fw = lambda tap, c: pvec[:, pb + O_FW + tap * KF + c: pb + O_FW + tap * KF + c + 1]
                fb = lambda c: pvec[:, pb + O_FB + c: pb + O_FB + c + 1]

                def conv_taps(taps, n, c, fa1, fa2):
                    for j, (si, st0, tap) in enumerate(taps):
                        if j == 0:
                            S.op("dve", lambda E, si=si, st0=st0, tap=tap: E.tensor_scalar(out=fa1[:, 0:n], in0=uc[si][:, st0:st0 + n], scalar1=fw(tap, c), scalar2=fb(c), op0=ALU.mult, op1=ALU.add),
                                 reads=[uc[si], pvec], writes=[fa1])
                        else:
                            S.op("dve", lambda E, si=si, st0=st0, tap=tap: E.scalar_tensor_tensor(out=fa1[:, 0:n], in0=uc[si][:, st0:st0 + n], scalar=fw(tap, c), in1=fa1[:, 0:n], op0=ALU.mult, op1=ALU.add),
                                 reads=[uc[si], pvec, fa1], writes=[fa1])
                        yield
                    S.op("act", lambda E: E.activation(out=fa2[:, 0:n], in_=fa1[:, 0:n], func=AF.Gelu), reads=[fa1], writes=[fa2])

                def drive(gens):
                    gens = list(gens)
                    while gens:
                        for g_ in list(gens):
                            try:
                                next(g_)
                            except StopIteration:
                                gens.remove(g_)

                cc_ = [0]
                for c in range(KF):
                    cc_[0] = c
                    wb_ = wuv[c % 2]
                    S.dma("pool", wb_[:, :, 0:128], w_up[l, :, c * 128:(c + 1) * 128].rearrange("(k p) c -> p k c", p=128), writes=[wb_], sembuf=wb_)
                    S.dma("pool", wb_[:, :, 128:256], w_up[l, :, FF + c * 128:FF + (c + 1) * 128].rearrange("(k p) c -> p k c", p=128), writes=[wb_], sembuf=wb_)
                    for (i0, i1) in blocks(0, NF, 512):
                        ps = nps()
                        proj(ps[:, 0:i1 - i0], ps, wb_, lambda k, wb_=wb_: wb_[:, k, 0:128], h2T, lambda k, i0=i0, i1=i1: h2T[:, k, i0:i1])
                        S.op("dve", lambda E, ps=ps, i0=i0, i1=i1: E.tensor_tensor(out=uc[0][:, 1 + i0:1 + i1], in0=ps[:, 0:i1 - i0], in1=valid[:, LO + i0:LO + i1], op=ALU.mult),
                             reads=[ps, valid], writes=[uc[0]])
                    for mi in range(2):
                        S.op("dve", lambda E, mi=mi: E.tensor_tensor(out=uc[1 + mi][:, 1:1 + NF].rearrange("p (c t) -> p c t", t=64), in0=uc[0][:, 1:1 + NF].rearrange("p (c t) -> p c t", t=64),
                                                                 in1=mcb[:, mi:mi + 1, :].to_broadcast([128, NF // 64, 64]), op=ALU.mult), reads=[uc[0], mcb], writes=[uc[1 + mi]])
                    taps = []
                    for ky in range(3):
                        for kx in range(3):
                            offs = (ky - 1) * 64 + (kx - 1)
                            si = {0: 1, 1: 0, 2: 2}[kx]
                            taps.append((si, 1 + 64 + offs, ky * 3 + kx))
                    drive([conv_taps([(si, st0 + hb * FH, tap) for (si, st0, tap) in taps], FH, c, fa1s[hb], fa2s[hb]) for hb in range(2)])
                    for hb in range(2):
                        fa2 = fa2s[hb]
                        for (j0, j1) in blocks(hb * FH, (hb + 1) * FH, 512):
                            ps = nps()
                            proj(ps[:, 0:j1 - j0], ps, wb_, lambda k, wb_=wb_: wb_[:, k, 128:256], h2T, lambda k, j0=j0, j1=j1: h2T[:, k, 64 + j0:64 + j1])
                            S.op("dve", lambda E, ps=ps, j0=j0, j1=j1, c=c, hb=hb: E.tensor_tensor(out=zT[:, c, j0:j1], in0=ps[:, 0:j1 - j0], in1=fa2[:, j0 - hb * FH:j1 - hb * FH], op=ALU.mult),
                                 reads=[ps, fa2], writes=[zT])
                    if ctx_stream:
                        ps = nps()
                        proj(ps[:, 0:CTX], ps, wb_, lambda k, wb_=wb_: wb_[:, k, 0:128], h2Tc, lambda k: h2Tc[:, k, 0:CTX])
                        S.op("pool", lambda E: E.memset(uc[0][:, 0:CTX + 2], 0.0), writes=[uc[0]])
                        S.op("act", lambda E, ps=ps: E.activation(out=uc[0][:, 1:1 + CTX], in_=ps[:, 0:CTX], func=AF.Copy), reads=[ps], writes=[uc[0]])
                        drive([conv_taps([(0, 0, 3), (0, 1, 4), (0, 2, 5)], CTX, c, fa1s[0], fa2s[0])])
                        fa2 = fa2s[0]
                        ps = nps()
                        proj(ps[:, 0:CTX], ps, wb_, lambda k, wb_=wb_: wb_[:, k, 128:256], h2Tc, lambda k: h2Tc[:, k, 0:CTX])
                        S.op("dve", lambda E, ps=ps, c=c: E.tensor_tensor(out=zTc[:, c, 0:CTX], in0=ps[:, 0:CTX], in1=fa2[:, 0:CTX], op=ALU.mult), reads=[ps, fa2], writes=[zTc])
                        S.op("pool", lambda E: E.memset(uc[0][:, 0:1], 0.0), writes=[uc[0]])
                    pass
                S.barrier()
                S.emit()
            with contextlib.ExitStack() as ph:
                wd = [S.sbuf("wd%d" % i, [128, KF, 128], BF16, ph) for i in range(2)]
                xr = [S.sbuf("xr%d" % i, [128, 512], F32, ph) for i in range(2)]
                xw = [S.sbuf("xw%d" % i, [128, 512], F32, ph) for i in range(2)]
                cnt = 0
                for e in range(KD):
                    wd_ = wd[e % 2]
                    S.dma("pool", wd_[:, :, :], w_down[l, :, e * 128:(e + 1) * 128].rearrange("(c p) n -> p c n", p=128), writes=[wd_], sembuf=wd_)
                    jobs = [(zT, j0, j1, xmid, DR["xmid"], T0, x_dst, 0) for (j0, j1) in blocks(0, NT, 512)]
                    if ctx_stream:
                        jobs.append((zTc, 0, CTX, cxmid, DR["cxmid"], 0, cx_out, 1))
                    for (zb, j0, j1, xsrc, xsb, xoff, dstap, s) in jobs:
                        n = j1 - j0
                        ps = nps()
                        for c in range(KF):
                            S.op("pe", lambda E, c=c, zb=zb, j0=j0, j1=j1, ps=ps, wd_=wd_, n=n: E.matmul(ps[:, 0:n], lhsT=wd_[:, c, :], rhs=zb[:, c, j0:j1], start=(c == 0), stop=(c == KF - 1)),
                                 reads=[wd_, zb], writes=[ps], acc=True)
                        xr_, xw_ = xr[cnt % 2], xw[cnt % 2]
                        cnt += 1
                        S.dma("sp", xr_[:, 0:n], xsrc[e, :, xoff + j0:xoff + j1], reads=[xsb], writes=[xr_], sembuf=xr_)
                        S.op("dve", lambda E, e=e, s=s, ps=ps, xr_=xr_, xw_=xw_, n=n: E.scalar_tensor_tensor(out=xw_[:, 0:n], in0=ps[:, 0:n], scalar=G2(e, s), in1=xr_[:, 0:n], op0=ALU.mult, op1=ALU.add),
                             reads=[ps, modv, xr_], writes=[xw_])
                        if s == 0 and is_last:
                            S.dma("sp", xnew[e, :, j0:j1], xw_[:, 0:n], reads=[xw_], writes=[DR["xnew"]], sembuf=xw_)
                        else:
                            S.dma("sp", dstap[e, :, j0:j1], xw_[:, 0:n], reads=[xw_], writes=[(x_dst_buf if s == 0 else DR["CX1"])], sembuf=xw_)
                S.barrier()
                S.emit()
            ffn.close()
            if is_last:
                with contextlib.ExitStack() as ph:
                    nt = NormTmp(ph)
                    fo = [S.sbuf("fo%d" % i, [128, KD, 512], F32, ph) for i in range(2)]
                    for bi, (j0, j1) in enumerate(blocks(0, NT, 512)):
                        fo_ = fo[bi % 2]
                        norm_block(nt, xnew[:, :, j0:j1].rearrange("k p t -> p k t"), DR["xnew"], j1 - j0,
                                   lambda k: pvec[:, O_FNW + k:O_FNW + k + 1], lambda k: None, fo_, lambda k, fo_=fo_, j0=j0, j1=j1: fo_[:, k, 0:j1 - j0])
                        S.dma("sp", x_dst[:, :, j0:j1].rearrange("k p t -> p k t"), fo_[:, :, 0:j1 - j0], reads=[fo_], writes=[x_dst_buf], sembuf=fo_)
                    S.barrier()
                    S.emit()
            else:
                S.barrier()
                S.emit()

        DRE = DR["ext"]
        with contextlib.ExitStack() as ph:
            zt = S.sbuf("zt", [128, NST], F32, ph)
            S.op("pool", lambda E: E.memset(zt[:, :], 0.0), writes=[zt])
            for q in range(NSEG):
                S.dma("sp", ST[q], zt[:, :], reads=[zt], writes=[DR["ST"]], sembuf=zt)
            S.barrier()
            S.emit()
        for l in range(DEPTH):
            layer_setup(l)
            for i in range(NSEG):
                if l == 0:
                    if i == 0:
                        continue
                    run_pass("A", l, i, xT4[i], cxT_in, DRE, DRE, None, None, False, False, valid4_d[i], fold4_d[i])
                else:
                    run_pass("A", l, i, XW[i], CX1, DR["XW"], DR["CX1"], None, None, False, False, valid4_d[i], fold4_d[i])
            if l == 0:
                for i in range(NSEG):
                    run_pass("B", l, i, xT4[i], cxT_in, DRE, DRE, X1[i], DR["X1"], i == 0, False, valid4_d[i], fold4_d[i])
                for i in range(NSEG):
                    S.dma("sp", XW[i, :, :, HX:HX + NT], X1[i], reads=[DR["X1"]], writes=[DR["XW"]], sembuf=DR["XW"])
                    S.dma("sp", XW[i, :, :, 0:HX], X1[(i - 1) % NSEG, :, :, NT - HX:NT], reads=[DR["X1"]], writes=[DR["XW"]], sembuf=DR["XW"])
                    S.dma("sp", XW[i, :, :, HX + NT:W], X1[(i + 1) % NSEG, :, :, 0:HX], reads=[DR["X1"]], writes=[DR["XW"]], sembuf=DR["XW"])
                S.barrier()
                S.emit()
            else:
                with contextlib.ExitStack() as ph:
                    oh = S.sbuf("oh", [128, NSEG], F32, ph)
                    S.dma("sp", oh[:, :], onehot_d, writes=[oh], sembuf=oh)
                    xs = [S.sbuf("sel%d" % q, [128, KD, 512], F32, ph) for q in range(NSEG)]
                    acc = [S.sbuf("selacc%d" % q, [128, KD, 512], F32, ph) for q in range(2)]
                    for bi, (b0, b1) in enumerate(blocks(0, W, 512)):
                        n = b1 - b0
                        a_ = acc[bi % 2]
                        for q in range(NSEG):
                            S.dma("sp", xs[q][:, :, 0:n], XW[q, :, :, b0:b1].rearrange("k p t -> p k t"), reads=[DR["XW"]], writes=[xs[q]], sembuf=xs[q])
                        S.op("dve", lambda E, a_=a_, n=n: E.tensor_scalar(out=a_[:, :, 0:n], in0=xs[0][:, :, 0:n], scalar1=oh[:, 0:1], scalar2=None, op0=ALU.mult), reads=[xs[0], oh], writes=[a_])
                        for q in range(1, NSEG):
                            S.op("dve", lambda E, a_=a_, n=n, q=q: E.scalar_tensor_tensor(out=a_[:, :, 0:n], in0=xs[q][:, :, 0:n], scalar=oh[:, q:q + 1], in1=a_[:, :, 0:n], op0=ALU.mult, op1=ALU.add),
                                 reads=[xs[q], oh, a_], writes=[a_])
                        S.dma("sp", XWO[:, :, b0:b1].rearrange("k p t -> p k t"), a_[:, :, 0:n], reads=[a_], writes=[DR["XWO"]], sembuf=a_)
                    S.barrier()
                    S.emit()
                run_pass("B", l, 0, XWO, CX1, DR["XWO"], DR["CX1"], x_out, DR["x_out"], False, True, validown_d, foldown_d)
        S.barrier()
        S.emit(final=True)
    return nc


def _chan(v):
    return np.ascontiguousarray(v.reshape(-1, 128).T)


def pack_pvec(inp):
    pv = np.zeros((128, NPV), np.float32)
    for l in range(DEPTH):
        b = l * PL
        pv[:, b + O_N1W:b + O_N1W + 8] = _chan(inp["norm1_w"][l])
        pv[:, b + O_BMOD:b + O_BMOD + 48] = _chan(inp["b_mod"][l])
        pv[:, b + O_GNW] = inp["hg_gnorm_w"][l]
        for tap in range(31):
            pv[:, b + O_CVW + tap * 8: b + O_CVW + tap * 8 + 8] = _chan(inp["cv_dw_w"][l, tap])
        pv[:, b + O_CVB:b + O_CVB + 8] = _chan(inp["cv_dw_b"][l])
        pv[:, b + O_LNW:b + O_LNW + 8] = _chan(inp["cv_ln_w"][l])
        pv[:, b + O_LNB:b + O_LNB + 8] = _chan(inp["cv_ln_b"][l])
        pv[:, b + O_N2W:b + O_N2W + 8] = _chan(inp["norm2_w"][l])
        fw = inp["ffn_dw_w"][l].reshape(9, FF)
        for tap in range(9):
            pv[:, b + O_FW + tap * KF: b + O_FW + (tap + 1) * KF] = _chan(fw[tap])
        pv[:, b + O_FB:b + O_FB + KF] = _chan(inp["ffn_dw_b"][l])
    for d in range(2):
        for l in range(DEPTH):
            pv[:, O_LBZ + d * 16 + l * 8: O_LBZ + d * 16 + l * 8 + 8] = _chan(inp["hg_lb_logits"][d, l])
    pv[:, O_FNW:O_FNW + 8] = _chan(inp["final_norm_w"])
    return pv


def make_consts():
    cst = np.zeros((128, 6, 128), np.float32)
    cst[:, 0, :] = np.eye(128, dtype=np.float32)
    cst[:, 1, :] = 1.0 / D
    cst[:, 2, :] = 1.0 / 128
    s = np.arange(64)[:, None]
    t = np.arange(64)[None, :]
    mF = (s <= t).astype(np.float32)
    mB = (s >= t).astype(np.float32)
    cst[0:32, 3, 0:64] = mF[0:32, :]
    cst[0:32, 3, 64:96] = mF[32:64, 32:64]
    cst[0:32, 4, 0:64] = mB[32:64, :]
    cst[0:32, 4, 64:96] = mB[0:32, 0:32]
    rst = np.ones((128, 512), np.float32)
    rst[:, ::64] = 0.0
    mcol = np.ones((128, 2, W), np.float32)
    pos = np.arange(W)
    mcol[:, 0, pos % 64 == 63] = 0.0
    mcol[:, 1, pos % 64 == 0] = 0.0
    return cst, rst, mcol


def window_T(xfull_b, seg):
    t0 = seg * NT - HX
    w = np.zeros((W, D), np.float32)
    a, b = max(t0, 0), min(t0 + W, SEQ)
    w[a - t0:b - t0] = xfull_b[a:b]
    return np.ascontiguousarray(w.T.reshape(KD, 128, W))


def core_static(inp, c):
    b, seg = c // NSEG, c % NSEG
    t0 = seg * NT - HX
    pos = np.arange(W) + t0
    valid = np.broadcast_to(((pos >= 0) & (pos < SEQ)).astype(np.float32), (128, W)).copy()
    fold = np.zeros((128, 2, NSEG), np.float32)
    for k in range(NSEG):
        fold[:, 0, k] = 1.0 if k < seg else 0.0
        fold[:, 1, k] = 1.0 if k > seg else 0.0
    cvec = np.stack([_chan(inp["c"][b]), _chan(inp["c_ctx"])], axis=-1)
    return dict(valid=valid, foldm=fold, cvec=np.ascontiguousarray(cvec))


_NC_CACHE = {}


def seg_static(seg):
    t0 = seg * NT - HX
    pos = np.arange(W) + t0
    valid = np.broadcast_to(((pos >= 0) & (pos < SEQ)).astype(np.float32), (128, W)).copy()
    fold = np.zeros((128, 2, NSEG), np.float32)
    for k in range(NSEG):
        fold[:, 0, k] = 1.0 if k < seg else 0.0
        fold[:, 1, k] = 1.0 if k > seg else 0.0
    return valid, fold


def kernel(**inp):
    inp = {k: np.asarray(v) for k, v in inp.items()}
    pv = pack_pvec(inp)
    cst, rst, mcol = make_consts()
    x = np.ascontiguousarray(inp["x"], dtype=np.float32)
    ctx = np.ascontiguousarray(inp["ctx"], dtype=np.float32)
    segs = [seg_static(s) for s in range(NSEG)]
    valid4 = np.ascontiguousarray(np.stack([v for v, _ in segs], axis=0))
    fold4 = np.ascontiguousarray(np.stack([f for _, f in segs], axis=0))
    xT4 = [np.ascontiguousarray(np.stack([window_T(x[b], s) for s in range(NSEG)], axis=0)) for b in range(BATCH)]
    cxT = [np.ascontiguousarray(ctx[b].T.reshape(KD, 128, CTX)) for b in range(BATCH)]
    maps = []
    for c in range(NCORE):
        b, seg = c // NSEG, c % NSEG
        oh = np.zeros((128, NSEG), np.float32)
        oh[:, seg] = 1.0
        maps.append(dict(
            xT4=xT4[b], cxT=cxT[b], cvec=np.ascontiguousarray(np.stack([_chan(inp["c"][b]), _chan(inp["c_ctx"])], axis=-1)),
            valid4=valid4, fold4=fold4, validown=segs[seg][0], foldown=segs[seg][1], onehot=oh,
            pvec=pv, cst=cst, rst=rst, mcol=mcol, w_mod=inp["w_mod"], w_in=inp["w_in"],
            w_hg_out=inp["w_hg_out"], w_cv_out=inp["w_cv_out"], w_out=inp["w_out"], w_up=inp["w_up"], w_down=inp["w_down"]))
    if "nc" not in _NC_CACHE:
        _NC_CACHE["nc"] = build()
    res = run_bass_kernel_spmd(_NC_CACHE["nc"], maps, core_ids=list(range(NCORE)))
    out = np.empty((BATCH, SEQ, D), np.float32)
    for c in range(NCORE):
        b, seg = c // NSEG, c % NSEG
        out[b, seg * NT:(seg + 1) * NT, :] = res.results[c]["x_out"].reshape(D, NT).T
    return out
```

```python
import contextlib
import types
import numpy as np
import concourse.bass as bass
import concourse.mybir as mybir
from concourse.bass_utils import run_bass_kernel_spmd

F32 = mybir.dt.float32
BF16 = mybir.dt.bfloat16
AF = mybir.ActivationFunctionType
ALU = mybir.AluOpType

NSEM_ENG = 8
DEBUG = False
NLAYERS_RUN = 2

D = 1024
KD = 8
BATCH = 2
SEQ = 8192
DEPTH = 2
CTX = 256
PTOT = 9216
FF = 2816
KF = 22
NCORE = 8
NSEG = 4
NT = SEQ // NSEG
HX = 128
W = NT + 2 * HX
CS, CE = 1, 35
LO, HI = CS * 64, CE * 64
T0, T1 = HX, HX + NT
EPS = 1e-6
G_MIN = 1e-6
NST = 2 * 8 * 128 + 16

PL = 557
O_N1W, O_BMOD, O_GNW, O_CVW, O_CVB, O_LNW, O_LNB, O_N2W, O_FW, O_FB = 0, 8, 56, 57, 305, 313, 321, 329, 337, 535
O_LBZ = 2 * PL
O_FNW = O_LBZ + 32
NPV = O_FNW + 8


def freeze(fn, depth=0):
    if not isinstance(fn, types.FunctionType) or fn.__closure__ is None:
        return fn
    cells = []
    for c in fn.__closure__:
        try:
            v = c.cell_contents
        except ValueError:
            cells.append(c)
            continue
        if isinstance(v, types.FunctionType) and depth < 4:
            v = freeze(v, depth + 1)
        cells.append(types.CellType(v))
    g = types.FunctionType(fn.__code__, fn.__globals__, fn.__name__, fn.__defaults__, tuple(cells))
    g.__kwdefaults__ = fn.__kwdefaults__
    return g


class Buf:
    def __init__(self, name, t=None):
        self.name = name
        self.t = t
        self.last_w = None
        self.reads = []
        self.dsem = None

    def __getitem__(self, k):
        return self.t[k]


class Sched:
    ENGS = ("pe", "act", "dve", "pool", "sp")

    def __init__(self, nc, stack):
        self.nc = nc
        self.stack = stack
        self.streams = {e: [] for e in self.ENGS}
        self.count = {e: 0 for e in self.ENGS}
        self.sems = {}
        self.semval = {}
        self.waited = {e: {} for e in self.ENGS}
        self.final_events = []
        self.pending = {}
        self.last_ev = {}
        self.uid = 0
        self.free_dsems = []

    def sem(self, key):
        if key not in self.sems:
            self.sems[key] = self.stack.enter_context(self.nc.semaphore("s_%s_%s" % (key[0], key[1])))
            self.semval[key] = 0
        return self.sems[key]

    def sbuf(self, name, shape, dt, stack=None):
        self.uid += 1
        name = "%s_%d" % (name, self.uid)
        t = (stack or self.stack).enter_context(self.nc.sbuf_tensor(name, list(shape), dt))
        b = Buf(name, t)
        if stack is not None:
            stack.callback(self._release, b)
        return b

    def _release(self, b):
        if b.dsem is not None:
            self.free_dsems.append(b.dsem)
            b.dsem = None

    def psum(self, name, shape, dt):
        t = self.stack.enter_context(self.nc.psum_tensor(name, list(shape), dt))
        return Buf(name, t)

    def _deps(self, eng, reads, writes, acc):
        evs = []
        for b in reads:
            if b.last_w is not None:
                evs.append(b.last_w)
        for b in writes:
            if b.last_w is not None and not (acc and b.last_w[0][0] == "pe" and eng == "pe"):
                evs.append(b.last_w)
            evs.extend(b.reads)
        best = {}
        for (k, v) in evs:
            best[k] = max(best.get(k, 0), v)
        p = self.pending.pop(eng, None)
        if p:
            for (k, v) in p:
                best[k] = max(best.get(k, 0), v)
        waits = []
        w = self.waited[eng]
        for k, v in best.items():
            if w.get(k, 0) < v:
                waits.append((k, v))
                w[k] = v
        return waits

    def _commit(self, ev, reads, writes):
        for b in reads:
            b.reads.append(ev)
        for b in writes:
            b.last_w = ev
            b.reads = []
        self.last_ev[ev[0]] = ev[1]

    def op(self, eng, fn, reads=(), writes=(), acc=False):
        reads = [b for b in reads if b is not None]
        writes = [b for b in writes if b is not None]
        waits = self._deps(eng, reads, writes, acc)
        idx = self.count[eng]
        self.count[eng] += 1
        key = (eng, idx % NSEM_ENG)
        self.sem(key)
        ev = (key, idx // NSEM_ENG + 1)
        self.streams[eng].append((waits, freeze(fn), key, 1))
        self._commit(ev, reads, writes)
        return ev

    def dma(self, eng, out_ap, in_ap, reads=(), writes=(), sembuf=None):
        reads = [b for b in reads if b is not None]
        writes = [b for b in writes if b is not None]
        waits = self._deps(eng, reads, writes, False)
        if sembuf.dsem is None:
            if self.free_dsems:
                sembuf.dsem = self.free_dsems.pop()
            else:
                sembuf.dsem = ("dma", sembuf.name)
                self.sem(sembuf.dsem)
        key = sembuf.dsem
        self.semval[key] += 16
        ev = (key, self.semval[key])
        self.streams[eng].append((waits, lambda E: E.dma_start(out=out_ap, in_=in_ap), key, 16))
        self._commit(ev, reads, writes)
        return ev

    def barrier(self):
        evs = list(self.last_ev.items())
        for e in self.ENGS:
            self.pending[e] = list(evs)

    def emit(self, final=False):
        nc = self.nc
        sems = self.sems
        streams = self.streams
        final_events = list(self.last_ev.items()) if final else []
        with nc.Block() as block:
            def run(E, eng):
                for (waits, fn, key, inc) in streams[eng]:
                    for (k, v) in waits:
                        E.wait_ge(sems[k], v)
                    ins = fn(E)
                    ins.then_inc(sems[key], inc)
                if eng == "sp":
                    for (k, v) in final_events:
                        E.wait_ge(sems[k], v)
                streams[eng] = []

            @block.tensor
            def _(E):
                run(E, "pe")

            @block.scalar
            def _(E):
                run(E, "act")

            @block.vector
            def _(E):
                run(E, "dve")

            @block.gpsimd
            def _(E):
                run(E, "pool")

            @block.sync
            def _(E):
                run(E, "sp")


def blocks(lo, hi, n):
    out = []
    t = lo
    while t < hi:
        out.append((t, min(hi, t + n)))
        t += n
    return out


def build():
    nc = bass.Bass("TRN2", target_bir_lowering=False)
    dt_in = lambda name, shape: nc.dram_tensor(name, list(shape), F32, kind="ExternalInput").ap()
    xT4 = dt_in("xT4", [NSEG, KD, 128, W])
    cxT_in = dt_in("cxT", [KD, 128, CTX])
    cvec = dt_in("cvec", [128, KD, 2])
    valid4_d = dt_in("valid4", [NSEG, 128, W])
    pvec_d = dt_in("pvec", [128, NPV])
    cst_d = dt_in("cst", [128, 6, 128])
    rst_d = dt_in("rst", [128, 512])
    mcol_d = dt_in("mcol", [128, 2, W])
    fold4_d = dt_in("fold4", [NSEG, 128, 2, NSEG])
    validown_d = dt_in("validown", [128, W])
    foldown_d = dt_in("foldown", [128, 2, NSEG])
    onehot_d = dt_in("onehot", [128, NSEG])
    w_mod = dt_in("w_mod", [DEPTH, D, 6 * D])
    w_in = dt_in("w_in", [DEPTH, D, PTOT])
    w_hg_out = dt_in("w_hg_out", [DEPTH, D, D])
    w_cv_out = dt_in("w_cv_out", [DEPTH, D, D])
    w_out = dt_in("w_out", [DEPTH, D, D])
    w_up = dt_in("w_up", [DEPTH, D, 2 * FF])
    w_down = dt_in("w_down", [DEPTH, FF, D])
    x_out = nc.dram_tensor("x_out", [KD, 128, NT], F32, kind="ExternalOutput").ap()
    itn = lambda name, shape, dt=F32: nc.dram_tensor(name, list(shape), dt, kind="Internal").ap()
    ST = itn("ST", [NSEG, 128, NST])
    X1 = itn("X1", [NSEG, KD, 128, NT])
    XW = itn("XW", [NSEG, KD, 128, W])
    XWO = itn("XWO", [KD, 128, W])
    CX1 = itn("CX1", [KD, 128, CTX])
    SCB = itn("SCB", [8, 128, 128])
    xmid = itn("xmid", [KD, 128, W])
    xnew = itn("xnew", [KD, 128, NT])
    cxmid = itn("cxmid", [KD, 128, CTX])
    oT_d = itn("oT_d", [8, 128, W], BF16)
    oTc_d = itn("oTc_d", [8, 128, CTX], BF16)
    DR = {n: Buf(n) for n in ("xmid", "xnew", "cxmid", "oT_d", "oTc_d", "x_out", "cx_out", "ST", "X1", "XW", "XWO", "CX1", "SCB", "ext")}

    with contextlib.ExitStack() as st:
        S = Sched(nc, st)
        PS = [S.psum("ps%d" % i, [128, 512], F32) for i in range(7)]
        PSB = S.psum("psb", [128, 1024], BF16)
        psi = [0]

        def nps():
            psi[0] = (psi[0] + 1) % 7
            return PS[psi[0]]

        pvec = S.sbuf("pvec", [128, NPV], F32)
        cst = S.sbuf("cst", [128, 6, 128], F32)
        identb = S.sbuf("identb", [128, 128], BF16)
        maskFb = S.sbuf("maskFb", [32, 96], BF16)
        maskBb = S.sbuf("maskBb", [32, 96], BF16)
        valid = S.sbuf("valid", [128, W], BF16)
        ones_c = S.sbuf("ones_c", [128, CTX], BF16)
        rst = S.sbuf("rst", [128, 512], F32)
        foldm = S.sbuf("foldm", [128, 2, NSEG], F32)
        modv = S.sbuf("modv", [128, 48, 2], F32)
        A1 = S.sbuf("A1", [128, KD, 2], F32)
        A2 = S.sbuf("A2", [128, KD, 2], F32)
        lbv = S.sbuf("lbv", [128, 2, 3, KD], F32)
        rcp = S.sbuf("rcp", [128, KD], F32)
        epsc = S.sbuf("epsc", [128, 1], F32)
        carry = S.sbuf("carry", [128, 8, 128], F32)
        S.op("pool", lambda E: E.memset(epsc[:, :], EPS), writes=[epsc])
        S.dma("sp", pvec[:, :], pvec_d, writes=[pvec], sembuf=pvec)
        S.dma("sp", cst[:, :, :], cst_d, writes=[cst], sembuf=cst)
        S.dma("sp", rst[:, :], rst_d, writes=[rst], sembuf=rst)
        S.dma("pool", identb[:, :], cst_d[:, 0, :], writes=[identb], sembuf=identb)
        S.dma("pool", maskFb[:, :], cst_d[0:32, 3, 0:96], writes=[maskFb], sembuf=maskFb)
        S.dma("pool", maskBb[:, :], cst_d[0:32, 4, 0:96], writes=[maskBb], sembuf=maskBb)
        S.op("pool", lambda E: E.memset(ones_c[:, :], 1.0), writes=[ones_c])
        onesD = cst[:, 1, :]
        ones128 = cst[:, 2, :]

        def layer_setup(l):
            pb = l * PL
            for d in range(2):
                lb_ap = lbv[:, d, 0, :]
                if l == 0:
                    S.op("dve", lambda E, lb_ap=lb_ap: E.memset(lb_ap, 0.0), writes=[lbv])
                else:
                    z0 = pvec[:, O_LBZ + d * 16: O_LBZ + d * 16 + 8]
                    z1 = pvec[:, O_LBZ + d * 16 + 8: O_LBZ + d * 16 + 16]
                    S.op("dve", lambda E, lb_ap=lb_ap, z0=z0, z1=z1: E.tensor_tensor(out=lb_ap, in0=z1, in1=z0, op=ALU.subtract), reads=[pvec], writes=[lbv])
                    S.op("act", lambda E, lb_ap=lb_ap: E.activation(out=lb_ap, in_=lb_ap, func=AF.Sigmoid), reads=[lbv], writes=[lbv])
                a_ap = lbv[:, d, 1, :]
                sm_ap = lbv[:, d, 2, :]
                S.op("dve", lambda E, lb_ap=lb_ap, a_ap=a_ap: E.tensor_scalar(out=a_ap, in0=lb_ap, scalar1=-1.0, scalar2=1.0, op0=ALU.mult, op1=ALU.add), reads=[lbv], writes=[lbv])
                S.op("dve", lambda E, lb_ap=lb_ap, sm_ap=sm_ap: E.tensor_scalar(out=sm_ap, in0=lb_ap, scalar1=-1.0, scalar2=G_MIN, op0=ALU.mult, op1=ALU.add), reads=[lbv], writes=[lbv])
                S.op("dve", lambda E, a_ap=a_ap: E.reciprocal(out=rcp[:, :], in_=a_ap), reads=[lbv], writes=[rcp])
                S.op("dve", lambda E, sm_ap=sm_ap: E.tensor_tensor(out=sm_ap, in0=sm_ap, in1=rcp[:, :], op=ALU.mult), reads=[lbv, rcp], writes=[lbv])

            with contextlib.ExitStack() as ph:
                csb = S.sbuf("csb", [128, KD, 2], F32, ph)
                scb = S.sbuf("scb", [128, KD, 2], F32, ph)
                wm = [S.sbuf("wm%d" % i, [128, KD, 512], F32, ph) for i in range(2)]
                S.dma("sp", csb[:, :, :], cvec, writes=[csb], sembuf=csb)
                S.op("act", lambda E: E.activation(out=scb[:, :, :], in_=csb[:, :, :], func=AF.Silu), reads=[csb], writes=[scb])
                pm = PS[0]
                for piece in range(12):
                    buf = wm[piece % 2]
                    S.dma("sp", buf[:, :, :], w_mod[l, :, piece * 512:(piece + 1) * 512].rearrange("(k p) c -> p k c", p=128), writes=[buf], sembuf=buf)
                    for mm in range(4):
                        m = piece * 4 + mm
                        for k in range(KD):
                            S.op("pe", lambda E, buf=buf, k=k, mm=mm, m=m: E.matmul(pm[:, m * 2:m * 2 + 2], lhsT=buf[:, k, mm * 128:(mm + 1) * 128], rhs=scb[:, k, :], start=(k == 0), stop=(k == KD - 1)),
                                 reads=[buf, scb], writes=[pm], acc=True)
                S.op("dve", lambda E: E.tensor_tensor(out=modv[:, :, :], in0=pm[:, 0:96].rearrange("p (m s) -> p m s", s=2),
                                                      in1=pvec[:, pb + O_BMOD: pb + O_BMOD + 48].unsqueeze(2).to_broadcast([128, 48, 2]), op=ALU.add),
                     reads=[pm, pvec], writes=[modv])
                for (Ax, off, onw) in ((A1, 8, O_N1W), (A2, 32, O_N2W)):
                    S.op("dve", lambda E, Ax=Ax, off=off, onw=onw: E.scalar_tensor_tensor(out=Ax[:, :, :], in0=modv[:, off:off + 8, :], scalar=1.0,
                                                                                          in1=pvec[:, pb + onw: pb + onw + 8].unsqueeze(2).to_broadcast([128, 8, 2]),
                                                                                          op0=ALU.add, op1=ALU.mult), reads=[modv, pvec], writes=[Ax])
                S.barrier()
                S.emit()
        SH1 = lambda k, s: modv[:, 0 + k, s:s + 1]
        G1 = lambda k, s: modv[:, 16 + k, s:s + 1]
        SH2 = lambda k, s: modv[:, 24 + k, s:s + 1]
        G2 = lambda k, s: modv[:, 40 + k, s:s + 1]

        def wload(dst, src_ap):
            S.dma("pool", dst[:, :, :], src_ap.rearrange("(k p) c -> p k c", p=128), writes=[dst], sembuf=dst)

        def proj(ps_ap, psbuf, wbuf, wap_fn, rbuf, rap_fn, nk=KD):
            for k in range(nk):
                S.op("pe", lambda E, k=k: E.matmul(ps_ap, lhsT=wap_fn(k), rhs=rap_fn(k), start=(k == 0), stop=(k == nk - 1)),
                     reads=[wbuf, rbuf], writes=[psbuf], acc=True)

        class NormTmp:
            def __init__(self, ph, n=512, nbuf=1):
                self.xbs = [S.sbuf("n_xb", [128, KD, n], F32, ph) for _ in range(nbuf)]
                self.sqs = [S.sbuf("n_sq", [128, KD, n], F32, ph) for _ in range(nbuf)]
                self.cnt = 0
                self.xb = self.xbs[0]
                self.sq = self.sqs[0]
                self.rstd = S.sbuf("n_rstd", [128, n], F32, ph)
                self.tmp = [S.sbuf("n_tmp%d" % i, [128, n], F32, ph) for i in range(2)]

            def rotate(self):
                self.cnt += 1
                self.xb = self.xbs[self.cnt % len(self.xbs)]
                self.sq = self.sqs[self.cnt % len(self.sqs)]

        def norm_block(nt, src_ap, srcbuf, n, Afn, SHfn, outbuf, out_fn):
            nt.rotate()
            xb, sq = nt.xb, nt.sq
            S.dma("sp", xb[:, :, 0:n], src_ap, reads=[srcbuf], writes=[xb], sembuf=xb)
            S.op("act", lambda E: E.activation(out=sq[:, :, 0:n], in_=xb[:, :, 0:n], func=AF.Square), reads=[xb], writes=[sq])
            ps = nps()
            for k in range(KD):
                S.op("pe", lambda E, k=k: E.matmul(ps[:, 0:n], lhsT=onesD, rhs=sq[:, k, 0:n], start=(k == 0), stop=(k == KD - 1)),
                     reads=[cst, sq], writes=[ps], acc=True)
            S.op("act", lambda E: E.activation(out=nt.rstd[:, 0:n], in_=ps[:, 0:n], func=AF.Ln, bias=epsc[:, 0:1], scale=1.0), reads=[ps, epsc], writes=[nt.rstd])
            S.op("act", lambda E: E.activation(out=nt.rstd[:, 0:n], in_=nt.rstd[:, 0:n], func=AF.Exp, scale=-0.5), reads=[nt.rstd], writes=[nt.rstd])
            for k in range(KD):
                tmp = nt.tmp[k % 2]
                S.op("dve", lambda E, k=k, tmp=tmp: E.scalar_tensor_tensor(out=tmp[:, 0:n], in0=xb[:, k, 0:n], scalar=Afn(k), in1=nt.rstd[:, 0:n], op0=ALU.mult, op1=ALU.mult),
                     reads=[xb, nt.rstd, A1, A2, pvec], writes=[tmp])
                sh = SHfn(k)
                if sh is None:
                    S.op("act", lambda E, k=k, tmp=tmp: E.activation(out=out_fn(k), in_=tmp[:, 0:n], func=AF.Copy), reads=[tmp], writes=[outbuf])
                else:
                    S.op("act", lambda E, k=k, tmp=tmp, sh=sh: E.activation(out=out_fn(k), in_=tmp[:, 0:n], func=AF.Identity, bias=sh, scale=1.0),
                         reads=[tmp, modv], writes=[outbuf])


        def run_pass(kind, l, i, xT, cxT, DRX, DRC, x_dst, x_dst_buf, ctx_stream, is_last, valid_src, fold_src):
            pb = l * PL
            st_in = ST
            ctx_needed = not (l == 0 and i > 0)
            S.dma("sp", foldm[:, :, :], fold_src, writes=[foldm], sembuf=foldm)
            S.dma("pool", valid[:, :], valid_src, writes=[valid], sembuf=valid)
            cx_out = CX1
            mixer = contextlib.ExitStack()
            hT = S.sbuf("hT", [128, KD, W], BF16, mixer)
            hTc = S.sbuf("hTc", [128, KD, CTX], BF16, mixer)
            with contextlib.ExitStack() as ph:
                nt = NormTmp(ph, nbuf=2)
                for (b0, b1) in blocks(0, W, 512):
                    norm_block(nt, xT[:, :, b0:b1].rearrange("k p t -> p k t"), DRX, b1 - b0,
                               lambda k: A1[:, k, 0:1], lambda k: SH1(k, 0), hT, lambda k, b0=b0, b1=b1: hT[:, k, b0:b1])
                if kind == "B" and ctx_needed:
                    norm_block(nt, cxT.rearrange("k p t -> p k t"), DRC, CTX,
                               lambda k: A1[:, k, 1:2], lambda k: SH1(k, 1), hTc, lambda k: hTc[:, k, 0:CTX])
                S.barrier()
                S.emit()

            hg = contextlib.ExitStack()
            full = (kind == "B")
            NW = 5
            wh = [S.sbuf("wh%d" % i, [128, KD, NW * 128], BF16, hg) for i in range(2)]
            nchw = W // 64
            kT = {d: S.sbuf("kT%d" % d, [64, nchw, 128], BF16, hg) for d in range(2)}
            vT = S.sbuf("vT", [64, nchw, 128], BF16, hg)
            vT2 = S.sbuf("vT2", [32, nchw, 128], BF16, hg)
            stt_ = {d: S.sbuf("st%d" % d, [128, nchw], F32, hg) for d in range(2)}
            if full:
                qt = {d: S.sbuf("qt%d" % d, [128, W], BF16, hg) for d in range(2)}
                kt = {d: S.sbuf("kt%d" % d, [128, W], BF16, hg) for d in range(2)}
                srt = {d: S.sbuf("sr%d" % d, [128, nchw], F32, hg) for d in range(2)}
                of = S.sbuf("of", [128, W], F32, hg)
                ob2 = S.sbuf("ob2", [128, W], F32, hg)
                ob = kt[0]
                stin_r = [S.sbuf("stin0", [128, NSEG, 2, 128], F32, hg)] * 2
                stA = S.sbuf("stA", [128, NSEG, 16], F32, hg)
                S.dma("sp", stA[:, :, :], st_in[:, :, 2048:2064].rearrange("s p c -> p s c"), reads=[DR["ST"]], writes=[stA], sembuf=stA)
                Sbf = {d: [S.sbuf("Sbf%d_%d" % (d, i), [128, 128], BF16, hg) for i in range(2)] for d in range(2)}
                atall = {d: S.sbuf("atall%d" % d, [32, nchw, 96], BF16, hg) for d in range(2)}
                alpha = S.sbuf("alpha", [128, 2], F32, hg)
            Sst = {d: S.sbuf("Sst%d" % d, [128, 128], F32, hg) for d in range(2)}
            stpack = S.sbuf("stpack", [128, NST], F32, hg) if not full else None
            nslots = 1 if full else 2
            tpd = {(sl, d): {n: S.sbuf("t%d%d_%s" % (sl, d, n), [128, 512], F32, hg) for n in ("s", "sn", "c", "D", "dq", "e2", "e3")} for d in range(2) for sl in range(nslots)}
            tp = tpd[(0, 0)]
            tkbd = {(sl, d): S.sbuf("t_kb%d%d" % (sl, d), [128, 512], BF16, hg) for d in range(2) for sl in range(nslots)}
            tvbs = [S.sbuf("t_vb%d" % sl, [128, 512], BF16, hg) for sl in range(nslots)]

            def drive(gens):
                gens = list(gens)
                while gens:
                    for g_ in list(gens):
                        try:
                            next(g_)
                        except StopIteration:
                            gens.remove(g_)


            def tr32(src, dst, c0, nch, dst2=None):
                for j in range(nch):
                    S.op("pe", lambda E, j=j: E.transpose(PSB[0:64, j * 128:(j + 1) * 128], src[:, j * 64:(j + 1) * 64], identb[:, :]),
                         reads=[src, identb], writes=[PSB], acc=True)
                S.op("act", lambda E: E.activation(out=dst[:, c0:c0 + nch, :], in_=PSB[0:64, 0:nch * 128].rearrange("p (c d) -> p c d", d=128), func=AF.Copy),
                     reads=[PSB], writes=[dst])
                if dst2 is not None:
                    for j in range(nch):
                        S.op("pe", lambda E, j=j: E.transpose(PSB[0:32, j * 128:(j + 1) * 128], src[:, j * 64 + 32:(j + 1) * 64], identb[:, :]),
                             reads=[src, identb], writes=[PSB], acc=True)
                    S.op("act", lambda E: E.activation(out=dst2[:, c0:c0 + nch, :], in_=PSB[0:32, 0:nch * 128].rearrange("p (c d) -> p c d", d=128), func=AF.Copy),
                         reads=[PSB], writes=[dst2])

            def hg_gates(h, wb, hsrc, hbuf, t0, n, vmask, vbuf, want_q, dirs=(0, 1), slot=0, run=True):
                nch = n // 64
                c0 = t0 // 64
                v3 = lambda ap: ap.rearrange("p (c t) -> p c t", t=64)
                tvb = tvbs[slot]

                def vunit():
                    ps = nps()
                    proj(ps[:, 0:n], ps, wb, lambda k: wb[:, k, 3 * 128:4 * 128], hbuf, lambda k: hsrc[:, k, t0:t0 + n])
                    yield
                    S.op("act", lambda E: E.activation(out=tvb[:, 0:n], in_=ps[:, 0:n], func=AF.Copy), reads=[ps], writes=[tvb])
                    yield
                    yield
                    tr32(tvb, vT, c0, nch, vT2 if want_q else None)
                if want_q:
                    psq = nps()
                    proj(psq[:, 0:n], psq, wb, lambda k: wb[:, k, 0:128], hbuf, lambda k: hsrc[:, k, t0:t0 + n])
                def unit(d):
                    lb = lbv[:, d, 0, h:h + 1]
                    a = lbv[:, d, 1, h:h + 1]
                    smin = lbv[:, d, 2, h:h + 1]
                    psf = nps()
                    proj(psf[:, 0:n], psf, wb, lambda k, d=d: wb[:, k, (1 + d) * 128:(2 + d) * 128], hbuf, lambda k: hsrc[:, k, t0:t0 + n])
                    yield
                    T = tpd[(slot, d)]
                    s_, sn, cc, DD, dq, e2, e3 = T["s"], T["sn"], T["c"], T["D"], T["dq"], T["e2"], T["e3"]
                    lg, kk, e1 = s_, sn, dq
                    tkb_ = tkbd[(slot, d)]
                    S.op("act", lambda E: E.activation(out=s_[:, 0:n], in_=psf[:, 0:n], func=AF.Sigmoid), reads=[psf], writes=[s_])
                    S.op("act", lambda E: E.activation(out=sn[:, 0:n], in_=psf[:, 0:n], func=AF.Sigmoid, scale=-1.0), reads=[psf], writes=[sn])
                    yield
                    S.op("dve", lambda E, smin=smin: E.tensor_scalar(out=s_[:, 0:n], in0=s_[:, 0:n], scalar1=smin, scalar2=None, op0=ALU.max), reads=[s_, lbv], writes=[s_])
                    S.op("dve", lambda E, a=a: E.scalar_tensor_tensor(out=kk[:, 0:n], in0=sn[:, 0:n], scalar=a, in1=vmask, op0=ALU.mult, op1=ALU.mult),
                         reads=[sn, lbv, vbuf], writes=[kk])
                    yield
                    S.op("act", lambda E, a=a, lb=lb: E.activation(out=lg[:, 0:n], in_=s_[:, 0:n], func=AF.Ln, scale=a, bias=lb), reads=[s_, lbv], writes=[lg])
                    yield
                    S.op("pool", lambda E: E.tensor_tensor(out=lg[:, 0:n], in0=lg[:, 0:n], in1=vmask, op=ALU.mult), reads=[lg, vbuf], writes=[lg])
                    yield
                    S.op("dve", lambda E: E.tensor_tensor_scan(out=cc[:, 0:n], data0=rst[:, 0:n], data1=lg[:, 0:n], initial=0.0, op0=ALU.mult, op1=ALU.add),
                         reads=[rst, lg], writes=[cc])
                    yield
                    c63 = v3(cc[:, 0:n])[:, :, 63:64]
                    S.op("act", lambda E, d=d: E.activation(out=stt_[d][:, c0:c0 + nch], in_=v3(cc[:, 0:n])[:, :, 63], func=AF.Exp), reads=[cc], writes=[stt_[d]])
                    if d == 0:
                        Dv = cc
                    else:
                        S.op("dve", lambda E: E.tensor_tensor(out=DD[:, 0:n], in0=lg[:, 0:n], in1=cc[:, 0:n], op=ALU.subtract), reads=[lg, cc], writes=[DD])
                        S.op("dve", lambda E, c63=c63: E.tensor_tensor(out=v3(DD[:, 0:n]), in0=v3(DD[:, 0:n]), in1=c63.to_broadcast([128, nch, 64]), op=ALU.add),
                             reads=[DD, cc], writes=[DD])
                        Dv = DD
                    yield
                    S.op("dve", lambda E, c63=c63, Dv=Dv: E.tensor_tensor(out=v3(e3[:, 0:n]), in0=v3(Dv[:, 0:n]), in1=c63.to_broadcast([128, nch, 64]), op=ALU.subtract),
                         reads=[Dv, cc], writes=[e3])
                    if want_q:
                        dref = v3(Dv[:, 0:n])[:, :, 31:32]
                        S.op("dve", lambda E, Dv=Dv, dref=dref: E.tensor_tensor(out=v3(dq[:, 0:n]), in0=v3(Dv[:, 0:n]), in1=dref.to_broadcast([128, nch, 64]), op=ALU.subtract),
                             reads=[Dv], writes=[dq])
                    yield
                    S.op("act", lambda E: E.activation(out=e3[:, 0:n], in_=e3[:, 0:n], func=AF.Exp, scale=-1.0), reads=[e3], writes=[e3])
                    if want_q:
                        S.op("act", lambda E, d=d, Dv=Dv: E.activation(out=srt[d][:, c0:c0 + nch], in_=v3(Dv[:, 0:n])[:, :, 31], func=AF.Exp), reads=[Dv], writes=[srt[d]])
                        S.op("act", lambda E: E.activation(out=e2[:, 0:n], in_=dq[:, 0:n], func=AF.Exp, scale=-1.0), reads=[dq], writes=[e2])
                        S.op("act", lambda E: E.activation(out=e1[:, 0:n], in_=dq[:, 0:n], func=AF.Exp), reads=[dq], writes=[e1])
                    yield
                    S.op("pool", lambda E: E.tensor_tensor(out=tkb_[:, 0:n], in0=kk[:, 0:n], in1=e3[:, 0:n], op=ALU.mult), reads=[kk, e3], writes=[tkb_])
                    if want_q:
                        S.op("dve", lambda E, d=d: E.tensor_tensor(out=qt[d][:, t0:t0 + n], in0=psq[:, 0:n], in1=e1[:, 0:n], op=ALU.mult), reads=[psq, e1], writes=[qt[d]])
                        S.op("pool", lambda E, d=d: E.tensor_tensor(out=kt[d][:, t0:t0 + n], in0=kk[:, 0:n], in1=e2[:, 0:n], op=ALU.mult), reads=[kk, e2], writes=[kt[d]])
                    yield
                    tr32(tkb_, kT[d], c0, nch)

                gens = [unit(d) for d in dirs] + [vunit()]
                if not run:
                    return gens
                drive(gens)

            def hg_scan(chunks_f, chunks_b, with_out, obase, snap=None):
                nsteps = max(len(chunks_f), len(chunks_b))
                POB = {0: [PS[0], PS[6]], 1: [PS[1], PS[3]]}
                PA = {0: PS[2], 1: PS[3]}
                PP = {0: PS[4], 1: PS[5]}
                masks = {0: maskFb, 1: maskBb}
                pending = {0: [], 1: []}

                def flush(d):
                    if not pending[d]:
                        return
                    cl = sorted(pending[d])
                    ta, tb = cl[0] * 64, (cl[-1] + 1) * 64
                    pa = ((cl[0] * 64) % 512)
                    n = tb - ta
                    pob = POB[d][(ta // 512) % 2]
                    dst_ = of if d == 0 else ob2
                    S.op("act", lambda E: E.activation(out=dst_[:, ta:tb], in_=pob[:, pa:pa + n], func=AF.Copy), reads=[pob], writes=[dst_])
                    pending[d] = []

                if with_out:
                    PAr = [PS[2], PS[3], PS[4], PS[5]]
                    pai = 0
                    for i in range(nsteps):
                        for d, chl in ((0, chunks_f), (1, chunks_b)):
                            if i >= len(chl):
                                continue
                            c = chl[i]
                            tsl = slice(c * 64, (c + 1) * 64)
                            h1 = slice(c * 64, c * 64 + 32)
                            h2 = slice(c * 64 + 32, c * 64 + 64)
                            ka, kb_ = (h1, h2) if d == 0 else (h2, h1)
                            pa_ = PAr[pai % 4]
                            pai += 1
                            S.op("pe", lambda E, d=d, tsl=tsl, ka=ka, pa_=pa_: E.matmul(pa_[0:32, 0:64], lhsT=kt[d][:, ka], rhs=qt[d][:, tsl], start=True, stop=True),
                                 reads=[kt[d], qt[d]], writes=[pa_])
                            S.op("pe", lambda E, d=d, kb_=kb_, pa_=pa_: E.matmul(pa_[0:32, 64:96], lhsT=kt[d][:, kb_], rhs=qt[d][:, kb_], start=True, stop=True),
                                 reads=[kt[d], qt[d]], writes=[pa_], acc=True)
                            S.op("dve", lambda E, d=d, c=c, pa_=pa_: E.tensor_tensor(out=atall[d][:, c, :], in0=pa_[0:32, 0:96], in1=masks[d][:, :], op=ALU.mult),
                                 reads=[pa_, masks[d]], writes=[atall[d]])
                for i in range(nsteps):
                    for d, chl in ((0, chunks_f), (1, chunks_b)):
                        if i >= len(chl):
                            continue
                        c = chl[i]
                        tsl = slice(c * 64, (c + 1) * 64)
                        if with_out:
                            sb = Sbf[d][i % 2]
                            S.op("dve", lambda E, d=d, sb=sb, c=c: E.tensor_scalar(out=sb[:, :], in0=Sst[d][:, :], scalar1=srt[d][:, c:c + 1], scalar2=None, op0=ALU.mult),
                                 reads=[Sst[d], srt[d]], writes=[sb])
                            va = vT[0:32, c, :] if d == 0 else vT2[:, c, :]
                            vb = vT2[:, c, :] if d == 0 else vT[0:32, c, :]
                            po = (c * 64) % 512
                            if pending[d] and (c * 64) // 512 != (pending[d][0] * 64) // 512:
                                flush(d)
                            po2 = po + 32 if d == 0 else po
                            pob = POB[d][((c * 64) // 512) % 2]
                            S.op("pe", lambda E, d=d, sb=sb, tsl=tsl, po=po, pob=pob: E.matmul(pob[:, po:po + 64], lhsT=sb[:, :], rhs=qt[d][:, tsl], start=True, stop=False),
                                 reads=[sb, qt[d]], writes=[pob], acc=True)
                            S.op("pe", lambda E, d=d, c=c, va=va, po=po, pob=pob: E.matmul(pob[:, po:po + 64], lhsT=va, rhs=atall[d][:, c, 0:64], start=False, stop=False),
                                 reads=[vT, vT2, atall[d]], writes=[pob], acc=True)
                            S.op("pe", lambda E, d=d, c=c, vb=vb, po2=po2, pob=pob: E.matmul(pob[:, po2:po2 + 32], lhsT=vb, rhs=atall[d][:, c, 64:96], start=False, stop=True),
                                 reads=[vT, vT2, atall[d]], writes=[pob], acc=True)
                            pending[d].append(c)
                        S.op("pe", lambda E, d=d, c=c: E.matmul(PP[d][:, 0:128], lhsT=kT[d][:, c, :], rhs=vT[:, c, :], start=True, stop=True),
                             reads=[kT[d], vT], writes=[PP[d]])
                        S.op("dve", lambda E, d=d, c=c: E.scalar_tensor_tensor(out=Sst[d][:, :], in0=Sst[d][:, :], scalar=stt_[d][:, c:c + 1], in1=PP[d][:, 0:128], op0=ALU.mult, op1=ALU.add),
                             reads=[Sst[d], stt_[d], PP[d]], writes=[Sst[d]])
                        if snap is not None and d == 0 and c == snap[0]:
                            S.op("act", lambda E: E.activation(out=snap[1], in_=Sst[0][:, :], func=AF.Copy), reads=[Sst[0]], writes=[carry])
                if with_out:
                    flush(0)
                    flush(1)

            def gnorm_store(h, lo, hi, dst_d, dstbuf, wb, hsrc, hbuf):
                gnw = pvec[:, pb + O_GNW: pb + O_GNW + 1]
                for (b0, b1) in blocks(lo, hi, 512):
                    n = b1 - b0
                    osq, orn, sgt = tp["s"], tp["sn"], tp["c"]
                    psg = nps()
                    proj(psg[:, 0:n], psg, wb, lambda k: wb[:, k, 4 * 128:5 * 128], hbuf, lambda k: hsrc[:, k, b0:b1])
                    S.op("act", lambda E: E.activation(out=sgt[:, 0:n], in_=psg[:, 0:n], func=AF.Silu), reads=[psg], writes=[sgt])
                    S.op("dve", lambda E: E.tensor_tensor(out=of[:, b0:b1], in0=of[:, b0:b1], in1=ob2[:, b0:b1], op=ALU.add), reads=[of, ob2], writes=[of])
                    S.op("act", lambda E: E.activation(out=osq[:, 0:n], in_=of[:, b0:b1], func=AF.Square), reads=[of], writes=[osq])
                    ps = nps()
                    S.op("pe", lambda E: E.matmul(ps[:, 0:n], lhsT=ones128, rhs=osq[:, 0:n], start=True, stop=True), reads=[cst, osq], writes=[ps])
                    S.op("act", lambda E: E.activation(out=orn[:, 0:n], in_=ps[:, 0:n], func=AF.Ln, bias=epsc[:, 0:1], scale=1.0), reads=[ps, epsc], writes=[orn])
                    S.op("act", lambda E: E.activation(out=orn[:, 0:n], in_=orn[:, 0:n], func=AF.Exp, scale=-0.5), reads=[orn], writes=[orn])
                    S.op("dve", lambda E: E.tensor_tensor(out=orn[:, 0:n], in0=orn[:, 0:n], in1=of[:, b0:b1], op=ALU.mult), reads=[orn, of], writes=[orn])
                    S.op("dve", lambda E: E.scalar_tensor_tensor(out=ob[:, b0:b1], in0=orn[:, 0:n], scalar=gnw, in1=sgt[:, 0:n], op0=ALU.mult, op1=ALU.mult),
                         reads=[orn, pvec, sgt], writes=[ob])
                S.dma("sp", dst_d[h, :, lo:hi], ob[:, lo:hi], reads=[ob], writes=[dstbuf], sembuf=ob)

            def load_head_w(hh):
                wb_ = wh[hh % 2]
                for g in range(NW):
                    S.dma("pool", wb_[:, :, g * 128:(g + 1) * 128], w_in[l, :, g * 1024 + hh * 128: g * 1024 + (hh + 1) * 128].rearrange("(k p) c -> p k c", p=128),
                          writes=[wb_], sembuf=wb_)

            load_head_w(0)
            for h in range(8):
                wb = wh[h % 2]
                if h + 1 < 8:
                    load_head_w(h + 1)
                if kind == "A":
                    dirs = [d for d in (0, 1) if (d == 0 and i < NSEG - 1 and l > 0) or (d == 1 and i > 0)]
                    if h == 0:
                        S.op("pool", lambda E: E.memset(stpack[:, :], 0.0), writes=[stpack])
                    blks = blocks(LO, HI, 512)
                    for j in range(0, len(blks), 2):
                        gens = []
                        for sl, (b0, b1) in enumerate(blks[j:j + 2]):
                            gens += hg_gates(h, wb, hT, hT, b0, b1 - b0, valid[:, b0:b1], valid, False, dirs, slot=sl, run=False)
                        drive(gens)
                    for d in dirs:
                        S.op("pool", lambda E, d=d: E.memset(Sst[d][:, :], 0.0), writes=[Sst[d]])
                    hg_scan(list(range(CS, CS + 32)) if 0 in dirs else [], list(range(CE - 1, CE - 33, -1)) if 1 in dirs else [], False, 0)
                    for d in dirs:
                        S.op("act", lambda E, d=d, h=h: E.activation(out=stpack[:, (d * 8 + h) * 128:(d * 8 + h + 1) * 128], in_=Sst[d][:, :], func=AF.Copy), reads=[Sst[d]], writes=[stpack])
                    for d, (ca, cb_) in ((0, (CS, CS + 32)), (1, (CE - 32, CE))):
                        if d not in dirs:
                            continue
                        col = 2048 + d * 8 + h
                        lgt = tp["dq"]
                        S.op("act", lambda E, d=d, ca=ca, cb_=cb_: E.activation(out=lgt[:, 0:32], in_=stt_[d][:, ca:cb_], func=AF.Ln), reads=[stt_[d]], writes=[lgt])
                        S.op("dve", lambda E: E.tensor_reduce(out=lgt[:, 32:33], in_=lgt[:, 0:32], axis=mybir.AxisListType.X, op=ALU.add), reads=[lgt], writes=[lgt])
                        S.op("act", lambda E, col=col: E.activation(out=stpack[:, col:col + 1], in_=lgt[:, 32:33], func=AF.Exp), reads=[lgt], writes=[stpack])
                else:
                    if ctx_needed:
                        hg_gates(h, wb, hTc, hTc, 0, CTX, ones_c[:, 0:CTX], ones_c, ctx_stream)
                        for d in range(2):
                            S.op("pool", lambda E, d=d: E.memset(Sst[d][:, :], 0.0), writes=[Sst[d]])
                        hg_scan([0, 1, 2, 3], [3, 2, 1, 0], ctx_stream, 0)
                        if ctx_stream:
                            gnorm_store(h, 0, CTX, oTc_d, DR["oTc_d"], wb, hTc, hTc)
                        if l == 0:
                            S.dma("sp", SCB[h], Sst[1][:, :], reads=[Sst[1]], writes=[DR["SCB"]], sembuf=Sst[1])
                    else:
                        S.dma("sp", Sst[1][:, :], SCB[h], reads=[DR["SCB"]], writes=[Sst[1]], sembuf=Sst[1])
                    stin = stin_r[h % 2]
                    for d in range(2):
                        S.dma("sp", stin[:, :, d, 0:128], st_in[:, :, (d * 8 + h) * 128:(d * 8 + h + 1) * 128].rearrange("s p c -> p s c"), reads=[DR["ST"]], writes=[stin], sembuf=stin)
                    for d in range(2):
                        if l == 0 and d == 0:
                            if i > 0:
                                S.op("act", lambda E, h=h: E.activation(out=Sst[0][:, :], in_=carry[:, h, :], func=AF.Copy), reads=[carry], writes=[Sst[0]])
                            continue
                        order = range(NSEG) if d == 0 else range(NSEG - 1, -1, -1)
                        for kseg in order:
                            m = foldm[:, d, kseg:kseg + 1]
                            S.op("dve", lambda E, kseg=kseg, d=d, m=m, h=h: E.tensor_scalar(out=alpha[:, 0:1], in0=stA[:, kseg, d * 8 + h:d * 8 + h + 1], scalar1=-1.0, scalar2=m, op0=ALU.add, op1=ALU.mult),
                                 reads=[stA, foldm], writes=[alpha])
                            S.op("dve", lambda E: E.tensor_scalar(out=alpha[:, 0:1], in0=alpha[:, 0:1], scalar1=1.0, scalar2=None, op0=ALU.add), reads=[alpha], writes=[alpha])
                            S.op("dve", lambda E, d=d: E.tensor_scalar(out=Sst[d][:, :], in0=Sst[d][:, :], scalar1=alpha[:, 0:1], scalar2=None, op0=ALU.mult), reads=[Sst[d], alpha], writes=[Sst[d]])
                            S.op("dve", lambda E, d=d, kseg=kseg, m=m, stin=stin: E.scalar_tensor_tensor(out=Sst[d][:, :], in0=stin[:, kseg, d, 0:128], scalar=m, in1=Sst[d][:, :], op0=ALU.mult, op1=ALU.add),
                                 reads=[stin, foldm, Sst[d]], writes=[Sst[d]])
                    for (b0, b1) in blocks(LO, HI, 512):
                        hg_gates(h, wb, hT, hT, b0, b1 - b0, valid[:, b0:b1], valid, True)
                    hg_scan(list(range(CS, CE)), list(range(CE - 1, CS - 1, -1)), True, 0, snap=((CS + 31, carry[:, h, :]) if l == 0 else None))
                    gnorm_store(h, LO, HI, oT_d, DR["oT_d"], wb, hT, hT)
                pass

            if kind == "A":
                S.dma("sp", ST[i], stpack[:, :], reads=[stpack], writes=[DR["ST"]], sembuf=stpack)
                S.barrier()
                S.emit()
                hg.close()
                mixer.close()
                return
            S.barrier()
            S.emit()
            hg.close()
            mixer.close()

            with contextlib.ExitStack() as ph:
                wres = S.sbuf("wres", [128, KD, 4096], BF16, ph)
                for j in range(8):
                    S.dma("pool", wres[:, :, j * 512:(j + 1) * 512], w_in[l, :, 5120 + j * 512: 5120 + (j + 1) * 512].rearrange("(k p) c -> p k c", p=128), writes=[wres], sembuf=wres)
                w3r = [S.sbuf("w3r%d" % i, [128, KD, 128], BF16, ph) for i in range(3)]
                w3i = [0]

                def w3load(src):
                    w3i[0] = (w3i[0] + 1) % 3
                    b = w3r[w3i[0]]
                    S.dma("pool", b[:, :, :], src.rearrange("(k p) c -> p k c", p=128), writes=[b], sembuf=b)
                    return b
                nt = NormTmp(ph)
                hTb = S.sbuf("hTb", [128, KD, 512], BF16, ph)
                ubs = [S.sbuf("ub%d" % q, [128, 512], F32, ph) for q in range(2)]
                sbbs = [S.sbuf("sbb%d" % q, [128, 512], F32, ph) for q in range(2)]
                accs = [S.sbuf("acc%d" % q, [128, 480], F32, ph) for q in range(2)]
                vTb = S.sbuf("vTb", [128, KD, 480], F32, ph)
                mean = S.sbuf("mean", [128, 480], F32, ph)
                var = S.sbuf("var", [128, 480], F32, ph)
                t1 = [S.sbuf("t1_%d" % i, [128, 480], F32, ph) for i in range(2)]
                yc = S.sbuf("yc", [128, KD, 480], BF16, ph)
                yT = S.sbuf("yT", [128, KD, 480], BF16, ph)
                oTb = S.sbuf("oTb", [128, 8, 480], BF16, ph)
                sgcs = [S.sbuf("sgc%d" % q, [128, 480], F32, ph) for q in range(2)]
                sghs = [S.sbuf("sgh%d" % q, [128, 480], F32, ph) for q in range(2)]
                m1s = [S.sbuf("m1_0", [128, 480], F32, ph)] * 2
                m2s = [S.sbuf("m2_0", [128, 480], F32, ph)] * 2
                xo = [S.sbuf("xo%d" % i, [128, 480], F32, ph) for i in range(2)]
                cvw = lambda tap, c: pvec[:, pb + O_CVW + tap * 8 + c: pb + O_CVW + tap * 8 + c + 1]

                def merge_seq(xsrc, lo, hi, srclo, srchi, s, oTsrc, oTbuf, vmask_buf, dst, dstbuf):
                    for (b0, b1) in blocks(lo, hi, 480):
                        n = b1 - b0
                        a0, a1 = max(srclo, b0 - 15), min(srchi, b1 + 15)
                        m = a1 - a0
                        off = a0 - (b0 - 15)
                        ctr = b0 - a0
                        norm_block(nt, xsrc[:, :, a0:a1].rearrange("k p t -> p k t"), DRX, m,
                                   lambda k: A1[:, k, s:s + 1], lambda k: SH1(k, s), hTb, lambda k: hTb[:, k, 0:m])
                        S.dma("sp", oTb[:, :, 0:n], oTsrc[:, :, b0:b1].rearrange("h p t -> p h t"), reads=[oTbuf], writes=[oTb], sembuf=oTb)
                        def cunit(c, slot):
                            ub_, sbb_, acc_ = ubs[slot], sbbs[slot], accs[slot]
                            psa = nps()
                            proj(psa[:, 0:m], psa, wres, lambda k, c=c: wres[:, k, c * 128:(c + 1) * 128], hTb, lambda k: hTb[:, k, 0:m])
                            psb_ = nps()
                            proj(psb_[:, 0:m], psb_, wres, lambda k, c=c: wres[:, k, 1024 + c * 128:1024 + (c + 1) * 128], hTb, lambda k: hTb[:, k, 0:m])
                            yield
                            S.op("act", lambda E: E.activation(out=sbb_[:, 0:m], in_=psb_[:, 0:m], func=AF.Sigmoid), reads=[psb_], writes=[sbb_])
                            yield
                            S.op("pool", lambda E: E.tensor_tensor(out=sbb_[:, 0:m], in0=sbb_[:, 0:m], in1=vmask_buf[:, a0:a1], op=ALU.mult), reads=[sbb_, vmask_buf], writes=[sbb_])
                            if off > 0 or m < n + 30:
                                S.op("pool", lambda E: E.memset(ub_[:, :], 0.0), writes=[ub_])
                            yield
                            S.op("dve", lambda E: E.tensor_tensor(out=ub_[:, off:off + m], in0=psa[:, 0:m], in1=sbb_[:, 0:m], op=ALU.mult), reads=[psa, sbb_], writes=[ub_])
                            yield
                            S.op("dve", lambda E, c=c: E.tensor_scalar(out=acc_[:, 0:n], in0=ub_[:, 0:n], scalar1=cvw(0, c), scalar2=pvec[:, pb + O_CVB + c: pb + O_CVB + c + 1], op0=ALU.mult, op1=ALU.add),
                                 reads=[ub_, pvec], writes=[acc_])
                            for tap in range(1, 31):
                                yield
                                S.op("dve", lambda E, c=c, tap=tap: E.scalar_tensor_tensor(out=acc_[:, 0:n], in0=ub_[:, tap:tap + n], scalar=cvw(tap, c), in1=acc_[:, 0:n], op0=ALU.mult, op1=ALU.add),
                                     reads=[ub_, pvec, acc_], writes=[acc_])
                            yield
                            S.op("act", lambda E, c=c: E.activation(out=vTb[:, c, 0:n], in_=acc_[:, 0:n], func=AF.Copy), reads=[acc_], writes=[vTb])

                        for c0_ in range(0, KD, 2):
                            gens = [cunit(c0_, 0), cunit(c0_ + 1, 1)]
                            while gens:
                                for g_ in list(gens):
                                    try:
                                        next(g_)
                                    except StopIteration:
                                        gens.remove(g_)
                        S.op("act", lambda E: E.activation(out=nt.sq[:, :, 0:n], in_=vTb[:, :, 0:n], func=AF.Square), reads=[vTb], writes=[nt.sq])
                        psm, psq = nps(), nps()
                        for k in range(KD):
                            S.op("pe", lambda E, k=k: E.matmul(psm[:, 0:n], lhsT=onesD, rhs=vTb[:, k, 0:n], start=(k == 0), stop=(k == KD - 1)), reads=[cst, vTb], writes=[psm], acc=True)
                        for k in range(KD):
                            S.op("pe", lambda E, k=k: E.matmul(psq[:, 0:n], lhsT=onesD, rhs=nt.sq[:, k, 0:n], start=(k == 0), stop=(k == KD - 1)), reads=[cst, nt.sq], writes=[psq], acc=True)
                        S.op("act", lambda E: E.activation(out=mean[:, 0:n], in_=psm[:, 0:n], func=AF.Copy), reads=[psm], writes=[mean])
                        S.op("dve", lambda E: E.tensor_tensor(out=var[:, 0:n], in0=mean[:, 0:n], in1=mean[:, 0:n], op=ALU.mult), reads=[mean], writes=[var])
                        S.op("dve", lambda E: E.tensor_tensor(out=var[:, 0:n], in0=psq[:, 0:n], in1=var[:, 0:n], op=ALU.subtract), reads=[psq, var], writes=[var])
                        S.op("act", lambda E: E.activation(out=var[:, 0:n], in_=var[:, 0:n], func=AF.Ln, bias=epsc[:, 0:1], scale=1.0), reads=[var, epsc], writes=[var])
                        S.op("act", lambda E: E.activation(out=var[:, 0:n], in_=var[:, 0:n], func=AF.Exp, scale=-0.5), reads=[var], writes=[var])
                        for c in range(KD):
                            tt = t1[c % 2]
                            S.op("dve", lambda E, c=c, tt=tt: E.tensor_tensor(out=tt[:, 0:n], in0=vTb[:, c, 0:n], in1=mean[:, 0:n], op=ALU.subtract), reads=[vTb, mean], writes=[tt])
                            S.op("pool", lambda E, tt=tt: E.tensor_tensor(out=tt[:, 0:n], in0=tt[:, 0:n], in1=var[:, 0:n], op=ALU.mult), reads=[tt, var], writes=[tt])
                            S.op("act", lambda E, c=c, tt=tt: E.activation(out=yc[:, c, 0:n], in_=tt[:, 0:n], func=AF.Silu, scale=pvec[:, pb + O_LNW + c: pb + O_LNW + c + 1], bias=pvec[:, pb + O_LNB + c: pb + O_LNB + c + 1]),
                                 reads=[tt, pvec], writes=[yc])
                        wsrcs = []
                        for dd in range(KD):
                            wsrcs += [w_cv_out[l, :, dd * 128:(dd + 1) * 128], w_hg_out[l, :, dd * 128:(dd + 1) * 128]]
                        for e in range(KD):
                            wsrcs.append(w_out[l, :, e * 128:(e + 1) * 128])
                        wissued = []

                        def wissue():
                            if len(wissued) < len(wsrcs):
                                wissued.append(w3load(wsrcs[len(wissued)]))

                        def wtake(idx):
                            while len(wissued) <= idx:
                                wissue()
                            b_ = wissued[idx]
                            if len(wissued) < idx + 3:
                                wissue()
                            return b_

                        wissue()
                        wissue()
                        for dd in range(KD):
                            sgc, sgh, m1, m2 = sgcs[dd % 2], sghs[dd % 2], m1s[dd % 2], m2s[dd % 2]
                            wcv_ = wtake(2 * dd)
                            ps1 = nps()
                            proj(ps1[:, 0:n], ps1, wcv_, lambda k, wcv_=wcv_: wcv_[:, k, :], yc, lambda k: yc[:, k, 0:n])
                            psg = nps()
                            proj(psg[:, 0:n], psg, wres, lambda k, dd=dd: wres[:, k, 3072 + dd * 128:3072 + (dd + 1) * 128], hTb, lambda k: hTb[:, k, ctr:ctr + n])
                            S.op("act", lambda E: E.activation(out=sgc[:, 0:n], in_=psg[:, 0:n], func=AF.Sigmoid), reads=[psg], writes=[sgc])
                            S.op("dve", lambda E: E.tensor_tensor(out=m1[:, 0:n], in0=ps1[:, 0:n], in1=sgc[:, 0:n], op=ALU.mult), reads=[ps1, sgc], writes=[m1])
                            whg_ = wtake(2 * dd + 1)
                            ps2 = nps()
                            proj(ps2[:, 0:n], ps2, whg_, lambda k, whg_=whg_: whg_[:, k, :], oTb, lambda k: oTb[:, k, 0:n])
                            psh = nps()
                            proj(psh[:, 0:n], psh, wres, lambda k, dd=dd: wres[:, k, 2048 + dd * 128:2048 + (dd + 1) * 128], hTb, lambda k: hTb[:, k, ctr:ctr + n])
                            S.op("act", lambda E: E.activation(out=sgh[:, 0:n], in_=psh[:, 0:n], func=AF.Sigmoid), reads=[psh], writes=[sgh])
                            S.op("dve", lambda E: E.tensor_tensor(out=m2[:, 0:n], in0=ps2[:, 0:n], in1=sgh[:, 0:n], op=ALU.mult), reads=[ps2, sgh], writes=[m2])
                            S.op("pool", lambda E, dd=dd: E.tensor_tensor(out=yT[:, dd, 0:n], in0=m1[:, 0:n], in1=m2[:, 0:n], op=ALU.add), reads=[m1, m2], writes=[yT])
                        for e in range(KD):
                            wo_ = wtake(2 * KD + e)
                            pso = nps()
                            proj(pso[:, 0:n], pso, wo_, lambda k, wo_=wo_: wo_[:, k, :], yT, lambda k: yT[:, k, 0:n])
                            xb_ = xo[e % 2]
                            S.op("dve", lambda E, e=e, xb_=xb_: E.scalar_tensor_tensor(out=xb_[:, 0:n], in0=pso[:, 0:n], scalar=G1(e, s), in1=nt.xb[:, e, ctr:ctr + n], op0=ALU.mult, op1=ALU.add),
                                 reads=[pso, modv, nt.xb], writes=[xb_])
                            S.dma("sp", dst[e, :, b0:b1], xb_[:, 0:n], reads=[xb_], writes=[dstbuf], sembuf=xb_)
                    S.emit()

                merge_seq(xT, LO, HI, 0, W, 0, oT_d, DR["oT_d"], valid, xmid, DR["xmid"])
                if ctx_stream:
                    merge_seq(cxT, 0, CTX, 0, CTX, 1, oTc_d, DR["oTc_d"], ones_c, cxmid, DR["cxmid"])
                S.barrier()
                S.emit()

            NF = HI - LO
            ffn = contextlib.ExitStack()
            h2T = S.sbuf("h2T", [128, KD, NF], BF16, ffn)
            h2Tc = S.sbuf("h2Tc", [128, KD, CTX], BF16, ffn) if ctx_stream else None
            with contextlib.ExitStack() as ph:
                nt = NormTmp(ph, nbuf=2)
                for (b0, b1) in blocks(LO, HI, 512):
                    norm_block(nt, xmid[:, :, b0:b1].rearrange("k p t -> p k t"), DR["xmid"], b1 - b0,
                               lambda k: A2[:, k, 0:1], lambda k: SH2(k, 0), h2T, lambda k, b0=b0, b1=b1: h2T[:, k, b0 - LO:b1 - LO])
                if ctx_stream:
                    norm_block(nt, cxmid.rearrange("k p t -> p k t"), DR["cxmid"], CTX,
                               lambda k: A2[:, k, 1:2], lambda k: SH2(k, 1), h2Tc, lambda k: h2Tc[:, k, 0:CTX])
                S.barrier()
                S.emit()
            zT = S.sbuf("zT", [128, KF, NT], BF16, ffn)
            zTc = S.sbuf("zTc", [128, KF, CTX], BF16, ffn) if ctx_stream else None
            mcb = S.sbuf("mcb", [128, 2, 64], BF16, ffn)
            S.dma("pool", mcb[:, :, :], mcol_d[:, :, 0:64], writes=[mcb], sembuf=mcb)
            with contextlib.ExitStack() as ph:
                uc = [S.sbuf("uc%d" % i, [128, NF + 2], F32, ph) for i in range(3)]
                FH = NT // 2
                fa1s = [S.sbuf("fa1_%d" % q, [128, FH], F32, ph) for q in range(2)]
                fa2s = [S.sbuf("fa2_%d" % q, [128, FH], F32, ph) for q in range(2)]
                wuv = [S.sbuf("wuv%d" % i, [128, KD, 256], BF16, ph) for i in range(2)]
                for i in range(3):
                    S.op("pool", lambda E, i=i: E.memset(uc[i][:, :], 0.0), writes=[uc[i]])
                fw = lambda tap, c: pvec[:, pb + O_FW + tap * KF + c: pb + O_FW + tap * KF + c + 1]
                fb = lambda c: pvec[:, pb + O_FB + c: pb + O_FB + c + 1]

                def conv_taps(taps, n, c, fa1, fa2):
                    for j, (si, st0, tap) in enumerate(taps):
                        if j == 0:
                            S.op("dve", lambda E, si=si, st0=st0, tap=tap: E.tensor_scalar(out=fa1[:, 0:n], in0=uc[si][:, st0:st0 + n], scalar1=fw(tap, c), scalar2=fb(c), op0=ALU.mult, op1=ALU.add),
                                 reads=[uc[si], pvec], writes=[fa1])
                        else:
                            S.op("dve", lambda E, si=si, st0=st0, tap=tap: E.scalar_tensor_tensor(out=fa1[:, 0:n], in0=uc[si][:, st0:st0 + n], scalar=fw(tap, c), in1=fa1[:, 0:n], op0=ALU.mult, op1=ALU.add),
                                 reads=[uc[si], pvec, fa1], writes=[fa1])
                        yield
                    S.op("act", lambda E: E.activation(out=fa2[:, 0:n], in_=fa1[:, 0:n], func=AF.Gelu), reads=[fa1], writes=[fa2])

                def drive(gens):
                    gens = list(gens)
                    while gens:
                        for g_ in list(gens):
                            try:
                                next(g_)
                            except StopIteration:
                                gens.remove(g_)

                cc_ = [0]
                for c in range(KF):
                    cc_[0] = c
                    wb_ = wuv[c % 2]
                    S.dma("pool", wb_[:, :, 0:128], w_up[l, :, c * 128:(c + 1) * 128].rearrange("(k p) c -> p k c", p=128), writes=[wb_], sembuf=wb_)
                    S.dma("pool", wb_[:, :, 128:256], w_up[l, :, FF + c * 128:FF + (c + 1) * 128].rearrange("(k p) c -> p k c", p=128), writes=[wb_], sembuf=wb_)
                    for (i0, i1) in blocks(0, NF, 512):
                        ps = nps()
                        proj(ps[:, 0:i1 - i0], ps, wb_, lambda k, wb_=wb_: wb_[:, k, 0:128], h2T, lambda k, i0=i0, i1=i1: h2T[:, k, i0:i1])
                        S.op("dve", lambda E, ps=ps, i0=i0, i1=i1: E.tensor_tensor(out=uc[0][:, 1 + i0:1 + i1], in0=ps[:, 0:i1 - i0], in1=valid[:, LO + i0:LO + i1], op=ALU.mult),
                             reads=[ps, valid], writes=[uc[0]])
                    for mi in range(2):
                        S.op("dve", lambda E, mi=mi: E.tensor_tensor(out=uc[1 + mi][:, 1:1 + NF].rearrange("p (c t) -> p c t", t=64), in0=uc[0][:, 1:1 + NF].rearrange("p (c t) -> p c t", t=64),
                                                                 in1=mcb[:, mi:mi + 1, :].to_broadcast([128, NF // 64, 64]), op=ALU.mult), reads=[uc[0], mcb], writes=[uc[1 + mi]])
                    taps = []
                    for ky in range(3):
                        for kx in range(3):
                            offs = (ky - 1) * 64 + (kx - 1)
                            si = {0: 1, 1: 0, 2: 2}[kx]
                            taps.append((si, 1 + 64 + offs, ky * 3 + kx))
                    drive([conv_taps([(si, st0 + hb * FH, tap) for (si, st0, tap) in taps], FH, c, fa1s[hb], fa2s[hb]) for hb in range(2)])
                    for hb in range(2):
                        fa2 = fa2s[hb]
                        for (j0, j1) in blocks(hb * FH, (hb + 1) * FH, 512):
                            ps = nps()
                            proj(ps[:, 0:j1 - j0], ps, wb_, lambda k, wb_=wb_: wb_[:, k, 128:256], h2T, lambda k, j0=j0, j1=j1: h2T[:, k, 64 + j0:64 + j1])
                            S.op("dve", lambda E, ps=ps, j0=j0, j1=j1, c=c, hb=hb: E.tensor_tensor(out=zT[:, c, j0:j1], in0=ps[:, 0:j1 - j0], in1=fa2[:, j0 - hb * FH:j1 - hb * FH], op=ALU.mult),
                                 reads=[ps, fa2], writes=[zT])
                    if ctx_stream:
                        ps = nps()
                        proj(ps[:, 0:CTX], ps, wb_, lambda k, wb_=wb_: wb_[:, k, 0:128], h2Tc, lambda k: h2Tc[:, k, 0:CTX])
                        S.op("pool", lambda E: E.memset(uc[0][:, 0:CTX + 2], 0.0), writes=[uc[0]])
                        S.op("act", lambda E, ps=ps: E.activation(out=uc[0][:, 1:1 + CTX], in_=ps[:, 0:CTX], func=AF.Copy), reads=[ps], writes=[uc[0]])
                        drive([conv_taps([(0, 0, 3), (0, 1, 4), (0, 2, 5)], CTX, c, fa1s[0], fa2s[0])])
                        fa2 = fa2s[0]
                        ps = nps()
                        proj(ps[:, 0:CTX], ps, wb_, lambda k, wb_=wb_: wb_[:, k, 128:256], h2Tc, lambda k: h2Tc[:, k, 0:CTX])
                        S.op("dve", lambda E, ps=ps, c=c: E.tensor_tensor(out=zTc[:, c, 0:CTX], in0=ps[:, 0:CTX], in1=fa2[:, 0:CTX], op=ALU.mult), reads=[ps, fa2], writes=[zTc])
                        S.op("pool", lambda E: E.memset(uc[0][:, 0:1], 0.0), writes=[uc[0]])
                    pass
                S.barrier()
                S.emit()
            with contextlib.ExitStack() as ph:
                wd = [S.sbuf("wd%d" % i, [128, KF, 128], BF16, ph) for i in range(2)]
                xr = [S.sbuf("xr%d" % i, [128, 512], F32, ph) for i in range(2)]
                xw = [S.sbuf("xw%d" % i, [128, 512], F32, ph) for i in range(2)]
                cnt = 0
                for e in range(KD):
                    wd_ = wd[e % 2]
                    S.dma("pool", wd_[:, :, :], w_down[l, :, e * 128:(e + 1) * 128].rearrange("(c p) n -> p c n", p=128), writes=[wd_], sembuf=wd_)
                    jobs = [(zT, j0, j1, xmid, DR["xmid"], T0, x_dst, 0) for (j0, j1) in blocks(0, NT, 512)]
                    if ctx_stream:
                        jobs.append((zTc, 0, CTX, cxmid, DR["cxmid"], 0, cx_out, 1))
                    for (zb, j0, j1, xsrc, xsb, xoff, dstap, s) in jobs:
                        n = j1 - j0
                        ps = nps()
                        for c in range(KF):
                            S.op("pe", lambda E, c=c, zb=zb, j0=j0, j1=j1, ps=ps, wd_=wd_, n=n: E.matmul(ps[:, 0:n], lhsT=wd_[:, c, :], rhs=zb[:, c, j0:j1], start=(c == 0), stop=(c == KF - 1)),
                                 reads=[wd_, zb], writes=[ps], acc=True)
                        xr_, xw_ = xr[cnt % 2], xw[cnt % 2]
                        cnt += 1
                        S.dma("sp", xr_[:, 0:n], xsrc[e, :, xoff + j0:xoff + j1], reads=[xsb], writes=[xr_], sembuf=xr_)
                        S.op("dve", lambda E, e=e, s=s, ps=ps, xr_=xr_, xw_=xw_, n=n: E.scalar_tensor_tensor(out=xw_[:, 0:n], in0=ps[:, 0:n], scalar=G2(e, s), in1=xr_[:, 0:n], op0=ALU.mult, op1=ALU.add),
                             reads=[ps, modv, xr_], writes=[xw_])
                        if s == 0 and is_last:
                            S.dma("sp", xnew[e, :, j0:j1], xw_[:, 0:n], reads=[xw_], writes=[DR["xnew"]], sembuf=xw_)
                        else:
                            S.dma("sp", dstap[e, :, j0:j1], xw_[:, 0:n], reads=[xw_], writes=[(x_dst_buf if s == 0 else DR["CX1"])], sembuf=xw_)
                S.barrier()
                S.emit()
            ffn.close()
            if is_last:
                with contextlib.ExitStack() as ph:
                    nt = NormTmp(ph)
                    fo = [S.sbuf("fo%d" % i, [128, KD, 512], F32, ph) for i in range(2)]
                    for bi, (j0, j1) in enumerate(blocks(0, NT, 512)):
                        fo_ = fo[bi % 2]
                        norm_block(nt, xnew[:, :, j0:j1].rearrange("k p t -> p k t"), DR["xnew"], j1 - j0,
                                   lambda k: pvec[:, O_FNW + k:O_FNW + k + 1], lambda k: None, fo_, lambda k, fo_=fo_, j0=j0, j1=j1: fo_[:, k, 0:j1 - j0])
                        S.dma("sp", x_dst[:, :, j0:j1].rearrange("k p t -> p k t"), fo_[:, :, 0:j1 - j0], reads=[fo_], writes=[x_dst_buf], sembuf=fo_)
                    S.barrier()
                    S.emit()
            else:
                S.barrier()
                S.emit()

        DRE = DR["ext"]
        with contextlib.ExitStack() as ph:
            zt = S.sbuf("zt", [128, NST], F32, ph)
            S.op("pool", lambda E: E.memset(zt[:, :], 0.0), writes=[zt])
            for q in range(NSEG):
                S.dma("sp", ST[q], zt[:, :], reads=[zt], writes=[DR["ST"]], sembuf=zt)
            S.barrier()
            S.emit()
        for l in range(DEPTH):
            layer_setup(l)
            for i in range(NSEG):
                if l == 0:
                    if i == 0:
                        continue
                    run_pass("A", l, i, xT4[i], cxT_in, DRE, DRE, None, None, False, False, valid4_d[i], fold4_d[i])
                else:
                    run_pass("A", l, i, XW[i], CX1, DR["XW"], DR["CX1"], None, None, False, False, valid4_d[i], fold4_d[i])
            if l == 0:
                for i in range(NSEG):
                    run_pass("B", l, i, xT4[i], cxT_in, DRE, DRE, X1[i], DR["X1"], i == 0, False, valid4_d[i], fold4_d[i])
                for i in range(NSEG):
                    S.dma("sp", XW[i, :, :, HX:HX + NT], X1[i], reads=[DR["X1"]], writes=[DR["XW"]], sembuf=DR["XW"])
                    S.dma("sp", XW[i, :, :, 0:HX], X1[(i - 1) % NSEG, :, :, NT - HX:NT], reads=[DR["X1"]], writes=[DR["XW"]], sembuf=DR["XW"])
                    S.dma("sp", XW[i, :, :, HX + NT:W], X1[(i + 1) % NSEG, :, :, 0:HX], reads=[DR["X1"]], writes=[DR["XW"]], sembuf=DR["XW"])
                S.barrier()
                S.emit()
            else:
                with contextlib.ExitStack() as ph:
                    oh = S.sbuf("oh", [128, NSEG], F32, ph)
                    S.dma("sp", oh[:, :], onehot_d, writes=[oh], sembuf=oh)
                    xs = [S.sbuf("sel%d" % q, [128, KD, 512], F32, ph) for q in range(NSEG)]
                    acc = [S.sbuf("selacc%d" % q, [128, KD, 512], F32, ph) for q in range(2)]
                    for bi, (b0, b1) in enumerate(blocks(0, W, 512)):
                        n = b1 - b0
                        a_ = acc[bi % 2]
                        for q in range(NSEG):
                            S.dma("sp", xs[q][:, :, 0:n], XW[q, :, :, b0:b1].rearrange("k p t -> p k t"), reads=[DR["XW"]], writes=[xs[q]], sembuf=xs[q])
                        S.op("dve", lambda E, a_=a_, n=n: E.tensor_scalar(out=a_[:, :, 0:n], in0=xs[0][:, :, 0:n], scalar1=oh[:, 0:1], scalar2=None, op0=ALU.mult), reads=[xs[0], oh], writes=[a_])
                        for q in range(1, NSEG):
                            S.op("dve", lambda E, a_=a_, n=n, q=q: E.scalar_tensor_tensor(out=a_[:, :, 0:n], in0=xs[q][:, :, 0:n], scalar=oh[:, q:q + 1], in1=a_[:, :, 0:n], op0=ALU.mult, op1=ALU.add),
                                 reads=[xs[q], oh, a_], writes=[a_])
                        S.dma("sp", XWO[:, :, b0:b1].rearrange("k p t -> p k t"), a_[:, :, 0:n], reads=[a_], writes=[DR["XWO"]], sembuf=a_)
                    S.barrier()
                    S.emit()
                run_pass("B", l, 0, XWO, CX1, DR["XWO"], DR["CX1"], x_out, DR["x_out"], False, True, validown_d, foldown_d)
        S.barrier()
        S.emit(final=True)
    return nc


def _chan(v):
    return np.ascontiguousarray(v.reshape(-1, 128).T)


def pack_pvec(inp):
    pv = np.zeros((128, NPV), np.float32)
    for l in range(DEPTH):
        b = l * PL
        pv[:, b + O_N1W:b + O_N1W + 8] = _chan(inp["norm1_w"][l])
        pv[:, b + O_BMOD:b + O_BMOD + 48] = _chan(inp["b_mod"][l])
        pv[:, b + O_GNW] = inp["hg_gnorm_w"][l]
        for tap in range(31):
            pv[:, b + O_CVW + tap * 8: b + O_CVW + tap * 8 + 8] = _chan(inp["cv_dw_w"][l, tap])
        pv[:, b + O_CVB:b + O_CVB + 8] = _chan(inp["cv_dw_b"][l])
        pv[:, b + O_LNW:b + O_LNW + 8] = _chan(inp["cv_ln_w"][l])
        pv[:, b + O_LNB:b + O_LNB + 8] = _chan(inp["cv_ln_b"][l])
        pv[:, b + O_N2W:b + O_N2W + 8] = _chan(inp["norm2_w"][l])
        fw = inp["ffn_dw_w"][l].reshape(9, FF)
        for tap in range(9):
            pv[:, b + O_FW + tap * KF: b + O_FW + (tap + 1) * KF] = _chan(fw[tap])
        pv[:, b + O_FB:b + O_FB + KF] = _chan(inp["ffn_dw_b"][l])
    for d in range(2):
        for l in range(DEPTH):
            pv[:, O_LBZ + d * 16 + l * 8: O_LBZ + d * 16 + l * 8 + 8] = _chan(inp["hg_lb_logits"][d, l])
    pv[:, O_FNW:O_FNW + 8] = _chan(inp["final_norm_w"])
    return pv


def make_consts():
    cst = np.zeros((128, 6, 128), np.float32)
    cst[:, 0, :] = np.eye(128, dtype=np.float32)
    cst[:, 1, :] = 1.0 / D
    cst[:, 2, :] = 1.0 / 128
    s = np.arange(64)[:, None]
    t = np.arange(64)[None, :]
    mF = (s <= t).astype(np.float32)
    mB = (s >= t).astype(np.float32)
    cst[0:32, 3, 0:64] = mF[0:32, :]
    cst[0:32, 3, 64:96] = mF[32:64, 32:64]
    cst[0:32, 4, 0:64] = mB[32:64, :]
    cst[0:32, 4, 64:96] = mB[0:32, 0:32]
    rst = np.ones((128, 512), np.float32)
    rst[:, ::64] = 0.0
    mcol = np.ones((128, 2, W), np.float32)
    pos = np.arange(W)
    mcol[:, 0, pos % 64 == 63] = 0.0
    mcol[:, 1, pos % 64 == 0] = 0.0
    return cst, rst, mcol


def window_T(xfull_b, seg):
    t0 = seg * NT - HX
    w = np.zeros((W, D), np.float32)
    a, b = max(t0, 0), min(t0 + W, SEQ)
    w[a - t0:b - t0] = xfull_b[a:b]
    return np.ascontiguousarray(w.T.reshape(KD, 128, W))


def core_static(inp, c):
    b, seg = c // NSEG, c % NSEG
    t0 = seg * NT - HX
    pos = np.arange(W) + t0
    valid = np.broadcast_to(((pos >= 0) & (pos < SEQ)).astype(np.float32), (128, W)).copy()
    fold = np.zeros((128, 2, NSEG), np.float32)
    for k in range(NSEG):
        fold[:, 0, k] = 1.0 if k < seg else 0.0
        fold[:, 1, k] = 1.0 if k > seg else 0.0
    cvec = np.stack([_chan(inp["c"][b]), _chan(inp["c_ctx"])], axis=-1)
    return dict(valid=valid, foldm=fold, cvec=np.ascontiguousarray(cvec))


_NC_CACHE = {}


def seg_static(seg):
    t0 = seg * NT - HX
    pos = np.arange(W) + t0
    valid = np.broadcast_to(((pos >= 0) & (pos < SEQ)).astype(np.float32), (128, W)).copy()
    fold = np.zeros((128, 2, NSEG), np.float32)
    for k in range(NSEG):
        fold[:, 0, k] = 1.0 if k < seg else 0.0
        fold[:, 1, k] = 1.0 if k > seg else 0.0
    return valid, fold


def kernel(**inp):
    inp = {k: np.asarray(v) for k, v in inp.items()}
    pv = pack_pvec(inp)
    cst, rst, mcol = make_consts()
    x = np.ascontiguousarray(inp["x"], dtype=np.float32)
    ctx = np.ascontiguousarray(inp["ctx"], dtype=np.float32)
    segs = [seg_static(s) for s in range(NSEG)]
    valid4 = np.ascontiguousarray(np.stack([v for v, _ in segs], axis=0))
    fold4 = np.ascontiguousarray(np.stack([f for _, f in segs], axis=0))
    xT4 = [np.ascontiguousarray(np.stack([window_T(x[b], s) for s in range(NSEG)], axis=0)) for b in range(BATCH)]
    cxT = [np.ascontiguousarray(ctx[b].T.reshape(KD, 128, CTX)) for b in range(BATCH)]
    maps = []
    for c in range(NCORE):
        b, seg = c // NSEG, c % NSEG
        oh = np.zeros((128, NSEG), np.float32)
        oh[:, seg] = 1.0
        maps.append(dict(
            xT4=xT4[b], cxT=cxT[b], cvec=np.ascontiguousarray(np.stack([_chan(inp["c"][b]), _chan(inp["c_ctx"])], axis=-1)),
            valid4=valid4, fold4=fold4, validown=segs[seg][0], foldown=segs[seg][1], onehot=oh,
            pvec=pv, cst=cst, rst=rst, mcol=mcol, w_mod=inp["w_mod"], w_in=inp["w_in"],
            w_hg_out=inp["w_hg_out"], w_cv_out=inp["w_cv_out"], w_out=inp["w_out"], w_up=inp["w_up"], w_down=inp["w_down"]))
    if "nc" not in _NC_CACHE:
        _NC_CACHE["nc"] = build()
    res = run_bass_kernel_spmd(_NC_CACHE["nc"], maps, core_ids=list(range(NCORE)))
    out = np.empty((BATCH, SEQ, D), np.float32)
    for c in range(NCORE):
        b, seg = c // NSEG, c % NSEG
        out[b, seg * NT:(seg + 1) * NT, :] = res.results[c]["x_out"].reshape(D, NT).T
    return out
```

```python
import contextlib
import types
import numpy as np
import concourse.bass as bass
import concourse.mybir as mybir
from concourse.bass_utils import run_bass_kernel_spmd

F32 = mybir.dt.float32
BF16 = mybir.dt.bfloat16
AF = mybir.ActivationFunctionType
ALU = mybir.AluOpType

NSEM_ENG = 8
DEBUG = False
NLAYERS_RUN = 2

D = 1024
KD = 8
BATCH = 2
SEQ = 8192
DEPTH = 2
CTX = 256
PTOT = 9216
FF = 2816
KF = 22
NCORE = 8
NSEG = 4
NT = SEQ // NSEG
HX = 128
W = NT + 2 * HX
CS, CE = 1, 35
LO, HI = CS * 64, CE * 64
T0, T1 = HX, HX + NT
EPS = 1e-6
G_MIN = 1e-6
NST = 2 * 8 * 128 + 16

PL = 557
O_N1W, O_BMOD, O_GNW, O_CVW, O_CVB, O_LNW, O_LNB, O_N2W, O_FW, O_FB = 0, 8, 56, 57, 305, 313, 321, 329, 337, 535
O_LBZ = 2 * PL
O_FNW = O_LBZ + 32
NPV = O_FNW + 8


def freeze(fn, depth=0):
    if not isinstance(fn, types.FunctionType) or fn.__closure__ is None:
        return fn
    cells = []
    for c in fn.__closure__:
        try:
            v = c.cell_contents
        except ValueError:
            cells.append(c)
            continue
        if isinstance(v, types.FunctionType) and depth < 4:
            v = freeze(v, depth + 1)
        cells.append(types.CellType(v))
    g = types.FunctionType(fn.__code__, fn.__globals__, fn.__name__, fn.__defaults__, tuple(cells))
    g.__kwdefaults__ = fn.__kwdefaults__
    return g


class Buf:
    def __init__(self, name, t=None):
        self.name = name
        self.t = t
        self.last_w = None
        self.reads = []
        self.dsem = None

    def __getitem__(self, k):
        return self.t[k]


class Sched:
    ENGS = ("pe", "act", "dve", "pool", "sp")

    def __init__(self, nc, stack):
        self.nc = nc
        self.stack = stack
        self.streams = {e: [] for e in self.ENGS}
        self.count = {e: 0 for e in self.ENGS}
        self.sems = {}
        self.semval = {}
        self.waited = {e: {} for e in self.ENGS}
        self.final_events = []
        self.pending = {}
        self.last_ev = {}
        self.uid = 0
        self.free_dsems = []

    def sem(self, key):
        if key not in self.sems:
            self.sems[key] = self.stack.enter_context(self.nc.semaphore("s_%s_%s" % (key[0], key[1])))
            self.semval[key] = 0
        return self.sems[key]

    def sbuf(self, name, shape, dt, stack=None):
        self.uid += 1
        name = "%s_%d" % (name, self.uid)
        t = (stack or self.stack).enter_context(self.nc.sbuf_tensor(name, list(shape), dt))
        b = Buf(name, t)
        if stack is not None:
            stack.callback(self._release, b)
        return b

    def _release(self, b):
        if b.dsem is not None:
            self.free_dsems.append(b.dsem)
            b.dsem = None

    def psum(self, name, shape, dt):
        t = self.stack.enter_context(self.nc.psum_tensor(name, list(shape), dt))
        return Buf(name, t)

    def _deps(self, eng, reads, writes, acc):
        evs = []
        for b in reads:
            if b.last_w is not None:
                evs.append(b.last_w)
        for b in writes:
            if b.last_w is not None and not (acc and b.last_w[0][0] == "pe" and eng == "pe"):
                evs.append(b.last_w)
            evs.extend(b.reads)
        best = {}
        for (k, v) in evs:
            best[k] = max(best.get(k, 0), v)
        p = self.pending.pop(eng, None)
        if p:
            for (k, v) in p:
                best[k] = max(best.get(k, 0), v)
        waits = []
        w = self.waited[eng]
        for k, v in best.items():
            if w.get(k, 0) < v:
                waits.append((k, v))
                w[k] = v
        return waits

    def _commit(self, ev, reads, writes):
        for b in reads:
            b.reads.append(ev)
        for b in writes:
            b.last_w = ev
            b.reads = []
        self.last_ev[ev[0]] = ev[1]

    def op(self, eng, fn, reads=(), writes=(), acc=False):
        reads = [b for b in reads if b is not None]
        writes = [b for b in writes if b is not None]
        waits = self._deps(eng, reads, writes, acc)
        idx = self.count[eng]
        self.count[eng] += 1
        key = (eng, idx % NSEM_ENG)
        self.sem(key)
        ev = (key, idx // NSEM_ENG + 1)
        self.streams[eng].append((waits, freeze(fn), key, 1))
        self._commit(ev, reads, writes)
        return ev

    def dma(self, eng, out_ap, in_ap, reads=(), writes=(), sembuf=None):
        reads = [b for b in reads if b is not None]
        writes = [b for b in writes if b is not None]
        waits = self._deps(eng, reads, writes, False)
        if sembuf.dsem is None:
            if self.free_dsems:
                sembuf.dsem = self.free_dsems.pop()
            else:
                sembuf.dsem = ("dma", sembuf.name)
                self.sem(sembuf.dsem)
        key = sembuf.dsem
        self.semval[key] += 16
        ev = (key, self.semval[key])
        self.streams[eng].append((waits, lambda E: E.dma_start(out=out_ap, in_=in_ap), key, 16))
        self._commit(ev, reads, writes)
        return ev

    def barrier(self):
        evs = list(self.last_ev.items())
        for e in self.ENGS:
            self.pending[e] = list(evs)

    def emit(self, final=False):
        nc = self.nc
        sems = self.sems
        streams = self.streams
        final_events = list(self.last_ev.items()) if final else []
        with nc.Block() as block:
            def run(E, eng):
                for (waits, fn, key, inc) in streams[eng]:
                    for (k, v) in waits:
                        E.wait_ge(sems[k], v)
                    ins = fn(E)
                    ins.then_inc(sems[key], inc)
                if eng == "sp":
                    for (k, v) in final_events:
                        E.wait_ge(sems[k], v)
                streams[eng] = []

            @block.tensor
            def _(E):
                run(E, "pe")

            @block.scalar
            def _(E):
                run(E, "act")

            @block.vector
            def _(E):
                run(E, "dve")

            @block.gpsimd
            def _(E):
                run(E, "pool")

            @block.sync
            def _(E):
                run(E, "sp")


def blocks(lo, hi, n):
    out = []
    t = lo
    while t < hi:
        out.append((t, min(hi, t + n)))
        t += n
    return out


def build():
    nc = bass.Bass("TRN2", target_bir_lowering=False)
    dt_in = lambda name, shape: nc.dram_tensor(name, list(shape), F32, kind="ExternalInput").ap()
    xT4 = dt_in("xT4", [NSEG, KD, 128, W])
    cxT_in = dt_in("cxT", [KD, 128, CTX])
    cvec = dt_in("cvec", [128, KD, 2])
    valid4_d = dt_in("valid4", [NSEG, 128, W])
    pvec_d = dt_in("pvec", [128, NPV])
    cst_d = dt_in("cst", [128, 6, 128])
    rst_d = dt_in("rst", [128, 512])
    mcol_d = dt_in("mcol", [128, 2, W])
    fold4_d = dt_in("fold4", [NSEG, 128, 2, NSEG])
    validown_d = dt_in("validown", [128, W])
    foldown_d = dt_in("foldown", [128, 2, NSEG])
    onehot_d = dt_in("onehot", [128, NSEG])
    w_mod = dt_in("w_mod", [DEPTH, D, 6 * D])
    w_in = dt_in("w_in", [DEPTH, D, PTOT])
    w_hg_out = dt_in("w_hg_out", [DEPTH, D, D])
    w_cv_out = dt_in("w_cv_out", [DEPTH, D, D])
    w_out = dt_in("w_out", [DEPTH, D, D])
    w_up = dt_in("w_up", [DEPTH, D, 2 * FF])
    w_down = dt_in("w_down", [DEPTH, FF, D])
    x_out = nc.dram_tensor("x_out", [KD, 128, NT], F32, kind="ExternalOutput").ap()
    itn = lambda name, shape, dt=F32: nc.dram_tensor(name, list(shape), dt, kind="Internal").ap()
    ST = itn("ST", [NSEG, 128, NST])
    X1 = itn("X1", [NSEG, KD, 128, NT])
    XW = itn("XW", [NSEG, KD, 128, W])
    XWO = itn("XWO", [KD, 128, W])
    CX1 = itn("CX1", [KD, 128, CTX])
    SCB = itn("SCB", [8, 128, 128])
    xmid = itn("xmid", [KD, 128, W])
    xnew = itn("xnew", [KD, 128, NT])
    cxmid = itn("cxmid", [KD, 128, CTX])
    oT_d = itn("oT_d", [8, 128, W], BF16)
    oTc_d = itn("oTc_d", [8, 128, CTX], BF16)
    DR = {n: Buf(n) for n in ("xmid", "xnew", "cxmid", "oT_d", "oTc_d", "x_out", "cx_out", "ST", "X1", "XW", "XWO", "CX1", "SCB", "ext")}

    with contextlib.ExitStack() as st:
        S = Sched(nc, st)
        PS = [S.psum("ps%d" % i, [128, 512], F32) for i in range(7)]
        PSB = S.psum("psb", [128, 1024], BF16)
        psi = [0]

        def nps():
            psi[0] = (psi[0] + 1) % 7
            return PS[psi[0]]

        pvec = S.sbuf("pvec", [128, NPV], F32)
        cst = S.sbuf("cst", [128, 6, 128], F32)
        identb = S.sbuf("identb", [128, 128], BF16)
        maskFb = S.sbuf("maskFb", [32, 96], BF16)
        maskBb = S.sbuf("maskBb", [32, 96], BF16)
        valid = S.sbuf("valid", [128, W], BF16)
        ones_c = S.sbuf("ones_c", [128, CTX], BF16)
        rst = S.sbuf("rst", [128, 512], F32)
        foldm = S.sbuf("foldm", [128, 2, NSEG], F32)
        modv = S.sbuf("modv", [128, 48, 2], F32)
        A1 = S.sbuf("A1", [128, KD, 2], F32)
        A2 = S.sbuf("A2", [128, KD, 2], F32)
        lbv = S.sbuf("lbv", [128, 2, 3, KD], F32)
        rcp = S.sbuf("rcp", [128, KD], F32)
        epsc = S.sbuf("epsc", [128, 1], F32)
        carry = S.sbuf("carry", [128, 8, 128], F32)
        S.op("pool", lambda E: E.memset(epsc[:, :], EPS), writes=[epsc])
        S.dma("sp", pvec[:, :], pvec_d, writes=[pvec], sembuf=pvec)
        S.dma("sp", cst[:, :, :], cst_d, writes=[cst], sembuf=cst)
        S.dma("sp", rst[:, :], rst_d, writes=[rst], sembuf=rst)
        S.dma("pool", identb[:, :], cst_d[:, 0, :], writes=[identb], sembuf=identb)
        S.dma("pool", maskFb[:, :], cst_d[0:32, 3, 0:96], writes=[maskFb], sembuf=maskFb)
        S.dma("pool", maskBb[:, :], cst_d[0:32, 4, 0:96], writes=[maskBb], sembuf=maskBb)
        S.op("pool", lambda E: E.memset(ones_c[:, :], 1.0), writes=[ones_c])
        onesD = cst[:, 1, :]
        ones128 = cst[:, 2, :]

        def layer_setup(l):
            pb = l * PL
            for d in range(2):
                lb_ap = lbv[:, d, 0, :]
                if l == 0:
                    S.op("dve", lambda E, lb_ap=lb_ap: E.memset(lb_ap, 0.0), writes=[lbv])
                else:
                    z0 = pvec[:, O_LBZ + d * 16: O_LBZ + d * 16 + 8]
                    z1 = pvec[:, O_LBZ + d * 16 + 8: O_LBZ + d * 16 + 16]
                    S.op("dve", lambda E, lb_ap=lb_ap, z0=z0, z1=z1: E.tensor_tensor(out=lb_ap, in0=z1, in1=z0, op=ALU.subtract), reads=[pvec], writes=[lbv])
                    S.op("act", lambda E, lb_ap=lb_ap: E.activation(out=lb_ap, in_=lb_ap, func=AF.Sigmoid), reads=[lbv], writes=[lbv])
                a_ap = lbv[:, d, 1, :]
                sm_ap = lbv[:, d, 2, :]
                S.op("dve", lambda E, lb_ap=lb_ap, a_ap=a_ap: E.tensor_scalar(out=a_ap, in0=lb_ap, scalar1=-1.0, scalar2=1.0, op0=ALU.mult, op1=ALU.add), reads=[lbv], writes=[lbv])
                S.op("dve", lambda E, lb_ap=lb_ap, sm_ap=sm_ap: E.tensor_scalar(out=sm_ap, in0=lb_ap, scalar1=-1.0, scalar2=G_MIN, op0=ALU.mult, op1=ALU.add), reads=[lbv], writes=[lbv])
                S.op("dve", lambda E, a_ap=a_ap: E.reciprocal(out=rcp[:, :], in_=a_ap), reads=[lbv], writes=[rcp])
                S.op("dve", lambda E, sm_ap=sm_ap: E.tensor_tensor(out=sm_ap, in0=sm_ap, in1=rcp[:, :], op=ALU.mult), reads=[lbv, rcp], writes=[lbv])

            with contextlib.ExitStack() as ph:
                csb = S.sbuf("csb", [128, KD, 2], F32, ph)
                scb = S.sbuf("scb", [128, KD, 2], F32, ph)
                wm = [S.sbuf("wm%d" % i, [128, KD, 512], F32, ph) for i in range(2)]
                S.dma("sp", csb[:, :, :], cvec, writes=[csb], sembuf=csb)
                S.op("act", lambda E: E.activation(out=scb[:, :, :], in_=csb[:, :, :], func=AF.Silu), reads=[csb], writes=[scb])
                pm = PS[0]
                for piece in range(12):
                    buf = wm[piece % 2]
                    S.dma("sp", buf[:, :, :], w_mod[l, :, piece * 512:(piece + 1) * 512].rearrange("(k p) c -> p k c", p=128), writes=[buf], sembuf=buf)
                    for mm in range(4):
                        m = piece * 4 + mm
                        for k in range(KD):
                            S.op("pe", lambda E, buf=buf, k=k, mm=mm, m=m: E.matmul(pm[:, m * 2:m * 2 + 2], lhsT=buf[:, k, mm * 128:(mm + 1) * 128], rhs=scb[:, k, :], start=(k == 0), stop=(k == KD - 1)),
                                 reads=[buf, scb], writes=[pm], acc=True)
                S.op("dve", lambda E: E.tensor_tensor(out=modv[:, :, :], in0=pm[:, 0:96].rearrange("p (m s) -> p m s", s=2),
                                                      in1=pvec[:, pb + O_BMOD: pb + O_BMOD + 48].unsqueeze(2).to_broadcast([128, 48, 2]), op=ALU.add),
                     reads=[pm, pvec], writes=[modv])
                for (Ax, off, onw) in ((A1, 8, O_N1W), (A2, 32, O_N2W)):
                    S.op("dve", lambda E, Ax=Ax, off=off, onw=onw: E.scalar_tensor_tensor(out=Ax[:, :, :], in0=modv[:, off:off + 8, :], scalar=1.0,
                                                                                          in1=pvec[:, pb + onw: pb + onw + 8].unsqueeze(2).to_broadcast([128, 8, 2]),
                                                                                          op0=ALU.add, op1=ALU.mult), reads=[modv, pvec], writes=[Ax])
                S.barrier()
                S.emit()
        SH1 = lambda k, s: modv[:, 0 + k, s:s + 1]
        G1 = lambda k, s: modv[:, 16 + k, s:s + 1]
        SH2 = lambda k, s: modv[:, 24 + k, s:s + 1]
        G2 = lambda k, s: modv[:, 40 + k, s:s + 1]

        def wload(dst, src_ap):
            S.dma("pool", dst[:, :, :], src_ap.rearrange("(k p) c -> p k c", p=128), writes=[dst], sembuf=dst)

        def proj(ps_ap, psbuf, wbuf, wap_fn, rbuf, rap_fn, nk=KD):
            for k in range(nk):
                S.op("pe", lambda E, k=k: E.matmul(ps_ap, lhsT=wap_fn(k), rhs=rap_fn(k), start=(k == 0), stop=(k == nk - 1)),
                     reads=[wbuf, rbuf], writes=[psbuf], acc=True)

        class NormTmp:
            def __init__(self, ph, n=512, nbuf=1):
                self.xbs = [S.sbuf("n_xb", [128, KD, n], F32, ph) for _ in range(nbuf)]
                self.sqs = [S.sbuf("n_sq", [128, KD, n], F32, ph) for _ in range(nbuf)]
                self.cnt = 0
                self.xb = self.xbs[0]
                self.sq = self.sqs[0]
                self.rstd = S.sbuf("n_rstd", [128, n], F32, ph)
                self.tmp = [S.sbuf("n_tmp%d" % i, [128, n], F32, ph) for i in range(2)]

            def rotate(self):
                self.cnt += 1
                self.xb = self.xbs[self.cnt % len(self.xbs)]
                self.sq = self.sqs[self.cnt % len(self.sqs)]

        def norm_block(nt, src_ap, srcbuf, n, Afn, SHfn, outbuf, out_fn):
            nt.rotate()
            xb, sq = nt.xb, nt.sq
            S.dma("sp", xb[:, :, 0:n], src_ap, reads=[srcbuf], writes=[xb], sembuf=xb)
            S.op("act", lambda E: E.activation(out=sq[:, :, 0:n], in_=xb[:, :, 0:n], func=AF.Square), reads=[xb], writes=[sq])
            ps = nps()
            for k in range(KD):
                S.op("pe", lambda E, k=k: E.matmul(ps[:, 0:n], lhsT=onesD, rhs=sq[:, k, 0:n], start=(k == 0), stop=(k == KD - 1)),
                     reads=[cst, sq], writes=[ps], acc=True)
            S.op("act", lambda E: E.activation(out=nt.rstd[:, 0:n], in_=ps[:, 0:n], func=AF.Ln, bias=epsc[:, 0:1], scale=1.0), reads=[ps, epsc], writes=[nt.rstd])
            S.op("act", lambda E: E.activation(out=nt.rstd[:, 0:n], in_=nt.rstd[:, 0:n], func=AF.Exp, scale=-0.5), reads=[nt.rstd], writes=[nt.rstd])
            for k in range(KD):
                tmp = nt.tmp[k % 2]
                S.op("dve", lambda E, k=k, tmp=tmp: E.scalar_tensor_tensor(out=tmp[:, 0:n], in0=xb[:, k, 0:n], scalar=Afn(k), in1=nt.rstd[:, 0:n], op0=ALU.mult, op1=ALU.mult),
                     reads=[xb, nt.rstd, A1, A2, pvec], writes=[tmp])
                sh = SHfn(k)
                if sh is None:
                    S.op("act", lambda E, k=k, tmp=tmp: E.activation(out=out_fn(k), in_=tmp[:, 0:n], func=AF.Copy), reads=[tmp], writes=[outbuf])
                else:
                    S.op("act", lambda E, k=k, tmp=tmp, sh=sh: E.activation(out=out_fn(k), in_=tmp[:, 0:n], func=AF.Identity, bias=sh, scale=1.0),
                         reads=[tmp, modv], writes=[outbuf])


        def run_pass(kind, l, i, xT, cxT, DRX, DRC, x_dst, x_dst_buf, ctx_stream, is_last, valid_src, fold_src):
            pb = l * PL
            st_in = ST
            ctx_needed = not (l == 0 and i > 0)
            S.dma("sp", foldm[:, :, :], fold_src, writes=[foldm], sembuf=foldm)
            S.dma("pool", valid[:, :], valid_src, writes=[valid], sembuf=valid)
            cx_out = CX1
            mixer = contextlib.ExitStack()
            hT = S.sbuf("hT", [128, KD, W], BF16, mixer)
            hTc = S.sbuf("hTc", [128, KD, CTX], BF16, mixer)
            with contextlib.ExitStack() as ph:
                nt = NormTmp(ph, nbuf=2)
                for (b0, b1) in blocks(0, W, 512):
                    norm_block(nt, xT[:, :, b0:b1].rearrange("k p t -> p k t"), DRX, b1 - b0,
                               lambda k: A1[:, k, 0:1], lambda k: SH1(k, 0), hT, lambda k, b0=b0, b1=b1: hT[:, k, b0:b1])
                if kind == "B" and ctx_needed:
                    norm_block(nt, cxT.rearrange("k p t -> p k t"), DRC, CTX,
                               lambda k: A1[:, k, 1:2], lambda k: SH1(k, 1), hTc, lambda k: hTc[:, k, 0:CTX])
                S.barrier()
                S.emit()

            hg = contextlib.ExitStack()
            full = (kind == "B")
            NW = 5
            wh = [S.sbuf("wh%d" % i, [128, KD, NW * 128], BF16, hg) for i in range(2)]
            nchw = W // 64
            kT = {d: S.sbuf("kT%d" % d, [64, nchw, 128], BF16, hg) for d in range(2)}
            vT = S.sbuf("vT", [64, nchw, 128], BF16, hg)
            vT2 = S.sbuf("vT2", [32, nchw, 128], BF16, hg)
            stt_ = {d: S.sbuf("st%d" % d, [128, nchw], F32, hg) for d in range(2)}
            if full:
                qt = {d: S.sbuf("qt%d" % d, [128, W], BF16, hg) for d in range(2)}
                kt = {d: S.sbuf("kt%d" % d, [128, W], BF16, hg) for d in range(2)}
                srt = {d: S.sbuf("sr%d" % d, [128, nchw], F32, hg) for d in range(2)}
                of = S.sbuf("of", [128, W], F32, hg)
                ob2 = S.sbuf("ob2", [128, W], F32, hg)
                ob = kt[0]
                stin_r = [S.sbuf("stin0", [128, NSEG, 2, 128], F32, hg)] * 2
                stA = S.sbuf("stA", [128, NSEG, 16], F32, hg)
                S.dma("sp", stA[:, :, :], st_in[:, :, 2048:2064].rearrange("s p c -> p s c"), reads=[DR["ST"]], writes=[stA], sembuf=stA)
                Sbf = {d: [S.sbuf("Sbf%d_%d" % (d, i), [128, 128], BF16, hg) for i in range(2)] for d in range(2)}
                atall = {d: S.sbuf("atall%d" % d, [32, nchw, 96], BF16, hg) for d in range(2)}
                alpha = S.sbuf("alpha", [128, 2], F32, hg)
            Sst = {d: S.sbuf("Sst%d" % d, [128, 128], F32, hg) for d in range(2)}
            stpack = S.sbuf("stpack", [128, NST], F32, hg) if not full else None
            nslots = 1 if full else 2
            tpd = {(sl, d): {n: S.sbuf("t%d%d_%s" % (sl, d, n), [128, 512], F32, hg) for n in ("s", "sn", "c", "D", "dq", "e2", "e3")} for d in range(2) for sl in range(nslots)}
            tp = tpd[(0, 0)]
            tkbd = {(sl, d): S.sbuf("t_kb%d%d" % (sl, d), [128, 512], BF16, hg) for d in range(2) for sl in range(nslots)}
            tvbs = [S.sbuf("t_vb%d" % sl, [128, 512], BF16, hg) for sl in range(nslots)]

            def drive(gens):
                gens = list(gens)
                while gens:
                    for g_ in list(gens):
                        try:
                            next(g_)
                        except StopIteration:
                            gens.remove(g_)


            def tr32(src, dst, c0, nch, dst2=None):
                for j in range(nch):
                    S.op("pe", lambda E, j=j: E.transpose(PSB[0:64, j * 128:(j + 1) * 128], src[:, j * 64:(j + 1) * 64], identb[:, :]),
                         reads=[src, identb], writes=[PSB], acc=True)
                S.op("act", lambda E: E.activation(out=dst[:, c0:c0 + nch, :], in_=PSB[0:64, 0:nch * 128].rearrange("p (c d) -> p c d", d=128), func=AF.Copy),
                     reads=[PSB], writes=[dst])
                if dst2 is not None:
                    for j in range(nch):
                        S.op("pe", lambda E, j=j: E.transpose(PSB[0:32, j * 128:(j + 1) * 128], src[:, j * 64 + 32:(j + 1) * 64], identb[:, :]),
                             reads=[src, identb], writes=[PSB], acc=True)
                    S.op("act", lambda E: E.activation(out=dst2[:, c0:c0 + nch, :], in_=PSB[0:32, 0:nch * 128].rearrange("p (c d) -> p c d", d=128), func=AF.Copy),
                         reads=[PSB], writes=[dst2])

            def hg_gates(h, wb, hsrc, hbuf, t0, n, vmask, vbuf, want_q, dirs=(0, 1), slot=0, run=True):
                nch = n // 64
                c0 = t0 // 64
                v3 = lambda ap: ap.rearrange("p (c t) -> p c t", t=64)
                tvb = tvbs[slot]

                def vunit():
                    ps = nps()
                    proj(ps[:, 0:n], ps, wb, lambda k: wb[:, k, 3 * 128:4 * 128], hbuf, lambda k: hsrc[:, k, t0:t0 + n])
                    yield
                    S.op("act", lambda E: E.activation(out=tvb[:, 0:n], in_=ps[:, 0:n], func=AF.Copy), reads=[ps], writes=[tvb])
                    yield
                    yield
                    tr32(tvb, vT, c0, nch, vT2 if want_q else None)
                if want_q:
                    psq = nps()
                    proj(psq[:, 0:n], psq, wb, lambda k: wb[:, k, 0:128], hbuf, lambda k: hsrc[:, k, t0:t0 + n])
                def unit(d):
                    lb = lbv[:, d, 0, h:h + 1]
                    a = lbv[:, d, 1, h:h + 1]
                    smin = lbv[:, d, 2, h:h + 1]
                    psf = nps()
                    proj(psf[:, 0:n], psf, wb, lambda k, d=d: wb[:, k, (1 + d) * 128:(2 + d) * 128], hbuf, lambda k: hsrc[:, k, t0:t0 + n])
                    yield
                    T = tpd[(slot, d)]
                    s_, sn, cc, DD, dq, e2, e3 = T["s"], T["sn"], T["c"], T["D"], T["dq"], T["e2"], T["e3"]
                    lg, kk, e1 = s_, sn, dq
                    tkb_ = tkbd[(slot, d)]
                    S.op("act", lambda E: E.activation(out=s_[:, 0:n], in_=psf[:, 0:n], func=AF.Sigmoid), reads=[psf], writes=[s_])
                    S.op("act", lambda E: E.activation(out=sn[:, 0:n], in_=psf[:, 0:n], func=AF.Sigmoid, scale=-1.0), reads=[psf], writes=[sn])
                    yield
                    S.op("dve", lambda E, smin=smin: E.tensor_scalar(out=s_[:, 0:n], in0=s_[:, 0:n], scalar1=smin, scalar2=None, op0=ALU.max), reads=[s_, lbv], writes=[s_])
                    S.op("dve", lambda E, a=a: E.scalar_tensor_tensor(out=kk[:, 0:n], in0=sn[:, 0:n], scalar=a, in1=vmask, op0=ALU.mult, op1=ALU.mult),
                         reads=[sn, lbv, vbuf], writes=[kk])
                    yield
                    S.op("act", lambda E, a=a, lb=lb: E.activation(out=lg[:, 0:n], in_=s_[:, 0:n], func=AF.Ln, scale=a, bias=lb), reads=[s_, lbv], writes=[lg])
                    yield
                    S.op("pool", lambda E: E.tensor_tensor(out=lg[:, 0:n], in0=lg[:, 0:n], in1=vmask, op=ALU.mult), reads=[lg, vbuf], writes=[lg])
                    yield
                    S.op("dve", lambda E: E.tensor_tensor_scan(out=cc[:, 0:n], data0=rst[:, 0:n], data1=lg[:, 0:n], initial=0.0, op0=ALU.mult, op1=ALU.add),
                         reads=[rst, lg], writes=[cc])
                    yield
                    c63 = v3(cc[:, 0:n])[:, :, 63:64]
                    S.op("act", lambda E, d=d: E.activation(out=stt_[d][:, c0:c0 + nch], in_=v3(cc[:, 0:n])[:, :, 63], func=AF.Exp), reads=[cc], writes=[stt_[d]])
                    if d == 0:
                        Dv = cc
                    else:
                        S.op("dve", lambda E: E.tensor_tensor(out=DD[:, 0:n], in0=lg[:, 0:n], in1=cc[:, 0:n], op=ALU.subtract), reads=[lg, cc], writes=[DD])
                        S.op("dve", lambda E, c63=c63: E.tensor_tensor(out=v3(DD[:, 0:n]), in0=v3(DD[:, 0:n]), in1=c63.to_broadcast([128, nch, 64]), op=ALU.add),
                             reads=[DD, cc], writes=[DD])
                        Dv = DD
                    yield
                    S.op("dve", lambda E, c63=c63, Dv=Dv: E.tensor_tensor(out=v3(e3[:, 0:n]), in0=v3(Dv[:, 0:n]), in1=c63.to_broadcast([128, nch, 64]), op=ALU.subtract),
                         reads=[Dv, cc], writes=[e3])
                    if want_q:
                        dref = v3(Dv[:, 0:n])[:, :, 31:32]
                        S.op("dve", lambda E, Dv=Dv, dref=dref: E.tensor_tensor(out=v3(dq[:, 0:n]), in0=v3(Dv[:, 0:n]), in1=dref.to_broadcast([128, nch, 64]), op=ALU.subtract),
                             reads=[Dv], writes=[dq])
                    yield
                    S.op("act", lambda E: E.activation(out=e3[:, 0:n], in_=e3[:, 0:n], func=AF.Exp, scale=-1.0), reads=[e3], writes=[e3])
                    if want_q:
                        S.op("act", lambda E, d=d, Dv=Dv: E.activation(out=srt[d][:, c0:c0 + nch], in_=v3(Dv[:, 0:n])[:, :, 31], func=AF.Exp), reads=[Dv], writes=[srt[d]])
                        S.op("act", lambda E: E.activation(out=e2[:, 0:n], in_=dq[:, 0:n], func=AF.Exp, scale=-1.0), reads=[dq], writes=[e2])
                        S.op("act", lambda E: E.activation(out=e1[:, 0:n], in_=dq[:, 0:n], func=AF.Exp), reads=[dq], writes=[e1])
                    yield
                    S.op("pool", lambda E: E.tensor_tensor(out=tkb_[:, 0:n], in0=kk[:, 0:n], in1=e3[:, 0:n], op=ALU.mult), reads=[kk, e3], writes=[tkb_])
                    if want_q:
                        S.op("dve", lambda E, d=d: E.tensor_tensor(out=qt[d][:, t0:t0 + n], in0=psq[:, 0:n], in1=e1[:, 0:n], op=ALU.mult), reads=[psq, e1], writes=[qt[d]])
                        S.op("pool", lambda E, d=d: E.tensor_tensor(out=kt[d][:, t0:t0 + n], in0=kk[:, 0:n], in1=e2[:, 0:n], op=ALU.mult), reads=[kk, e2], writes=[kt[d]])
                    yield
                    tr32(tkb_, kT[d], c0, nch)

                gens = [unit(d) for d in dirs] + [vunit()]
                if not run:
                    return gens
                drive(gens)

            def hg_scan(chunks_f, chunks_b, with_out, obase, snap=None):
                nsteps = max(len(chunks_f), len(chunks_b))
                POB = {0: [PS[0], PS[6]], 1: [PS[1], PS[3]]}
                PA = {0: PS[2], 1: PS[3]}
                PP = {0: PS[4], 1: PS[5]}
                masks = {0: maskFb, 1: maskBb}
                pending = {0: [], 1: []}

                def flush(d):
                    if not pending[d]:
                        return
                    cl = sorted(pending[d])
                    ta, tb = cl[0] * 64, (cl[-1] + 1) * 64
                    pa = ((cl[0] * 64) % 512)
                    n = tb - ta
                    pob = POB[d][(ta // 512) % 2]
                    dst_ = of if d == 0 else ob2
                    S.op("act", lambda E: E.activation(out=dst_[:, ta:tb], in_=pob[:, pa:pa + n], func=AF.Copy), reads=[pob], writes=[dst_])
                    pending[d] = []

                if with_out:
                    PAr = [PS[2], PS[3], PS[4], PS[5]]
                    pai = 0
                    for i in range(nsteps):
                        for d, chl in ((0, chunks_f), (1, chunks_b)):
                            if i >= len(chl):
                                continue
                            c = chl[i]
                            tsl = slice(c * 64, (c + 1) * 64)
                            h1 = slice(c * 64, c * 64 + 32)
                            h2 = slice(c * 64 + 32, c * 64 + 64)
                            ka, kb_ = (h1, h2) if d == 0 else (h2, h1)
                            pa_ = PAr[pai % 4]
                            pai += 1
                            S.op("pe", lambda E, d=d, tsl=tsl, ka=ka, pa_=pa_: E.matmul(pa_[0:32, 0:64], lhsT=kt[d][:, ka], rhs=qt[d][:, tsl], start=True, stop=True),
                                 reads=[kt[d], qt[d]], writes=[pa_])
                            S.op("pe", lambda E, d=d, kb_=kb_, pa_=pa_: E.matmul(pa_[0:32, 64:96], lhsT=kt[d][:, kb_], rhs=qt[d][:, kb_], start=True, stop=True),
                                 reads=[kt[d], qt[d]], writes=[pa_], acc=True)
                            S.op("dve", lambda E, d=d, c=c, pa_=pa_: E.tensor_tensor(out=atall[d][:, c, :], in0=pa_[0:32, 0:96], in1=masks[d][:, :], op=ALU.mult),
                                 reads=[pa_, masks[d]], writes=[atall[d]])
                for i in range(nsteps):
                    for d, chl in ((0, chunks_f), (1, chunks_b)):
                        if i >= len(chl):
                            continue
                        c = chl[i]
                        tsl = slice(c * 64, (c + 1) * 64)
                        if with_out:
                            sb = Sbf[d][i % 2]
                            S.op("dve", lambda E, d=d, sb=sb, c=c: E.tensor_scalar(out=sb[:, :], in0=Sst[d][:, :], scalar1=srt[d][:, c:c + 1], scalar2=None, op0=ALU.mult),
                                 reads=[Sst[d], srt[d]], writes=[sb])
                            va = vT[0:32, c, :] if d == 0 else vT2[:, c, :]
                            vb = vT2[:, c, :] if d == 0 else vT[0:32, c, :]
                            po = (c * 64) % 512
                            if pending[d] and (c * 64) // 512 != (pending[d][0] * 64) // 512:
                                flush(d)
                            po2 = po + 32 if d == 0 else po
                            pob = POB[d][((c * 64) // 512) % 2]
                            S.op("pe", lambda E, d=d, sb=sb, tsl=tsl, po=po, pob=pob: E.matmul(pob[:, po:po + 64], lhsT=sb[:, :], rhs=qt[d][:, tsl], start=True, stop=False),
                                 reads=[sb, qt[d]], writes=[pob], acc=True)
                            S.op("pe", lambda E, d=d, c=c, va=va, po=po, pob=pob: E.matmul(pob[:, po:po + 64], lhsT=va, rhs=atall[d][:, c, 0:64], start=False, stop=False),
                                 reads=[vT, vT2, atall[d]], writes=[pob], acc=True)
                            S.op("pe", lambda E, d=d, c=c, vb=vb, po2=po2, pob=pob: E.matmul(pob[:, po2:po2 + 32], lhsT=vb, rhs=atall[d][:, c, 64:96], start=False, stop=True),
                                 reads=[vT, vT2, atall[d]], writes=[pob], acc=True)
                            pending[d].append(c)
                        S.op("pe", lambda E, d=d, c=c: E.matmul(PP[d][:, 0:128], lhsT=kT[d][:, c, :], rhs=vT[:, c, :], start=True, stop=True),
                             reads=[kT[d], vT], writes=[PP[d]])
                        S.op("dve", lambda E, d=d, c=c: E.scalar_tensor_tensor(out=Sst[d][:, :], in0=Sst[d][:, :], scalar=stt_[d][:, c:c + 1], in1=PP[d][:, 0:128], op0=ALU.mult, op1=ALU.add),
                             reads=[Sst[d], stt_[d], PP[d]], writes=[Sst[d]])
                        if snap is not None and d == 0 and c == snap[0]:
                            S.op("act", lambda E: E.activation(out=snap[1], in_=Sst[0][:, :], func=AF.Copy), reads=[Sst[0]], writes=[carry])
                if with_out:
                    flush(0)
                    flush(1)

            def gnorm_store(h, lo, hi, dst_d, dstbuf, wb, hsrc, hbuf):
                gnw = pvec[:, pb + O_GNW: pb + O_GNW + 1]
                for (b0, b1) in blocks(lo, hi, 512):
                    n = b1 - b0
                    osq, orn, sgt = tp["s"], tp["sn"], tp["c"]
                    psg = nps()
                    proj(psg[:, 0:n], psg, wb, lambda k: wb[:, k, 4 * 128:5 * 128], hbuf, lambda k: hsrc[:, k, b0:b1])
                    S.op("act", lambda E: E.activation(out=sgt[:, 0:n], in_=psg[:, 0:n], func=AF.Silu), reads=[psg], writes=[sgt])
                    S.op("dve", lambda E: E.tensor_tensor(out=of[:, b0:b1], in0=of[:, b0:b1], in1=ob2[:, b0:b1], op=ALU.add), reads=[of, ob2], writes=[of])
                    S.op("act", lambda E: E.activation(out=osq[:, 0:n], in_=of[:, b0:b1], func=AF.Square), reads=[of], writes=[osq])
                    ps = nps()
                    S.op("pe", lambda E: E.matmul(ps[:, 0:n], lhsT=ones128, rhs=osq[:, 0:n], start=True, stop=True), reads=[cst, osq], writes=[ps])
                    S.op("act", lambda E: E.activation(out=orn[:, 0:n], in_=ps[:, 0:n], func=AF.Ln, bias=epsc[:, 0:1], scale=1.0), reads=[ps, epsc], writes=[orn])
                    S.op("act", lambda E: E.activation(out=orn[:, 0:n], in_=orn[:, 0:n], func=AF.Exp, scale=-0.5), reads=[orn], writes=[orn])
                    S.op("dve", lambda E: E.tensor_tensor(out=orn[:, 0:n], in0=orn[:, 0:n], in1=of[:, b0:b1], op=ALU.mult), reads=[orn, of], writes=[orn])
                    S.op("dve", lambda E: E.scalar_tensor_tensor(out=ob[:, b0:b1], in0=orn[:, 0:n], scalar=gnw, in1=sgt[:, 0:n], op0=ALU.mult, op1=ALU.mult),
                         reads=[orn, pvec, sgt], writes=[ob])
                S.dma("sp", dst_d[h, :, lo:hi], ob[:, lo:hi], reads=[ob], writes=[dstbuf], sembuf=ob)

            def load_head_w(hh):
                wb_ = wh[hh % 2]
                for g in range(NW):
                    S.dma("pool", wb_[:, :, g * 128:(g + 1) * 128], w_in[l, :, g * 1024 + hh * 128: g * 1024 + (hh + 1) * 128].rearrange("(k p) c -> p k c", p=128),
                          writes=[wb_], sembuf=wb_)

            load_head_w(0)
            for h in range(8):
                wb = wh[h % 2]
                if h + 1 < 8:
                    load_head_w(h + 1)
                if kind == "A":
                    dirs = [d for d in (0, 1) if (d == 0 and i < NSEG - 1 and l > 0) or (d == 1 and i > 0)]
                    if h == 0:
                        S.op("pool", lambda E: E.memset(stpack[:, :], 0.0), writes=[stpack])
                    blks = blocks(LO, HI, 512)
                    for j in range(0, len(blks), 2):
                        gens = []
                        for sl, (b0, b1) in enumerate(blks[j:j + 2]):
                            gens += hg_gates(h, wb, hT, hT, b0, b1 - b0, valid[:, b0:b1], valid, False, dirs, slot=sl, run=False)
                        drive(gens)
                    for d in dirs:
                        S.op("pool", lambda E, d=d: E.memset(Sst[d][:, :], 0.0), writes=[Sst[d]])
                    hg_scan(list(range(CS, CS + 32)) if 0 in dirs else [], list(range(CE - 1, CE - 33, -1)) if 1 in dirs else [], False, 0)
                    for d in dirs:
                        S.op("act", lambda E, d=d, h=h: E.activation(out=stpack[:, (d * 8 + h) * 128:(d * 8 + h + 1) * 128], in_=Sst[d][:, :], func=AF.Copy), reads=[Sst[d]], writes=[stpack])
                    for d, (ca, cb_) in ((0, (CS, CS + 32)), (1, (CE - 32, CE))):
                        if d not in dirs:
                            continue
                        col = 2048 + d * 8 + h
                        lgt = tp["dq"]
                        S.op("act", lambda E, d=d, ca=ca, cb_=cb_: E.activation(out=lgt[:, 0:32], in_=stt_[d][:, ca:cb_], func=AF.Ln), reads=[stt_[d]], writes=[lgt])
                        S.op("dve", lambda E: E.tensor_reduce(out=lgt[:, 32:33], in_=lgt[:, 0:32], axis=mybir.AxisListType.X, op=ALU.add), reads=[lgt], writes=[lgt])
                        S.op("act", lambda E, col=col: E.activation(out=stpack[:, col:col + 1], in_=lgt[:, 32:33], func=AF.Exp), reads=[lgt], writes=[stpack])
                else:
                    if ctx_needed:
                        hg_gates(h, wb, hTc, hTc, 0, CTX, ones_c[:, 0:CTX], ones_c, ctx_stream)
                        for d in range(2):
                            S.op("pool", lambda E, d=d: E.memset(Sst[d][:, :], 0.0), writes=[Sst[d]])
                        hg_scan([0, 1, 2, 3], [3, 2, 1, 0], ctx_stream, 0)
                        if ctx_stream:
                            gnorm_store(h, 0, CTX, oTc_d, DR["oTc_d"], wb, hTc, hTc)
                        if l == 0:
                            S.dma("sp", SCB[h], Sst[1][:, :], reads=[Sst[1]], writes=[DR["SCB"]], sembuf=Sst[1])
                    else:
                        S.dma("sp", Sst[1][:, :], SCB[h], reads=[DR["SCB"]], writes=[Sst[1]], sembuf=Sst[1])
                    stin = stin_r[h % 2]
                    for d in range(2):
                        S.dma("sp", stin[:, :, d, 0:128], st_in[:, :, (d * 8 + h) * 128:(d * 8 + h + 1) * 128].rearrange("s p c -> p s c"), reads=[DR["ST"]], writes=[stin], sembuf=stin)
                    for d in range(2):
                        if l == 0 and d == 0:
                            if i > 0:
                                S.op("act", lambda E, h=h: E.activation(out=Sst[0][:, :], in_=carry[:, h, :], func=AF.Copy), reads=[carry], writes=[Sst[0]])
                            continue
                        order = range(NSEG) if d == 0 else range(NSEG - 1, -1, -1)
                        for kseg in order:
                            m = foldm[:, d, kseg:kseg + 1]
                            S.op("dve", lambda E, kseg=kseg, d=d, m=m, h=h: E.tensor_scalar(out=alpha[:, 0:1], in0=stA[:, kseg, d * 8 + h:d * 8 + h + 1], scalar1=-1.0, scalar2=m, op0=ALU.add, op1=ALU.mult),
                                 reads=[stA, foldm], writes=[alpha])
                            S.op("dve", lambda E: E.tensor_scalar(out=alpha[:, 0:1], in0=alpha[:, 0:1], scalar1=1.0, scalar2=None, op0=ALU.add), reads=[alpha], writes=[alpha])
                            S.op("dve", lambda E, d=d: E.tensor_scalar(out=Sst[d][:, :], in0=Sst[d][:, :], scalar1=alpha[:, 0:1], scalar2=None, op0=ALU.mult), reads=[Sst[d], alpha], writes=[Sst[d]])
                            S.op("dve", lambda E, d=d, kseg=kseg, m=m, stin=stin: E.scalar_tensor_tensor(out=Sst[d][:, :], in0=stin[:, kseg, d, 0:128], scalar=m, in1=Sst[d][:, :], op0=ALU.mult, op1=ALU.add),
                                 reads=[stin, foldm, Sst[d]], writes=[Sst[d]])
                    for (b0, b1) in blocks(LO, HI, 512):
                        hg_gates(h, wb, hT, hT, b0, b1 - b0, valid[:, b0:b1], valid, True)
                    hg_scan(list(range(CS, CE)), list(range(CE - 1, CS - 1, -1)), True, 0, snap=((CS + 31, carry[:, h, :]) if l == 0 else None))
                    gnorm_store(h, LO, HI, oT_d, DR["oT_d"], wb, hT, hT)
                pass

            if kind == "A":
                S.dma("sp", ST[i], stpack[:, :], reads=[stpack], writes=[DR["ST"]], sembuf=stpack)
                S.barrier()
                S.emit()
                hg.close()
                mixer.close()
                return
            S.barrier()
            S.emit()
            hg.close()
            mixer.close()

            with contextlib.ExitStack() as ph:
                wres = S.sbuf("wres", [128, KD, 4096], BF16, ph)
                for j in range(8):
                    S.dma("pool", wres[:, :, j * 512:(j + 1) * 512], w_in[l, :, 5120 + j * 512: 5120 + (j + 1) * 512].rearrange("(k p) c -> p k c", p=128), writes=[wres], sembuf=wres)
                w3r = [S.sbuf("w3r%d" % i, [128, KD, 128], BF16, ph) for i in range(3)]
                w3i = [0]

                def w3load(src):
                    w3i[0] = (w3i[0] + 1) % 3
                    b = w3r[w3i[0]]
                    S.dma("pool", b[:, :, :], src.rearrange("(k p) c -> p k c", p=128), writes=[b], sembuf=b)
                    return b
                nt = NormTmp(ph)
                hTb = S.sbuf("hTb", [128, KD, 512], BF16, ph)
                ubs = [S.sbuf("ub%d" % q, [128, 512], F32, ph) for q in range(2)]
                sbbs = [S.sbuf("sbb%d" % q, [128, 512], F32, ph) for q in range(2)]
                accs = [S.sbuf("acc%d" % q, [128, 480], F32, ph) for q in range(2)]
                vTb = S.sbuf("vTb", [128, KD, 480], F32, ph)
                mean = S.sbuf("mean", [128, 480], F32, ph)
                var = S.sbuf("var", [128, 480], F32, ph)
                t1 = [S.sbuf("t1_%d" % i, [128, 480], F32, ph) for i in range(2)]
                yc = S.sbuf("yc", [128, KD, 480], BF16, ph)
                yT = S.sbuf("yT", [128, KD, 480], BF16, ph)
                oTb = S.sbuf("oTb", [128, 8, 480], BF16, ph)
                sgcs = [S.sbuf("sgc%d" % q, [128, 480], F32, ph) for q in range(2)]
                sghs = [S.sbuf("sgh%d" % q, [128, 480], F32, ph) for q in range(2)]
                m1s = [S.sbuf("m1_0", [128, 480], F32, ph)] * 2
                m2s = [S.sbuf("m2_0", [128, 480], F32, ph)] * 2
                xo = [S.sbuf("xo%d" % i, [128, 480], F32, ph) for i in range(2)]
                cvw = lambda tap, c: pvec[:, pb + O_CVW + tap * 8 + c: pb + O_CVW + tap * 8 + c + 1]

                def merge_seq(xsrc, lo, hi, srclo, srchi, s, oTsrc, oTbuf, vmask_buf, dst, dstbuf):
                    for (b0, b1) in blocks(lo, hi, 480):
                        n = b1 - b0
                        a0, a1 = max(srclo, b0 - 15), min(srchi, b1 + 15)
                        m = a1 - a0
                        off = a0 - (b0 - 15)
                        ctr = b0 - a0
                        norm_block(nt, xsrc[:, :, a0:a1].rearrange("k p t -> p k t"), DRX, m,
                                   lambda k: A1[:, k, s:s + 1], lambda k: SH1(k, s), hTb, lambda k: hTb[:, k, 0:m])
                        S.dma("sp", oTb[:, :, 0:n], oTsrc[:, :, b0:b1].rearrange("h p t -> p h t"), reads=[oTbuf], writes=[oTb], sembuf=oTb)
                        def cunit(c, slot):
                            ub_, sbb_, acc_ = ubs[slot], sbbs[slot], accs[slot]
                            psa = nps()
                            proj(psa[:, 0:m], psa, wres, lambda k, c=c: wres[:, k, c * 128:(c + 1) * 128], hTb, lambda k: hTb[:, k, 0:m])
                            psb_ = nps()
                            proj(psb_[:, 0:m], psb_, wres, lambda k, c=c: wres[:, k, 1024 + c * 128:1024 + (c + 1) * 128], hTb, lambda k: hTb[:, k, 0:m])
                            yield
                            S.op("act", lambda E: E.activation(out=sbb_[:, 0:m], in_=psb_[:, 0:m], func=AF.Sigmoid), reads=[psb_], writes=[sbb_])
                            yield
                            S.op("pool", lambda E: E.tensor_tensor(out=sbb_[:, 0:m], in0=sbb_[:, 0:m], in1=vmask_buf[:, a0:a1], op=ALU.mult), reads=[sbb_, vmask_buf], writes=[sbb_])
                            if off > 0 or m < n + 30:
                                S.op("pool", lambda E: E.memset(ub_[:, :], 0.0), writes=[ub_])
                            yield
                            S.op("dve", lambda E: E.tensor_tensor(out=ub_[:, off:off + m], in0=psa[:, 0:m], in1=sbb_[:, 0:m], op=ALU.mult), reads=[psa, sbb_], writes=[ub_])
                            yield
                            S.op("dve", lambda E, c=c: E.tensor_scalar(out=acc_[:, 0:n], in0=ub_[:, 0:n], scalar1=cvw(0, c), scalar2=pvec[:, pb + O_CVB + c: pb + O_CVB + c + 1], op0=ALU.mult, op1=ALU.add),
                                 reads=[ub_, pvec], writes=[acc_])
                            for tap in range(1, 31):
                                yield
                                S.op("dve", lambda E, c=c, tap=tap: E.scalar_tensor_tensor(out=acc_[:, 0:n], in0=ub_[:, tap:tap + n], scalar=cvw(tap, c), in1=acc_[:, 0:n], op0=ALU.mult, op1=ALU.add),
                                     reads=[ub_, pvec, acc_], writes=[acc_])
                            yield
                            S.op("act", lambda E, c=c: E.activation(out=vTb[:, c, 0:n], in_=acc_[:, 0:n], func=AF.Copy), reads=[acc_], writes=[vTb])

                        for c0_ in range(0, KD, 2):
                            gens = [cunit(c0_, 0), cunit(c0_ + 1, 1)]
                            while gens:
                                for g_ in list(gens):
                                    try:
                                        next(g_)
                                    except StopIteration:
                                        gens.remove(g_)
                        S.op("act", lambda E: E.activation(out=nt.sq[:, :, 0:n], in_=vTb[:, :, 0:n], func=AF.Square), reads=[vTb], writes=[nt.sq])
                        psm, psq = nps(), nps()
                        for k in range(KD):
                            S.op("pe", lambda E, k=k: E.matmul(psm[:, 0:n], lhsT=onesD, rhs=vTb[:, k, 0:n], start=(k == 0), stop=(k == KD - 1)), reads=[cst, vTb], writes=[psm], acc=True)
                        for k in range(KD):
                            S.op("pe", lambda E, k=k: E.matmul(psq[:, 0:n], lhsT=onesD, rhs=nt.sq[:, k, 0:n], start=(k == 0), stop=(k == KD - 1)), reads=[cst, nt.sq], writes=[psq], acc=True)
                        S.op("act", lambda E: E.activation(out=mean[:, 0:n], in_=psm[:, 0:n], func=AF.Copy), reads=[psm], writes=[mean])
                        S.op("dve", lambda E: E.tensor_tensor(out=var[:, 0:n], in0=mean[:, 0:n], in1=mean[:, 0:n], op=ALU.mult), reads=[mean], writes=[var])
                        S.op("dve", lambda E: E.tensor_tensor(out=var[:, 0:n], in0=psq[:, 0:n], in1=var[:, 0:n], op=ALU.subtract), reads=[psq, var], writes=[var])
                        S.op("act", lambda E: E.activation(out=var[:, 0:n], in_=var[:, 0:n], func=AF.Ln, bias=epsc[:, 0:1], scale=1.0), reads=[var, epsc], writes=[var])
                        S.op("act", lambda E: E.activation(out=var[:, 0:n], in_=var[:, 0:n], func=AF.Exp, scale=-0.5), reads=[var], writes=[var])
                        for c in range(KD):
                            tt = t1[c % 2]
                            S.op("dve", lambda E, c=c, tt=tt: E.tensor_tensor(out=tt[:, 0:n], in0=vTb[:, c, 0:n], in1=mean[:, 0:n], op=ALU.subtract), reads=[vTb, mean], writes=[tt])
                            S.op("pool", lambda E, tt=tt: E.tensor_tensor(out=tt[:, 0:n], in0=tt[:, 0:n], in1=var[:, 0:n], op=ALU.mult), reads=[tt, var], writes=[tt])
                            S.op("act", lambda E, c=c, tt=tt: E.activation(out=yc[:, c, 0:n], in_=tt[:, 0:n], func=AF.Silu, scale=pvec[:, pb + O_LNW + c: pb + O_LNW + c + 1], bias=pvec[:, pb + O_LNB + c: pb + O_LNB + c + 1]),
                                 reads=[tt, pvec], writes=[yc])
                        wsrcs = []
                        for dd in range(KD):
                            wsrcs += [w_cv_out[l, :, dd * 128:(dd + 1) * 128], w_hg_out[l, :, dd * 128:(dd + 1) * 128]]
                        for e in range(KD):
                            wsrcs.append(w_out[l, :, e * 128:(e + 1) * 128])
                        wissued = []

                        def wissue():
                            if len(wissued) < len(wsrcs):
                                wissued.append(w3load(wsrcs[len(wissued)]))

                        def wtake(idx):
                            while len(wissued) <= idx:
                                wissue()
                            b_ = wissued[idx]
                            if len(wissued) < idx + 3:
                                wissue()
                            return b_

                        wissue()
                        wissue()
                        for dd in range(KD):
                            sgc, sgh, m1, m2 = sgcs[dd % 2], sghs[dd % 2], m1s[dd % 2], m2s[dd % 2]
                            wcv_ = wtake(2 * dd)
                            ps1 = nps()
                            proj(ps1[:, 0:n], ps1, wcv_, lambda k, wcv_=wcv_: wcv_[:, k, :], yc, lambda k: yc[:, k, 0:n])
                            psg = nps()
                            proj(psg[:, 0:n], psg, wres, lambda k, dd=dd: wres[:, k, 3072 + dd * 128:3072 + (dd + 1) * 128], hTb, lambda k: hTb[:, k, ctr:ctr + n])
                            S.op("act", lambda E: E.activation(out=sgc[:, 0:n], in_=psg[:, 0:n], func=AF.Sigmoid), reads=[psg], writes=[sgc])
                            S.op("dve", lambda E: E.tensor_tensor(out=m1[:, 0:n], in0=ps1[:, 0:n], in1=sgc[:, 0:n], op=ALU.mult), reads=[ps1, sgc], writes=[m1])
                            whg_ = wtake(2 * dd + 1)
                            ps2 = nps()
                            proj(ps2[:, 0:n], ps2, whg_, lambda k, whg_=whg_: whg_[:, k, :], oTb, lambda k: oTb[:, k, 0:n])
                            psh = nps()
                            proj(psh[:, 0:n], psh, wres, lambda k, dd=dd: wres[:, k, 2048 + dd * 128:2048 + (dd + 1) * 128], hTb, lambda k: hTb[:, k, ctr:ctr + n])
                            S.op("act", lambda E: E.activation(out=sgh[:, 0:n], in_=psh[:, 0:n], func=AF.Sigmoid), reads=[psh], writes=[sgh])
                            S.op("dve", lambda E: E.tensor_tensor(out=m2[:, 0:n], in0=ps2[:, 0:n], in1=sgh[:, 0:n], op=ALU.mult), reads=[ps2, sgh], writes=[m2])
                            S.op("pool", lambda E, dd=dd: E.tensor_tensor(out=yT[:, dd, 0:n], in0=m1[:, 0:n], in1=m2[:, 0:n], op=ALU.add), reads=[m1, m2], writes=[yT])
                        for e in range(KD):
                            wo_ = wtake(2 * KD + e)
                            pso = nps()
                            proj(pso[:, 0:n], pso, wo_, lambda k, wo_=wo_: wo_[:, k, :], yT, lambda k: yT[:, k, 0:n])
                            xb_ = xo[e % 2]
                            S.op("dve", lambda E, e=e, xb_=xb_: E.scalar_tensor_tensor(out=xb_[:, 0:n], in0=pso[:, 0:n], scalar=G1(e, s), in1=nt.xb[:, e, ctr:ctr + n], op0=ALU.mult, op1=ALU.add),
                                 reads=[pso, modv, nt.xb], writes=[xb_])
                            S.dma("sp", dst[e, :, b0:b1], xb_[:, 0:n], reads=[xb_], writes=[dstbuf], sembuf=xb_)
                    S.emit()

                merge_seq(xT, LO, HI, 0, W, 0, oT_d, DR["oT_d"], valid, xmid, DR["xmid"])
                if ctx_stream:
                    merge_seq(cxT, 0, CTX, 0, CTX, 1, oTc_d, DR["oTc_d"], ones_c, cxmid, DR["cxmid"])
                S.barrier()
                S.emit()

            NF = HI - LO
            ffn = contextlib.ExitStack()
            h2T = S.sbuf("h2T", [128, KD, NF], BF16, ffn)
            h2Tc = S.sbuf("h2Tc", [128, KD, CTX], BF16, ffn) if ctx_stream else None
            with contextlib.ExitStack() as ph:
                nt = NormTmp(ph, nbuf=2)
                for (b0, b1) in blocks(LO, HI, 512):
                    norm_block(nt, xmid[:, :, b0:b1].rearrange("k p t -> p k t"), DR["xmid"], b1 - b0,
                               lambda k: A2[:, k, 0:1], lambda k: SH2(k, 0), h2T, lambda k, b0=b0, b1=b1: h2T[:, k, b0 - LO:b1 - LO])
                if ctx_stream:
                    norm_block(nt, cxmid.rearrange("k p t -> p k t"), DR["cxmid"], CTX,
                               lambda k: A2[:, k, 1:2], lambda k: SH2(k, 1), h2Tc, lambda k: h2Tc[:, k, 0:CTX])
                S.barrier()
                S.emit()
            zT = S.sbuf("zT", [128, KF, NT], BF16, ffn)
            zTc = S.sbuf("zTc", [128, KF, CTX], BF16, ffn) if ctx_stream else None
            mcb = S.sbuf("mcb", [128, 2, 64], BF16, ffn)
            S.dma("pool", mcb[:, :, :], mcol_d[:, :, 0:64], writes=[mcb], sembuf=mcb)
            with contextlib.ExitStack() as ph:
                uc = [S.sbuf("uc%d" % i, [128, NF + 2], F32, ph) for i in range(3)]
                FH = NT // 2
                fa1s = [S.sbuf("fa1_%d" % q, [128, FH], F32, ph) for q in range(2)]
                fa2s = [S.sbuf("fa2_%d" % q, [128, FH], F32, ph) for q in range(2)]
                wuv = [S.sbuf("wuv%d" % i, [128, KD, 256], BF16, ph) for i in range(2)]
                for i in range(3):
                    S.op("pool", lambda E, i=i: E.memset(uc[i][:, :], 0.0), writes=[uc[i]])
                fw = lambda tap, c: pvec[:, pb + O_FW + tap * KF + c: pb + O_FW + tap * KF + c + 1]
                fb = lambda c: pvec[:, pb + O_FB + c: pb + O_FB + c + 1]

                def conv_taps(taps, n, c, fa1, fa2):
                    for j, (si, st0, tap) in enumerate(taps):
                        if j == 0:
                            S.op("dve", lambda E, si=si, st0=st0, tap=tap: E.tensor_scalar(out=fa1[:, 0:n], in0=uc[si][:, st0:st0 + n], scalar1=fw(tap, c), scalar2=fb(c), op0=ALU.mult, op1=ALU.add),
                                 reads=[uc[si], pvec], writes=[fa1])
                        else:
                            S.op("dve", lambda E, si=si, st0=st0, tap=tap: E.scalar_tensor_tensor(out=fa1[:, 0:n], in0=uc[si][:, st0:st0 + n], scalar=fw(tap, c), in1=fa1[:, 0:n], op0=ALU.mult, op1=ALU.add),
                                 reads=[uc[si], pvec, fa1], writes=[fa1])
                        yield
                    S.op("act", lambda E: E.activation(out=fa2[:, 0:n], in_=fa1[:, 0:n], func=AF.Gelu), reads=[fa1], writes=[fa2])

                def drive(gens):
                    gens = list(gens)
                    while gens:
                        for g_ in list(gens):
                            try:
                                next(g_)
                            except StopIteration:
                                gens.remove(g_)

                cc_ = [0]

                def load_wuv(cq):
                    wq_ = wuv[cq % 2]
                    S.dma("pool", wq_[:, :, 0:128], w_up[l, :, cq * 128:(cq + 1) * 128].rearrange("(k p) c -> p k c", p=128), writes=[wq_], sembuf=wq_)
                    S.dma("pool", wq_[:, :, 128:256], w_up[l, :, FF + cq * 128:FF + (cq + 1) * 128].rearrange("(k p) c -> p k c", p=128), writes=[wq_], sembuf=wq_)

                for c in range(KF):
                    cc_[0] = c
                    wb_ = wuv[c % 2]
                    if c == 0:
                        load_wuv(0)
                    if c + 1 < KF:
                        load_wuv(c + 1)
                    for (i0, i1) in blocks(0, NF, 512):
                        ps = nps()
                        proj(ps[:, 0:i1 - i0], ps, wb_, lambda k, wb_=wb_: wb_[:, k, 0:128], h2T, lambda k, i0=i0, i1=i1: h2T[:, k, i0:i1])
                        S.op("dve", lambda E, ps=ps, i0=i0, i1=i1: E.tensor_tensor(out=uc[0][:, 1 + i0:1 + i1], in0=ps[:, 0:i1 - i0], in1=valid[:, LO + i0:LO + i1], op=ALU.mult),
                             reads=[ps, valid], writes=[uc[0]])
                    for mi in range(2):
                        S.op("dve", lambda E, mi=mi: E.tensor_tensor(out=uc[1 + mi][:, 1:1 + NF].rearrange("p (c t) -> p c t", t=64), in0=uc[0][:, 1:1 + NF].rearrange("p (c t) -> p c t", t=64),
                                                                 in1=mcb[:, mi:mi + 1, :].to_broadcast([128, NF // 64, 64]), op=ALU.mult), reads=[uc[0], mcb], writes=[uc[1 + mi]])
                    taps = []
                    for ky in range(3):
                        for kx in range(3):
                            offs = (ky - 1) * 64 + (kx - 1)
                            si = {0: 1, 1: 0, 2: 2}[kx]
                            taps.append((si, 1 + 64 + offs, ky * 3 + kx))
                    drive([conv_taps([(si, st0 + hb * FH, tap) for (si, st0, tap) in taps], FH, c, fa1s[hb], fa2s[hb]) for hb in range(2)])
                    for hb in range(2):
                        fa2 = fa2s[hb]
                        for (j0, j1) in blocks(hb * FH, (hb + 1) * FH, 512):
                            ps = nps()
                            proj(ps[:, 0:j1 - j0], ps, wb_, lambda k, wb_=wb_: wb_[:, k, 128:256], h2T, lambda k, j0=j0, j1=j1: h2T[:, k, 64 + j0:64 + j1])
                            S.op("dve", lambda E, ps=ps, j0=j0, j1=j1, c=c, hb=hb: E.tensor_tensor(out=zT[:, c, j0:j1], in0=ps[:, 0:j1 - j0], in1=fa2[:, j0 - hb * FH:j1 - hb * FH], op=ALU.mult),
                                 reads=[ps, fa2], writes=[zT])
                    if ctx_stream:
                        ps = nps()
                        proj(ps[:, 0:CTX], ps, wb_, lambda k, wb_=wb_: wb_[:, k, 0:128], h2Tc, lambda k: h2Tc[:, k, 0:CTX])
                        S.op("pool", lambda E: E.memset(uc[0][:, 0:CTX + 2], 0.0), writes=[uc[0]])
                        S.op("act", lambda E, ps=ps: E.activation(out=uc[0][:, 1:1 + CTX], in_=ps[:, 0:CTX], func=AF.Copy), reads=[ps], writes=[uc[0]])
                        drive([conv_taps([(0, 0, 3), (0, 1, 4), (0, 2, 5)], CTX, c, fa1s[0], fa2s[0])])
                        fa2 = fa2s[0]
                        ps = nps()
                        proj(ps[:, 0:CTX], ps, wb_, lambda k, wb_=wb_: wb_[:, k, 128:256], h2Tc, lambda k: h2Tc[:, k, 0:CTX])
                        S.op("dve", lambda E, ps=ps, c=c: E.tensor_tensor(out=zTc[:, c, 0:CTX], in0=ps[:, 0:CTX], in1=fa2[:, 0:CTX], op=ALU.mult), reads=[ps, fa2], writes=[zTc])
                        S.op("pool", lambda E: E.memset(uc[0][:, 0:1], 0.0), writes=[uc[0]])
                    pass
                S.barrier()
                S.emit()
            with contextlib.ExitStack() as ph:
                wd = [S.sbuf("wd%d" % i, [128, KF, 128], BF16, ph) for i in range(2)]
                xr = [S.sbuf("xr%d" % i, [128, 512], F32, ph) for i in range(2)]
                xw = [S.sbuf("xw%d" % i, [128, 512], F32, ph) for i in range(2)]
                cnt = 0
                for e in range(KD):
                    wd_ = wd[e % 2]
                    if e == 0:
                        S.dma("pool", wd_[:, :, :], w_down[l, :, 0:128].rearrange("(c p) n -> p c n", p=128), writes=[wd_], sembuf=wd_)
                    if e + 1 < KD:
                        wdn_ = wd[(e + 1) % 2]
                        S.dma("pool", wdn_[:, :, :], w_down[l, :, (e + 1) * 128:(e + 2) * 128].rearrange("(c p) n -> p c n", p=128), writes=[wdn_], sembuf=wdn_)
                    jobs = [(zT, j0, j1, xmid, DR["xmid"], T0, x_dst, 0) for (j0, j1) in blocks(0, NT, 512)]
                    if ctx_stream:
                        jobs.append((zTc, 0, CTX, cxmid, DR["cxmid"], 0, cx_out, 1))
                    for (zb, j0, j1, xsrc, xsb, xoff, dstap, s) in jobs:
                        n = j1 - j0
                        ps = nps()
                        for c in range(KF):
                            S.op("pe", lambda E, c=c, zb=zb, j0=j0, j1=j1, ps=ps, wd_=wd_, n=n: E.matmul(ps[:, 0:n], lhsT=wd_[:, c, :], rhs=zb[:, c, j0:j1], start=(c == 0), stop=(c == KF - 1)),
                                 reads=[wd_, zb], writes=[ps], acc=True)
                        xr_, xw_ = xr[cnt % 2], xw[cnt % 2]
                        cnt += 1
                        S.dma("sp", xr_[:, 0:n], xsrc[e, :, xoff + j0:xoff + j1], reads=[xsb], writes=[xr_], sembuf=xr_)
                        S.op("dve", lambda E, e=e, s=s, ps=ps, xr_=xr_, xw_=xw_, n=n: E.scalar_tensor_tensor(out=xw_[:, 0:n], in0=ps[:, 0:n], scalar=G2(e, s), in1=xr_[:, 0:n], op0=ALU.mult, op1=ALU.add),
                             reads=[ps, modv, xr_], writes=[xw_])
                        if s == 0 and is_last:
                            S.dma("sp", xnew[e, :, j0:j1], xw_[:, 0:n], reads=[xw_], writes=[DR["xnew"]], sembuf=xw_)
                        else:
                            S.dma("sp", dstap[e, :, j0:j1], xw_[:, 0:n], reads=[xw_], writes=[(x_dst_buf if s == 0 else DR["CX1"])], sembuf=xw_)
                S.barrier()
                S.emit()
            ffn.close()
            if is_last:
                with contextlib.ExitStack() as ph:
                    nt = NormTmp(ph)
                    fo = [S.sbuf("fo%d" % i, [128, KD, 512], F32, ph) for i in range(2)]
                    for bi, (j0, j1) in enumerate(blocks(0, NT, 512)):
                        fo_ = fo[bi % 2]
                        norm_block(nt, xnew[:, :, j0:j1].rearrange("k p t -> p k t"), DR["xnew"], j1 - j0,
                                   lambda k: pvec[:, O_FNW + k:O_FNW + k + 1], lambda k: None, fo_, lambda k, fo_=fo_, j0=j0, j1=j1: fo_[:, k, 0:j1 - j0])
                        S.dma("sp", x_dst[:, :, j0:j1].rearrange("k p t -> p k t"), fo_[:, :, 0:j1 - j0], reads=[fo_], writes=[x_dst_buf], sembuf=fo_)
                    S.barrier()
                    S.emit()
            else:
                S.barrier()
                S.emit()

        DRE = DR["ext"]
        with contextlib.ExitStack() as ph:
            zt = S.sbuf("zt", [128, NST], F32, ph)
            S.op("pool", lambda E: E.memset(zt[:, :], 0.0), writes=[zt])
            for q in range(NSEG):
                S.dma("sp", ST[q], zt[:, :], reads=[zt], writes=[DR["ST"]], sembuf=zt)
            S.barrier()
            S.emit()
        for l in range(DEPTH):
            layer_setup(l)
            for i in range(NSEG):
                if l == 0:
                    if i == 0:
                        continue
                    run_pass("A", l, i, xT4[i], cxT_in, DRE, DRE, None, None, False, False, valid4_d[i], fold4_d[i])
                else:
                    run_pass("A", l, i, XW[i], CX1, DR["XW"], DR["CX1"], None, None, False, False, valid4_d[i], fold4_d[i])
            if l == 0:
                for i in range(NSEG):
                    run_pass("B", l, i, xT4[i], cxT_in, DRE, DRE, X1[i], DR["X1"], i == 0, False, valid4_d[i], fold4_d[i])
                for i in range(NSEG):
                    S.dma("sp", XW[i, :, :, HX:HX + NT], X1[i], reads=[DR["X1"]], writes=[DR["XW"]], sembuf=DR["XW"])
                    S.dma("sp", XW[i, :, :, 0:HX], X1[(i - 1) % NSEG, :, :, NT - HX:NT], reads=[DR["X1"]], writes=[DR["XW"]], sembuf=DR["XW"])
                    S.dma("sp", XW[i, :, :, HX + NT:W], X1[(i + 1) % NSEG, :, :, 0:HX], reads=[DR["X1"]], writes=[DR["XW"]], sembuf=DR["XW"])
                S.barrier()
                S.emit()
            else:
                with contextlib.ExitStack() as ph:
                    oh = S.sbuf("oh", [128, NSEG], F32, ph)
                    S.dma("sp", oh[:, :], onehot_d, writes=[oh], sembuf=oh)
                    xs = [S.sbuf("sel%d" % q, [128, KD, 512], F32, ph) for q in range(NSEG)]
                    acc = [S.sbuf("selacc%d" % q, [128, KD, 512], F32, ph) for q in range(2)]
                    for bi, (b0, b1) in enumerate(blocks(0, W, 512)):
                        n = b1 - b0
                        a_ = acc[bi % 2]
                        for q in range(NSEG):
                            S.dma("sp", xs[q][:, :, 0:n], XW[q, :, :, b0:b1].rearrange("k p t -> p k t"), reads=[DR["XW"]], writes=[xs[q]], sembuf=xs[q])
                        S.op("dve", lambda E, a_=a_, n=n: E.tensor_scalar(out=a_[:, :, 0:n], in0=xs[0][:, :, 0:n], scalar1=oh[:, 0:1], scalar2=None, op0=ALU.mult), reads=[xs[0], oh], writes=[a_])
                        for q in range(1, NSEG):
                            S.op("dve", lambda E, a_=a_, n=n, q=q: E.scalar_tensor_tensor(out=a_[:, :, 0:n], in0=xs[q][:, :, 0:n], scalar=oh[:, q:q + 1], in1=a_[:, :, 0:n], op0=ALU.mult, op1=ALU.add),
                                 reads=[xs[q], oh, a_], writes=[a_])
                        S.dma("sp", XWO[:, :, b0:b1].rearrange("k p t -> p k t"), a_[:, :, 0:n], reads=[a_], writes=[DR["XWO"]], sembuf=a_)
                    S.barrier()
                    S.emit()
                run_pass("B", l, 0, XWO, CX1, DR["XWO"], DR["CX1"], x_out, DR["x_out"], False, True, validown_d, foldown_d)
        S.barrier()
        S.emit(final=True)
    return nc


def _chan(v):
    return np.ascontiguousarray(v.reshape(-1, 128).T)


def pack_pvec(inp):
    pv = np.zeros((128, NPV), np.float32)
    for l in range(DEPTH):
        b = l * PL
        pv[:, b + O_N1W:b + O_N1W + 8] = _chan(inp["norm1_w"][l])
        pv[:, b + O_BMOD:b + O_BMOD + 48] = _chan(inp["b_mod"][l])
        pv[:, b + O_GNW] = inp["hg_gnorm_w"][l]
        for tap in range(31):
            pv[:, b + O_CVW + tap * 8: b + O_CVW + tap * 8 + 8] = _chan(inp["cv_dw_w"][l, tap])
        pv[:, b + O_CVB:b + O_CVB + 8] = _chan(inp["cv_dw_b"][l])
        pv[:, b + O_LNW:b + O_LNW + 8] = _chan(inp["cv_ln_w"][l])
        pv[:, b + O_LNB:b + O_LNB + 8] = _chan(inp["cv_ln_b"][l])
        pv[:, b + O_N2W:b + O_N2W + 8] = _chan(inp["norm2_w"][l])
        fw = inp["ffn_dw_w"][l].reshape(9, FF)
        for tap in range(9):
            pv[:, b + O_FW + tap * KF: b + O_FW + (tap + 1) * KF] = _chan(fw[tap])
        pv[:, b + O_FB:b + O_FB + KF] = _chan(inp["ffn_dw_b"][l])
    for d in range(2):
        for l in range(DEPTH):
            pv[:, O_LBZ + d * 16 + l * 8: O_LBZ + d * 16 + l * 8 + 8] = _chan(inp["hg_lb_logits"][d, l])
    pv[:, O_FNW:O_FNW + 8] = _chan(inp["final_norm_w"])
    return pv


def make_consts():
    cst = np.zeros((128, 6, 128), np.float32)
    cst[:, 0, :] = np.eye(128, dtype=np.float32)
    cst[:, 1, :] = 1.0 / D
    cst[:, 2, :] = 1.0 / 128
    s = np.arange(64)[:, None]
    t = np.arange(64)[None, :]
    mF = (s <= t).astype(np.float32)
    mB = (s >= t).astype(np.float32)
    cst[0:32, 3, 0:64] = mF[0:32, :]
    cst[0:32, 3, 64:96] = mF[32:64, 32:64]
    cst[0:32, 4, 0:64] = mB[32:64, :]
    cst[0:32, 4, 64:96] = mB[0:32, 0:32]
    rst = np.ones((128, 512), np.float32)
    rst[:, ::64] = 0.0
    mcol = np.ones((128, 2, W), np.float32)
    pos = np.arange(W)
    mcol[:, 0, pos % 64 == 63] = 0.0
    mcol[:, 1, pos % 64 == 0] = 0.0
    return cst, rst, mcol


def window_T(xfull_b, seg):
    t0 = seg * NT - HX
    w = np.zeros((W, D), np.float32)
    a, b = max(t0, 0), min(t0 + W, SEQ)
    w[a - t0:b - t0] = xfull_b[a:b]
    return np.ascontiguousarray(w.T.reshape(KD, 128, W))


def core_static(inp, c):
    b, seg = c // NSEG, c % NSEG
    t0 = seg * NT - HX
    pos = np.arange(W) + t0
    valid = np.broadcast_to(((pos >= 0) & (pos < SEQ)).astype(np.float32), (128, W)).copy()
    fold = np.zeros((128, 2, NSEG), np.float32)
    for k in range(NSEG):
        fold[:, 0, k] = 1.0 if k < seg else 0.0
        fold[:, 1, k] = 1.0 if k > seg else 0.0
    cvec = np.stack([_chan(inp["c"][b]), _chan(inp["c_ctx"])], axis=-1)
    return dict(valid=valid, foldm=fold, cvec=np.ascontiguousarray(cvec))


_NC_CACHE = {}


def seg_static(seg):
    t0 = seg * NT - HX
    pos = np.arange(W) + t0
    valid = np.broadcast_to(((pos >= 0) & (pos < SEQ)).astype(np.float32), (128, W)).copy()
    fold = np.zeros((128, 2, NSEG), np.float32)
    for k in range(NSEG):
        fold[:, 0, k] = 1.0 if k < seg else 0.0
        fold[:, 1, k] = 1.0 if k > seg else 0.0
    return valid, fold


def kernel(**inp):
    inp = {k: np.asarray(v) for k, v in inp.items()}
    pv = pack_pvec(inp)
    cst, rst, mcol = make_consts()
    x = np.ascontiguousarray(inp["x"], dtype=np.float32)
    ctx = np.ascontiguousarray(inp["ctx"], dtype=np.float32)
    segs = [seg_static(s) for s in range(NSEG)]
    valid4 = np.ascontiguousarray(np.stack([v for v, _ in segs], axis=0))
    fold4 = np.ascontiguousarray(np.stack([f for _, f in segs], axis=0))
    xT4 = [np.ascontiguousarray(np.stack([window_T(x[b], s) for s in range(NSEG)], axis=0)) for b in range(BATCH)]
    cxT = [np.ascontiguousarray(ctx[b].T.reshape(KD, 128, CTX)) for b in range(BATCH)]
    maps = []
    for c in range(NCORE):
        b, seg = c // NSEG, c % NSEG
        oh = np.zeros((128, NSEG), np.float32)
        oh[:, seg] = 1.0
        maps.append(dict(
            xT4=xT4[b], cxT=cxT[b], cvec=np.ascontiguousarray(np.stack([_chan(inp["c"][b]), _chan(inp["c_ctx"])], axis=-1)),
            valid4=valid4, fold4=fold4, validown=segs[seg][0], foldown=segs[seg][1], onehot=oh,
            pvec=pv, cst=cst, rst=rst, mcol=mcol, w_mod=inp["w_mod"], w_in=inp["w_in"],
            w_hg_out=inp["w_hg_out"], w_cv_out=inp["w_cv_out"], w_out=inp["w_out"], w_up=inp["w_up"], w_down=inp["w_down"]))
    if "nc" not in _NC_CACHE:
        _NC_CACHE["nc"] = build()
    res = run_bass_kernel_spmd(_NC_CACHE["nc"], maps, core_ids=list(range(NCORE)))
    out = np.empty((BATCH, SEQ, D), np.float32)
    for c in range(NCORE):
        b, seg = c // NSEG, c % NSEG
        out[b, seg * NT:(seg + 1) * NT, :] = res.results[c]["x_out"].reshape(D, NT).T
    return out
```
